# Optimizing a Trainium2 kernel written in Bass

```python
import math, functools
import jax, jax.numpy as jnp
from jax import lax
import numpy as np

D_MODEL = 2048
BATCH = 2
SEQ = 4096
DEPTH = 4
DEC_BATCH = 32
DEC_SEQ = 8
PAST_LEN = 16384
PAGE_SIZE = 128

HEAD_DIM = 64
N_HEADS = D_MODEL // 128
N_KV_HEADS = N_HEADS // 4
Q_GROUP = N_HEADS // N_KV_HEADS
ATTN_W = N_HEADS * HEAD_DIM
KV_W = N_KV_HEADS * HEAD_DIM
WINDOW = 128
ATTN_BLOCK = 128
ROPE_THETA = 10000.0
SGU_CHUNK = 128
SGU_W = D_MODEL // 2
SGU_GROUPS = 8
SGU_CH = SGU_W // SGU_GROUPS
D_FF = 4 * D_MODEL
RMS_EPS = 1e-6
LN_EPS = 1e-5
NEG_INF = -1e30
SPLITS = (ATTN_W, ATTN_W + KV_W, ATTN_W + 2 * KV_W, ATTN_W + 2 * KV_W + SGU_W,
          ATTN_W + 2 * KV_W + 2 * SGU_W, ATTN_W + 2 * KV_W + 2 * SGU_W + D_MODEL)
IN_COLS = ATTN_W + 2 * KV_W + 2 * SGU_W + 2 * D_MODEL

kernel_name = "hybrid_swa_sink_sgu_gated_decoder_step"


def rms_norm(x, g):
    xf = x.astype(jnp.float32)
    y = xf * lax.rsqrt(jnp.mean(xf * xf, axis=-1, keepdims=True) + RMS_EPS)
    return (y * g.astype(jnp.float32)).astype(x.dtype)


def layer_norm(x, g, b):
    xf = x.astype(jnp.float32)
    mu = jnp.mean(xf, axis=-1, keepdims=True)
    var = jnp.mean(jnp.square(xf - mu), axis=-1, keepdims=True)
    y = (xf - mu) * lax.rsqrt(var + LN_EPS)
    return (y * g.astype(jnp.float32) + b.astype(jnp.float32)).astype(x.dtype)


def rope(x, pos):
    half = HEAD_DIM // 2
    inv = jnp.power(jnp.float32(ROPE_THETA), -jnp.arange(half, dtype=jnp.float32) / half)
    ang = pos.astype(jnp.float32)[:, None] * inv[None, :]
    cos = jnp.cos(ang)[:, None, :]
    sin = jnp.sin(ang)[:, None, :]
    xf = x.astype(jnp.float32)
    x1, x2 = xf[..., :half], xf[..., half:]
    return jnp.concatenate([x1 * cos - x2 * sin, x2 * cos + x1 * sin], axis=-1).astype(x.dtype)


def sink_attention(q, k, v, mask, sink):
    scale = HEAD_DIM ** -0.5
    s = jnp.einsum('...qhgd,...khd->...hgqk', q, k).astype(jnp.float32) * scale
    s = jnp.where(mask, s, NEG_INF)
    sk = sink.astype(jnp.float32).reshape(N_KV_HEADS, Q_GROUP, 1, 1)
    m = jnp.maximum(jnp.max(s, axis=-1, keepdims=True), sk)
    p = jnp.exp(s - m)
    p = p / (jnp.sum(p, axis=-1, keepdims=True) + jnp.exp(sk - m))
    return jnp.einsum('...hgqk,...khd->...qhgd', p.astype(v.dtype), v)


def swa_prompt(q, k, v, sink):
    B, L = q.shape[0], q.shape[1]
    nb = L // ATTN_BLOCK
    qb = q.reshape(B, nb, ATTN_BLOCK, N_KV_HEADS, Q_GROUP, HEAD_DIM)

    def with_prev(t):
        tb = t.reshape(B, nb, ATTN_BLOCK, N_KV_HEADS, HEAD_DIM)
        prev = jnp.pad(tb[:, :-1], ((0, 0), (1, 0), (0, 0), (0, 0), (0, 0)))
        return jnp.concatenate([prev, tb], axis=2)

    kk, vv = with_prev(k), with_prev(v)
    qi = jnp.arange(ATTN_BLOCK)[:, None]
    kj = jnp.arange(2 * ATTN_BLOCK)[None, :]
    diff = qi + ATTN_BLOCK - kj
    band = (diff >= 0) & (diff < WINDOW)
    blk = jnp.arange(nb)[:, None, None]
    key_valid = (blk - 1) * ATTN_BLOCK + kj[None] >= 0
    mask = (band[None] & key_valid)[:, None, None]
    o = sink_attention(qb, kk, vv, mask, sink)
    return o.reshape(B, L, ATTN_W), k[:, -WINDOW:], v[:, -WINDOW:]


def swa_sample(q, k, v, sink, ck, cv):
    Bd, S = q.shape[0], q.shape[1]
    kk = jnp.concatenate([ck.astype(k.dtype), k], axis=1)
    vv = jnp.concatenate([cv.astype(v.dtype), v], axis=1)
    q_pos = PAST_LEN + jnp.arange(S)
    k_pos = PAST_LEN - WINDOW + jnp.arange(WINDOW + S)
    diff = q_pos[:, None] - k_pos[None, :]
    mask = (diff >= 0) & (diff < WINDOW)
    qg = q.reshape(Bd, S, N_KV_HEADS, Q_GROUP, HEAD_DIM)
    o = sink_attention(qg, kk, vv, mask, sink)
    return o.reshape(Bd, S, ATTN_W), kk[:, -WINDOW:], vv[:, -WINDOW:]


def spatial_gate(u, v, w_s, b_s):
    B, L = v.shape[0], v.shape[1]
    csz = min(L, SGU_CHUNK)
    pad = (-L) % csz
    vp = jnp.pad(v, ((0, 0), (0, pad), (0, 0), (0, 0)))
    n = (L + pad) // csz
    vc = vp.reshape(B, n, csz, SGU_GROUPS, SGU_CH)
    tril = jnp.tril(jnp.ones((csz, csz), dtype=bool))
    w = jnp.where(tril[None], w_s[:, :csz, :csz], jnp.zeros((), w_s.dtype))
    mixed = jnp.einsum('gts,bnsgc->bntgc', w, vc) + b_s[:, :csz].T[None, None, :, :, None]
    mixed = mixed.reshape(B, n * csz, SGU_GROUPS, SGU_CH)[:, :L]
    return u * mixed


def block(x, pos, attend, n1, w_in, qg, kg, sink, ln_g, ln_b, sgu_w, sgu_b,
          w_au, w_su, w_out, n2, w1, w2):
    B, L = x.shape[0], x.shape[1]
    xn = rms_norm(x, n1)
    q, k, v, u, vs, ga, gm = jnp.split(xn @ w_in, SPLITS, axis=-1)
    q = rope(rms_norm(q.reshape(B, L, N_HEADS, HEAD_DIM), qg), pos)
    k = rope(rms_norm(k.reshape(B, L, N_KV_HEADS, HEAD_DIM), kg), pos)
    v = v.reshape(B, L, N_KV_HEADS, HEAD_DIM)
    a, k_state, v_state = attend(q, k, v, sink)
    u = jax.nn.gelu(u, approximate=False).reshape(B, L, SGU_GROUPS, SGU_CH)
    vs = layer_norm(jax.nn.gelu(vs, approximate=False), ln_g, ln_b).reshape(B, L, SGU_GROUPS, SGU_CH)
    m = spatial_gate(u, vs, sgu_w, sgu_b).reshape(B, L, SGU_W)
    merged = jax.nn.sigmoid(ga) * (a @ w_au) + jax.nn.sigmoid(gm) * (m @ w_su)
    h = x + merged @ w_out
    y = h + jnp.square(jax.nn.relu(rms_norm(h, n2) @ w1)) @ w2
    return y, k_state, v_state, vs


def setup_inputs(seed: int = 0) -> dict:
    key = jax.random.key(seed)
    ks = jax.random.split(key, 20)
    f32 = jnp.float32
    nrm = lambda k, shape, s: jax.random.normal(k, shape, f32) * s
    return {
        "x_prompt": nrm(ks[0], (BATCH, SEQ, D_MODEL), 1.0),
        "x_sample": nrm(ks[1], (DEC_BATCH, DEC_SEQ, D_MODEL), 1.0),
        "cache_k": nrm(ks[2], (DEPTH, DEC_BATCH, WINDOW, N_KV_HEADS, HEAD_DIM), 1.0),
        "cache_v": nrm(ks[3], (DEPTH, DEC_BATCH, WINDOW, N_KV_HEADS, HEAD_DIM), 1.0),
        "norm1_g": 1.0 + nrm(ks[4], (DEPTH, D_MODEL), 0.02),
        "w_in": nrm(ks[5], (DEPTH, D_MODEL, IN_COLS), D_MODEL ** -0.5),
        "q_norm_g": 1.0 + nrm(ks[6], (DEPTH, HEAD_DIM), 0.02),
        "k_norm_g": 1.0 + nrm(ks[7], (DEPTH, HEAD_DIM), 0.02),
        "attn_sinks": nrm(ks[8], (DEPTH, N_HEADS), 1.0),
        "sgu_ln_g": 1.0 + nrm(ks[9], (DEPTH, SGU_W), 0.02),
        "sgu_ln_b": nrm(ks[10], (DEPTH, SGU_W), 0.02),
        "sgu_w": nrm(ks[11], (DEPTH, SGU_GROUPS, SGU_CHUNK, SGU_CHUNK), SGU_CHUNK ** -0.5),
        "sgu_b": 1.0 + nrm(ks[12], (DEPTH, SGU_GROUPS, SGU_CHUNK), 0.02),
        "w_attn_up": nrm(ks[13], (DEPTH, ATTN_W, D_MODEL), ATTN_W ** -0.5),
        "w_sgu_up": nrm(ks[14], (DEPTH, SGU_W, D_MODEL), SGU_W ** -0.5),
        "w_out": nrm(ks[15], (DEPTH, D_MODEL, D_MODEL), D_MODEL ** -0.5),
        "norm2_g": 1.0 + nrm(ks[16], (DEPTH, D_MODEL), 0.02),
        "w_ff1": nrm(ks[17], (DEPTH, D_MODEL, D_FF), D_MODEL ** -0.5),
        "w_ff2": nrm(ks[18], (DEPTH, D_FF, D_MODEL), D_FF ** -0.5),
    }


def reference(x_prompt, x_sample, cache_k, cache_v, norm1_g, w_in, q_norm_g, k_norm_g,
              attn_sinks, sgu_ln_g, sgu_ln_b, sgu_w, sgu_b, w_attn_up, w_sgu_up, w_out,
              norm2_g, w_ff1, w_ff2):
    pos_p = jnp.arange(x_prompt.shape[1])
    pos_s = PAST_LEN + jnp.arange(x_sample.shape[1])
    xp, xs = x_prompt, x_sample
    kp_l, vp_l, ks_l, vs_l, sv_l = [], [], [], [], []
    for l in range(DEPTH):
        params = (norm1_g[l], w_in[l], q_norm_g[l], k_norm_g[l], attn_sinks[l], sgu_ln_g[l],
                  sgu_ln_b[l], sgu_w[l], sgu_b[l], w_attn_up[l], w_sgu_up[l], w_out[l],
                  norm2_g[l], w_ff1[l], w_ff2[l])
        xp, kp, vp, _ = block(xp, pos_p, swa_prompt, *params)
        sample_attend = functools.partial(swa_sample, ck=cache_k[l], cv=cache_v[l])
        xs, ksn, vsn, sgu_v = block(xs, pos_s, sample_attend, *params)
        kp_l.append(kp); vp_l.append(vp); ks_l.append(ksn); vs_l.append(vsn); sv_l.append(sgu_v)
    new_k_prompt = jnp.stack(kp_l)
    new_v_prompt = jnp.stack(vp_l)
    new_k_sample = jnp.stack(ks_l)
    new_v_sample = jnp.stack(vs_l)
    new_sgu_v_sample = jnp.stack(sv_l)
    return (xp, xs, new_k_prompt, new_v_prompt, new_k_sample, new_v_sample, new_sgu_v_sample)
```

```python
import math
from contextlib import ExitStack

import numpy as np
import ml_dtypes

import concourse.bass as bass
import concourse.mybir as mybir
from concourse.bass_utils import run_bass_kernel_spmd

F32 = mybir.dt.float32
BF16 = mybir.dt.bfloat16
AF = mybir.ActivationFunctionType
ALU = mybir.AluOpType
AX = mybir.AxisListType

D = 2048
NCH = 16
DEPTH = 4
WA = 672
WB = 768
XW = 800
IN_COLS_P = 8448
OQ, OK_, OV, OU, OVS, OGA, OGM = 0, 1024, 2048, 2304, 3328, 4352, 6400
NV_L = 34
NEG = -30000.0
EPL = 9
ATL = 9


class Op:
    __slots__ = ("eng", "emit", "deps", "needs_inc", "ticket", "chain", "semh")


class Sched:
    ENGS = ("pe", "act", "dve", "pool", "sp")

    def __init__(self):
        self.ops = {e: [] for e in self.ENGS}
        self.last_w = {}
        self.readers = {}
        self.chains = {}

    def add(self, eng, emit, reads=(), writes=(), chain=None):
        op = Op()
        op.eng = eng; op.emit = emit; op.deps = set(); op.needs_inc = False
        op.chain = chain; op.ticket = 0; op.semh = None
        lw = self.last_w; rd = self.readers
        for k in reads:
            w = lw.get(k)
            if w is not None:
                op.deps.add(w)
        for k in writes:
            w = lw.get(k)
            if w is not None:
                op.deps.add(w)
            r = rd.get(k)
            if r:
                op.deps.update(r.values())
        rk = ("c", chain, id(op)) if chain is not None else eng
        for k in reads:
            r = rd.get(k)
            if r is None:
                r = rd[k] = {}
            r[rk] = op
        for k in writes:
            lw[k] = op
            rd[k] = {}
        if chain is not None:
            ch = self.chains.setdefault(chain, [])
            if ch:
                op.deps.add(ch[-1])
            ch.append(op)
            op.needs_inc = True
        op.deps.discard(op)
        if eng == "pe":
            op.deps = {d for d in op.deps if not (d.eng == "pe" and d.chain is None)}
        for d in op.deps:
            d.needs_inc = True
        self.ops[eng].append(op)
        return op

    def finalize(self, eng_sems, chain_sems):
        for e in self.ENGS:
            n = 0
            for op in self.ops[e]:
                if op.chain is not None:
                    continue
                op.semh = eng_sems[e]
                if op.needs_inc:
                    n += 1
                    op.ticket = n
        for cname, ch in self.chains.items():
            for i, op in enumerate(ch):
                op.semh = chain_sems[cname]
                op.ticket = 16 * (i + 1)

    def emit_engine(self, e, h):
        waited = {}
        for op in self.ops[e]:
            need = {}
            for d in op.deps:
                key = id(d.semh)
                if need.get(key, (None, 0))[1] < d.ticket:
                    need[key] = (d.semh, d.ticket)
            for key, (s, v) in need.items():
                if waited.get(key, 0) < v:
                    h.wait_ge(s, v)
                    waited[key] = v
            if op.emit is not None:
                ins = op.emit(h)
                if op.needs_inc:
                    ins.then_inc(op.semh, 16 if op.chain is not None else 1)


def segs(c0, c1):
    n = c1 - c0
    ns = -(-n // 512)
    w = n // ns
    out = []
    for i in range(ns):
        a = c0 + i * w
        out.append((a, (c1 - a) if i == ns - 1 else w))
    return out


class _Stop(Exception):
    pass


def build_program(NL=DEPTH, TILES=("A", "B"), STOP=None):
    nc = bass.Bass("TRN2", target_bir_lowering=False)
    S = Sched()

    def din(name, shape, dt=F32):
        return nc.dram_tensor(name, list(shape), dt, kind="ExternalInput").ap()

    def dout(name, shape, dt=F32):
        return nc.dram_tensor(name, list(shape), dt, kind="ExternalOutput").ap()

    xA_d = din("xA", [D, XW]); xB_d = din("xB", [D, WB])
    cosA_d = din("cosA", [128, XW]); sinA_d = din("sinA", [128, XW])
    cosB_d = din("cosB", [128, WB]); sinB_d = din("sinB", [128, WB])
    win_d = din("w_in", [DEPTH, D, IN_COLS_P])
    wau_d = din("w_au", [DEPTH, 1024, D]); wsu_d = din("w_su", [DEPTH, 1024, D])
    wout_d = din("w_out", [DEPTH, D, D])
    w1_d = din("w_ff1", [DEPTH, D, 8192]); w2_d = din("w_ff2", [DEPTH, 8192, D])
    vecs_d = din("vecs", [128, NV_L * DEPTH])
    lnG_d = din("lnG", [DEPTH, 128, 1024]); lnB_d = din("lnB", [DEPTH, 128, 1024])
    sinks_d = din("sinks", [DEPTH, 128, 16])
    sguw_d = din("sgu_w", [DEPTH, 8, 128, 128]); sgub_d = din("sgu_b", [DEPTH, 1, 1024])
    ck_d = din("cache_k", [DEPTH, 4, 128, 256]); cv_d = din("cache_v", [DEPTH, 4, 128, 256])
    cb_d = din("cbf", [128, 4 * 128 + 2 * 512 + 128 + 512], BF16)
    cf_d = din("cf32", [128, 256])

    yA_d = dout("yA", [D, WA]); yB_d = dout("yB", [D, WB])
    kp_d = dout("kp", [DEPTH, 128, 256]); vp_d = dout("vp", [DEPTH, 128, 256])
    ks_d = dout("ks", [DEPTH, 4, 128, 256]); vs_d = dout("vs", [DEPTH, 4, 128, 256])
    sv_d = dout("sv", [DEPTH, 32, 1024])
    scr_d = nc.dram_tensor("scr", [DEPTH, 128, 1024 + 260], BF16, kind="Internal").ap()

    es = ExitStack()
    with es:
        def sb(name, shape, dt):
            return es.enter_context(nc.sbuf_tensor("sb_" + name, list(shape), dt))

        x = sb("x", [128, NCH, WB], F32)
        xn = sb("xn", [128, NCH, XW], BF16)
        R1 = sb("R1", [128, NCH, WB], BF16)
        R2 = sb("R2", [128, NCH, WB], BF16)
        wsl = [sb(f"wsl{i}", [128, 4096], BF16) for i in range(4)]
        cosT = sb("cosT", [128, XW], F32); sinT = sb("sinT", [128, XW], F32)
        vaug = sb("vaug", [128, 8, 4, 65], BF16)
        kTprev = sb("kTprev", [128, 8, 128], BF16)
        SC = sb("SC", [128, 2048], F32)
        arena = sb("arena", [128, 4096], BF16)
        bscr = arena[:, 0:1024]
        lnG = sb("lnG", [128, 1024], F32); lnB = sb("lnB", [128, 1024], F32)
        a_tok = arena[:, 2048:3072]
        pT = [arena[:, 3072:3584], arena[:, 3584:4096]]
        pTn = sb("pTn", [128, 256], BF16)
        wsT = sb("wsT", [128, 8, 128], BF16)
        brow = sb("brow", [128, 1024], BF16)
        kc32 = [sb("kc32_0", [128, 256], F32)] * 2
        vc32 = [sb("vc32_0", [128, 256], F32)] * 2
        kcd = sb("kcd", [128, 4, 2, 128], BF16)
        vcaug = sb("vcaug", [128, 4, 65], BF16)
        cb = sb("cb", [128, 4 * 128 + 2 * 512 + 128 + 512], BF16)
        cf = sb("cf", [128, 256], F32)
        vecs = sb("vecs", [128, NV_L * DEPTH], F32)
        sk32 = sb("sk32", [128, 16], F32); esk = sb("esk", [128, 16], F32)
        den = sb("den", [128, 16], F32); rden = sb("rden", [128, 16], F32)
        st = sb("st", [128, 8], F32)
        dummy = sb("dummy", [128, 2], F32)
        kst = sb("kst", [128, 256], F32); vst = sb("vst", [128, 256], F32)
        kfin = arena[:, 1024:2048].bitcast(F32).rearrange("p (a b) -> p a b", a=4)
        gv2 = arena[:, 0:2048].bitcast(F32); lnt2 = arena[:, 2048:4096].bitcast(F32)
        kcT = a_tok[:, :].rearrange("p (c t) -> p c t", c=8)
        psb = [es.enter_context(nc.psum_tensor(f"ps{i}", [128, 512], F32)) for i in range(8)]
        _CACHE["sbuf_left"] = nc.sbuf_bytes_remaining

        identb = cb[:, 0:128]; onesb = cb[:, 128:256]; Bd = cb[:, 256:384]; Rm = cb[:, 384:512]
        maskN = cb[:, 512:1024]; maskF = cb[:, 1024:1536]; maskc = cb[:, 1536:1664]
        maskn = lambda i: cb[:, 1664 + 128 * i: 1792 + 128 * i]
        identf = cf[:, 0:128]; tril = cf[:, 128:256]
        ones_row = cb[0:1, 128:256]

        eng_sems = {e: es.enter_context(nc.semaphore("sem_" + e)) for e in Sched.ENGS}
        chain_names = ["w0", "w1", "w2", "w3", "const", "xload", "par", "par2", "cache0", "cache1", "scrw", "scrr",
                       "oy", "okp", "ovp", "oks", "ovs", "osv", "occ"]
        chain_sems = {c: es.enter_context(nc.semaphore("ch_" + c)) for c in chain_names}

        free_banks = list(range(8))

        def balloc():
            assert free_banks, "out of PSUM banks"
            return free_banks.pop(0)

        def bfree(b):
            free_banks.append(b)

        def dma(queue, chain, out, in_, reads, writes):
            def emit(h, out=out, in_=in_):
                return h.dma_start(out=out, in_=in_)
            S.add(queue, emit, reads=reads, writes=writes, chain=chain)

        def act(out, in_, func, reads, writes, scale=None, bias=None):
            def emit(h):
                kw = {}
                if scale is not None:
                    kw["scale"] = scale
                if bias is not None:
                    kw["bias"] = bias
                return h.activation(out=out, in_=in_, func=func, **kw)
            S.add("act", emit, reads=reads, writes=writes)

        def dve(fn, reads, writes):
            S.add("dve", fn, reads=reads, writes=writes)

        def pe(items, reads, writes):
            def emit(h, items=items):
                ins = None
                for it in items:
                    if it[0] == "T":
                        ins = h.transpose(out=it[1], in_=it[2], identity=it[3])
                    else:
                        ins = h.matmul(it[0], it[1], it[2], start=it[3], stop=it[4])
                return ins
            S.add("pe", emit, reads=reads, writes=writes)

        def rstd_ops(out, in_ps, scale, eps, reads, writes):
            act(out, in_ps, AF.Ln, reads, writes, scale=scale, bias=eps)
            act(out, out, AF.Exp, writes, writes, scale=-0.5)

        wstate = {"n": 0}
        ssq = {"banks": None}

        def ssq_open(seglist):
            ssq["banks"] = {c0: balloc() for (c0, n, tag) in seglist}

        def ssq_take():
            b = ssq["banks"]; ssq["banks"] = None
            return b

        def wload(view, nk, ncols):
            s = wstate["n"] % 4
            wstate["n"] += 1
            dst = wsl[s][:, 0:nk * ncols].rearrange("p (k n) -> p k n", k=nk)
            dma("pool", f"w{s}", dst, view, reads=(), writes=(("w", s),))
            return dst, ("w", s)

        def proj(parts, ncols, seglist, epilogue, cg=512):
            ngroups = -(-ncols // cg)
            pending = [None]
            for g in range(ngroups):
                gc = min(cg, ncols - g * cg)
                loaded = []
                for (W, row0, nk, col0, src_fn, src_keys) in parts:
                    k0 = 0
                    while k0 < nk:
                        kk = min(4096 // gc, nk - k0)
                        view = W[row0 + k0 * 128: row0 + (k0 + kk) * 128, col0 + g * cg: col0 + g * cg + gc] \
                            .rearrange("(k p) n -> p k n", p=128)
                        dst, key = wload(view, kk, gc)
                        loaded.append((dst, key, k0, kk, src_fn, src_keys))
                        k0 += kk
                for seg in seglist:
                    c0, n, tag = seg
                    for cl in range(gc // 128):
                        ci = g * (cg // 128) + cl
                        bank = balloc()
                        items = []
                        rkeys = []
                        tot = sum(l[3] for l in loaded)
                        i = 0
                        for (dst, key, k0, kk, src_fn, src_keys) in loaded:
                            rkeys.append(key)
                            rkeys.extend(src_keys)
                            for k in range(kk):
                                items.append((psb[bank][:, 0:n], dst[:, k, cl * 128:(cl + 1) * 128],
                                              src_fn(k0 + k, c0, n), i == 0, i == tot - 1))
                                i += 1
                        pe(items, reads=rkeys, writes=(("ps", bank),))
                        if pending[0] is not None:
                            pending[0](); pending[0] = None
                        r = epilogue(ci, seg, bank)
                        bfree(bank)
                        if callable(r):
                            pending[0] = r
            if pending[0] is not None:
                pending[0](); pending[0] = None

        dma("sp", "const", cb[:, :], cb_d[:, :], (), (("cb",),))
        dma("sp", "const", cf[:, :], cf_d[:, :], (), (("cf",),))
        dma("sp", "const", vecs[:, :], vecs_d[:, :], (), (("vecs",),))
        for e in ("pe", "act", "dve"):
            S.add(e, None, reads=(("cb",), ("cf",), ("vecs",)), writes=())
        dve(lambda h: h.memset(vaug[:, :, :, :].rearrange("p a b c -> p (a b c)"), 1.0), (), tuple(("vaug", i) for i in range(8)))
        dve(lambda h: h.memset(vcaug[:, :, :].rearrange("p a b -> p (a b)"), 1.0), (), (("vcaug",),))
        dve(lambda h: h.memset(brow[:, :], 0.0), (), (("brow",),))
        dve(lambda h: h.memset(pTn[:, :], 0.0), (), (("pTn",),))
        dve(lambda h: h.memset(kcd[:, :, :, :].rearrange("p a b c -> p (a b c)"), 0.0), (), (("kcd",),))
        dve(lambda h: h.memset(a_tok[:, :], 0.0), (), tuple(("a_tok", c) for c in range(8)))
        dve(lambda h: h.memset(kfin[:, :, :].rearrange("p a b -> p (a b)"), 0.0), (), (("kfin",),))
        for i_ in range(2):
            dve(lambda h, i_=i_: h.memset(pT[i_][:, :], 0.0), (), (("pT", i_),))
        dve(lambda h: h.memset(R1[:, :, :].rearrange("p a b -> p (a b)"), 0.0), (), tuple(("R1", c) for c in range(NCH)))
        dve(lambda h: h.memset(R2[:, :, :].rearrange("p a b -> p (a b)"), 0.0), (), tuple(("R2", c) for c in range(NCH)))
        dve(lambda h: h.memset(xn[:, :, :].rearrange("p a b -> p (a b)"), 0.0), (), tuple(("xn", c) for c in range(NCH)))

        def xk(k):
            return ("x", k)

        def stop(n):
            if STOP == n:
                raise _Stop()

        try:
          for tile in TILES:
              W = WA if tile == "A" else WB
              nblk_tile = 5 if tile == "A" else 6
              if tile == "A":
                  xv = xA_d.rearrange("(k p) n -> p k n", p=128)
                  dma("sp", "xload", x[:, :, 0:WA], xv[:, :, 0:WA], (), tuple(xk(k) for k in range(NCH)))
                  x0 = SC[:, :].rearrange("p (k n) -> p k n", k=NCH)
                  dma("sp", "xload", x0, xv[:, :, WA:XW], (), tuple(("S", i) for i in range(4)))
                  dma("sp", "xload", cosT[:, :], cosA_d[:, :], (), (("cos",),))
                  dma("sp", "xload", sinT[:, :], sinA_d[:, :], (), (("sin",),))
              else:
                  xv = xB_d.rearrange("(k p) n -> p k n", p=128)
                  dma("sp", "xload", x[:, :, 0:WB], xv[:, :, :], (), tuple(xk(k) for k in range(NCH)))
                  dma("sp", "xload", cosT[:, 0:WB], cosB_d[:, :], (), (("cos",),))
                  dma("sp", "xload", sinT[:, 0:WB], sinB_d[:, :], (), (("sin",),))

              carry = None
              for l in range(NL):
                  vb = l * NV_L
                  g1 = lambda k, vb=vb: vecs[:, vb + k: vb + k + 1]
                  g2 = lambda k, vb=vb: vecs[:, vb + 16 + k: vb + 17 + k]
                  qg = vecs[:, vb + 32: vb + 33]; kg = vecs[:, vb + 33: vb + 34]
                  if tile == "A":
                      cf0 = 128 * l
                      ck0 = 128 * (l - 1) if l >= 1 else 0
                      blocks = list(range(l + 1, 6))
                      bcol = lambda b: (b - 1) * 128
                      kvblk = l
                  else:
                      cf0 = 0; ck0 = 0
                      blocks = list(range(6, 12))
                      bcol = lambda b: (b - 6) * 128
                      kvblk = None
                  vslot = lambda b: (b - 1) if tile == "A" else (b - 6)
                  full_segs = [(a, n, "m") for (a, n) in segs(cf0, W)]
                  kv_segs = [(a, n, "m") for (a, n) in segs(ck0, W)]
                  if tile == "A" and l == 0:
                      kv_segs = kv_segs + [(WA, 128, "b0")]

                  dma("sp", "par", lnG[:, :], lnG_d[l], (), (("lnG",),))
                  dma("sp", "par", lnB[:, :], lnB_d[l], (), (("lnB",),))
                  dma("sp", "par", sk32[:, :], sinks_d[l], (), (("sk32",),))
                  dma("pool", "par2", brow[0:1, :], sgub_d[l], (), (("brow",),))
                  act(esk[:, :], sk32[:, :], AF.Exp, (("sk32",),), (("esk",),))

                  def rmsnorm(src_is_x0, c0, n, gfn, dstc0, ssbank=None):
                      bank = balloc() if ssbank is None else ssbank
                      for k in range(NCH if ssbank is None else 0):
                          sq = bscr[:, (k % 2) * 512:(k % 2) * 512 + n]
                          src = x0[:, k, 0:n] if src_is_x0 else x[:, k, c0:c0 + n]
                          skeys = (("S", 0), ("S", 1), ("S", 2), ("S", 3)) if src_is_x0 else (xk(k),)
                          if k % 2 == 0:
                              act(sq, src, AF.Square, skeys, (("bs", 0),))
                          else:
                              dve(lambda h, sq=sq, src=src: h.tensor_tensor(out=sq, in0=src, in1=src, op=ALU.mult), skeys, (("bs", 1),))
                          pe([(psb[bank][:, 0:n], onesb, sq, k == 0, k == NCH - 1)], (("bs", k % 2),), (("ps", bank),))
                      rs = SC[:, 0:n] if not src_is_x0 else None
                      if src_is_x0:
                          rs = kfin[:, 0, 0:n]
                          rkey = ("kfin",)
                      else:
                          rkey = ("S", 0) if n <= 512 else None
                      wk = (rkey,) if rkey is not None else (("S", 0), ("S", 1))
                      rstd_ops(rs, psb[bank][:, 0:n], 1.0 / D, 1e-6, (("ps", bank),), wk)
                      bfree(bank)
                      for k in range(NCH):
                          src = x0[:, k, 0:n] if src_is_x0 else x[:, k, c0:c0 + n]
                          skeys = (("S", 0), ("S", 1), ("S", 2), ("S", 3)) if src_is_x0 else (xk(k),)
                          dve(lambda h, src=src, k=k: h.scalar_tensor_tensor(
                              out=xn[:, k, dstc0:dstc0 + n], in0=src, scalar=gfn(k), in1=rs, op0=ALU.mult, op1=ALU.mult),
                              skeys + wk, (("xn", k),))

                  if tile == "A" and l == 0:
                      rmsnorm(True, 0, 128, g1, WA)
                  for (a, n) in segs(ck0, W):
                      rmsnorm(False, a, n, g1, a, ssbank=(carry[a] if carry is not None else None))
                  carry = None
                  xn_keys = tuple(("xn", k) for k in range(NCH))
                  xn_src = lambda k, c0, n: xn[:, k, c0:c0 + n]

                  stop(1)
                  ws32 = SC[:, 1024:2048].rearrange("p (g s) -> p g s", g=8)
                  dma("sp", "par", ws32, sguw_d[l].rearrange("g t s -> t g s"), (), (("S", 2), ("S", 3)))
                  for g in range(8):
                      dve(lambda h, g=g: h.tensor_tensor(out=ws32[:, g, :], in0=ws32[:, g, :], in1=tril, op=ALU.mult),
                          (("S", 2), ("S", 3)), (("S", 2), ("S", 3)))
                  for half in range(2):
                      bank = balloc()
                      items = [("T", psb[bank][:, j * 128:(j + 1) * 128], ws32[:, half * 4 + j, :], identf) for j in range(4)]
                      pe(items, (("S", 2), ("S", 3)), (("ps", bank),))
                      act(wsT[:, half * 4:half * 4 + 4, :], psb[bank][:, :].rearrange("p (g t) -> p g t", g=4), AF.Copy,
                          (("ps", bank),), (("wsT",),))
                      bfree(bank)

                  stop(2)
                  qT = R1[:, 0:8, :]; kT = R1[:, 8:16, :]

                  def qk_epilogue(is_q):
                      gain = qg if is_q else kg

                      def ep(ci, seg, bank):
                          c0, n, tag = seg
                          z = psb[bank][:, 0:n]
                          sqz = bscr[:, 0:n]; y = bscr[:, 512:512 + n]
                          rs = SC[:, 0:n]; t1 = SC[:, 512:512 + n]; t2 = SC[:, 1024:1024 + n]
                          act(sqz, z, AF.Square, (("ps", bank),), (("bs", 0),))
                          act(y, z, AF.Copy, (("ps", bank),), (("bs", 1),), scale=gain)

                          def later():
                              b2 = balloc(); b3 = balloc()
                              pe([(psb[b2][:, 0:n], Bd, sqz, True, True)], (("bs", 0),), (("ps", b2),))
                              pe([(psb[b3][:, 0:n], Rm, y, True, True)], (("bs", 1),), (("ps", b3),))
                              rstd_ops(rs, psb[b2][:, 0:n], 1.0 / 64, 1e-6, (("ps", b2),), (("S", 0),))
                              dve(lambda h: h.tensor_tensor(out=t1, in0=y, in1=cosT[:, c0:c0 + n], op=ALU.mult),
                                  (("bs", 1), ("cos",)), (("S", 1),))
                              dve(lambda h: h.tensor_tensor(out=t2, in0=psb[b3][:, 0:n], in1=sinT[:, c0:c0 + n], op=ALU.mult),
                                  (("ps", b3), ("sin",)), (("S", 2),))
                              dve(lambda h: h.tensor_tensor(out=t1, in0=t1, in1=t2, op=ALU.add),
                                  (("S", 1), ("S", 2)), (("S", 1),))
                              if is_q:
                                  o = qT[:, ci, c0:c0 + n]; ok = ("R1", ci)
                              elif tag == "b0":
                                  o = kTprev[:, ci, 0:n]; ok = ("kTprev",)
                              else:
                                  o = kT[:, ci, c0:c0 + n]; ok = ("R1", 8 + ci)
                              dve(lambda h: h.tensor_tensor(out=o, in0=t1, in1=rs, op=ALU.mult),
                                  (("S", 1), ("S", 0)), (ok,))
                              if not is_q:
                                  if tile == "B" and c0 + n == WB and ci % 2 == 0:
                                      lo = WB - 128 - c0
                                      dve(lambda h: h.tensor_tensor(out=kfin[:, ci // 2, :], in0=t1[:, lo:lo + 128], in1=rs[:, lo:lo + 128], op=ALU.mult),
                                          (("S", 1), ("S", 0)), (("kfin",),))
                                  if tile == "A" and tag == "m" and c0 + n == WA and ci % 2 == 0:
                                      lo = WA - 32 - c0
                                      dve(lambda h: h.tensor_tensor(out=kfin[:, ci // 2, 0:32], in0=t1[:, lo:lo + 32], in1=rs[:, lo:lo + 32], op=ALU.mult),
                                          (("S", 1), ("S", 0)), (("kfin",),))
                              bfree(b2); bfree(b3)
                          return later
                      return ep

                  proj([(win_d[l], 0, NCH, OQ, xn_src, xn_keys)], 1024, full_segs, qk_epilogue(True))
                  proj([(win_d[l], 0, NCH, OK_, xn_src, xn_keys)], 1024, kv_segs, qk_epilogue(False))

                  stop(3)
                  vview = win_d[l][:, OV:OV + 256].rearrange("(k p) n -> p k n", p=128)
                  wv, wvkey = wload(vview, NCH, 256)

                  def vproj(colstart, m, dst_ap, dkey, f32_dst=None, f32key=None):
                      bank = balloc()
                      items = [(psb[bank][:, 0:256], xn[:, k, colstart:colstart + 128], wv[:, k, :], k == 0, k == NCH - 1)
                               for k in range(NCH)]
                      pe(items, (wvkey,) + xn_keys, (("ps", bank),))
                      act(dst_ap, psb[bank][:, 0:256].rearrange("p (h d) -> p h d", h=4), AF.Copy, (("ps", bank),), (dkey,))
                      if f32_dst is not None:
                          act(f32_dst, psb[bank][:, 0:256], AF.Copy, (("ps", bank),), (f32key,))
                      bfree(bank)

                  if tile == "A" and l == 0:
                      vproj(WA, 128, vaug[:, 7, :, 0:64], ("vaug", 7))
                  kvb_list = ([kvblk] if (kvblk is not None and l >= 1) else []) + blocks
                  for b in kvb_list:
                      last = (tile == "B" and b == 11)
                      vproj(bcol(b), 128, vaug[:, vslot(b), :, 0:64], ("vaug", vslot(b)),
                            vst[:, :] if last else None, ("vst",) if last else None)
                  if tile == "B":
                      dma("sp", "scrr", kTprev[:, :, :], scr_d[l][:, 0:1024].rearrange("p (c t) -> p c t", c=8),
                          (("scr", l),), (("kTprev",),))
                      dma("sp", "scrr", vaug[:, 7, :, :], scr_d[l][:, 1024:1284].rearrange("p (h d) -> p h d", h=4),
                          (("scr", l),), (("vaug", 7),))

                  stop(4)
                  aT = R2[:, 0:8, :]; mT = R2[:, 8:16, :]
                  qkeys = tuple(("R1", c) for c in range(8))
                  kkeys = tuple(("R1", 8 + c) for c in range(8))
                  akeys = tuple(("R2", c) for c in range(8))

                  def attn_norm(np_, obanks):
                      for bi, ob in enumerate(obanks):
                          h0 = bi * 7; nh = min(7, 16 - h0)
                          ov = psb[ob][0:np_, 0:nh * 65].rearrange("p (h e) -> p h e", h=nh)
                          dve(lambda h, ov=ov, h0=h0, nh=nh: h.tensor_tensor(out=den[0:np_, h0:h0 + nh], in0=ov[:, :, 64],
                                                                           in1=esk[0:np_, h0:h0 + nh], op=ALU.add),
                              (("ps", ob), ("esk",)), (("den", bi),))
                          dve(lambda h, h0=h0, nh=nh: h.reciprocal(out=rden[0:np_, h0:h0 + nh], in_=den[0:np_, h0:h0 + nh]),
                              (("den", bi),), (("rden", bi),))
                          dve(lambda h, ov=ov, h0=h0, nh=nh: h.tensor_tensor(
                              out=a_tok[0:np_, h0 * 64:(h0 + nh) * 64].rearrange("p (h d) -> p h d", h=nh), in0=ov[:, :, 0:64],
                              in1=rden[0:np_, h0:h0 + nh].unsqueeze(2).broadcast_to([np_, nh, 64]), op=ALU.mult),
                              (("ps", ob), ("rden", bi)), tuple(("a_tok", c) for c in range(h0 // 2, (h0 + nh - 1) // 2 + 1)))
                          bfree(ob)

                  def attn_tr(col0, ncols):
                      for half in range(2):
                          bank = balloc()
                          pv = psb[bank][:, 0:256].bitcast(BF16)
                          items = [("T", pv[:, j * 128:(j + 1) * 128], a_tok[:, (half * 4 + j) * 128:(half * 4 + j + 1) * 128],
                                    identb) for j in range(4)]
                          pe(items, tuple(("a_tok", half * 4 + j) for j in range(4)), (("ps", bank),))
                          act(aT[:, half * 4:half * 4 + 4, col0:col0 + ncols],
                              pv[:, 0:512].rearrange("p (c t) -> p c t", c=4)[:, :, 0:ncols], AF.Copy,
                              (("ps", bank),), akeys[half * 4:half * 4 + 4])
                          bfree(bank)

                  atok_all = tuple(("a_tok", c) for c in range(8))
                  pti = [0]
                  pend_tr = [None]
                  for b in blocks:
                      c_q = bcol(b)
                      first = (b == blocks[0]) and (tile == "B" or l == 0)
                      if first:
                          kprev = lambda kv, hf: kTprev[:, 2 * kv + hf, :]
                          kprev_key = ("kTprev",); vprev = 7
                      else:
                          cp = bcol(b - 1)
                          kprev = lambda kv, hf, cp=cp: kT[:, 2 * kv + hf, cp:cp + 128]
                          kprev_key = None; vprev = vslot(b - 1)
                      mask_ap = maskF if (b == 4) else maskN
                      obanks = [balloc(), balloc(), balloc()]

                      def pv_mm(pr, ps_i, obanks=obanks, vprev=vprev, b=b):
                          items = []
                          for hf in range(2):
                              hd = 2 * pr + hf; kv = hd // 4
                              ob = obanks[hd // 7]; off = (hd % 7) * 65
                              items.append((psb[ob][:, off:off + 65], pT[ps_i][:, (hf * 2) * 128:(hf * 2 + 1) * 128],
                                            vaug[:, vprev, kv, :], True, False))
                              items.append((psb[ob][:, off:off + 65], pT[ps_i][:, (hf * 2 + 1) * 128:(hf * 2 + 2) * 128],
                                            vaug[:, vslot(b), kv, :], False, True))
                          pe(items, (("pT", ps_i), ("vaug", vprev), ("vaug", vslot(b))), tuple(("ps", o) for o in obanks))

                      prev = None
                      for pr in range(8):
                          bank = balloc()
                          items = [(psb[bank][:, 0:512], identb, mask_ap, True, False)]
                          for hf in range(2):
                              hd = 2 * pr + hf; kv = hd // 4
                              q_ap = qT[:, pr, c_q:c_q + 128]
                              items.append((psb[bank][:, (hf * 2) * 128:(hf * 2 + 1) * 128], kprev(kv, hf), q_ap, False, False))
                              items.append((psb[bank][:, (hf * 2 + 1) * 128:(hf * 2 + 2) * 128],
                                            kT[:, 2 * kv + hf, c_q:c_q + 128], q_ap, False, hf == 1))
                          rk = qkeys + kkeys + ((kprev_key,) if kprev_key else ())
                          pe(items, rk, (("ps", bank),))
                          ps_i = pti[0] % 2; pti[0] += 1
                          act(pT[ps_i][:, :], psb[bank][:, 0:512], AF.Exp, (("ps", bank),), (("pT", ps_i),), scale=0.125)
                          bfree(bank)
                          if prev is not None:
                              pv_mm(*prev)
                          prev = (pr, ps_i)
                          if pr == 2 and pend_tr[0] is not None:
                              attn_tr(*pend_tr[0]); pend_tr[0] = None
                      pv_mm(*prev)
                      attn_norm(128, obanks)
                      pend_tr[0] = (c_q, 128)
                  if pend_tr[0] is not None:
                      attn_tr(*pend_tr[0]); pend_tr[0] = None

                  stop(5)
                  if tile == "B":
                      bank = balloc()
                      items = [("T", psb[bank][:, j * 128:(j + 1) * 128], kfin[:, j, :], identf) for j in range(4)]
                      pe(items, (("kfin",),), (("ps", bank),))
                      act(kst[:, :].rearrange("p (h d) -> p h d", h=4), psb[bank][:, :].rearrange("p (h e) -> p h e", h=4)[:, :, 0:64],
                          AF.Copy, (("ps", bank),), (("kst",),))
                      bfree(bank)
                      dma("sp", "okp", kp_d[l], kst[:, :], (("kst",),), (("okp", l),))
                      dma("sp", "ovp", vp_d[l], vst[:, :], (("vst",),), (("ovp", l),))
                  else:
                      c5 = bcol(5)
                      dma("sp", "scrw", scr_d[l][:, 0:1024].rearrange("p (c t) -> p c t", c=8), kT[:, :, c5:c5 + 128],
                          kkeys, (("scr", l),))
                      dma("sp", "scrw", scr_d[l][:, 1024:1284].rearrange("p (h d) -> p h d", h=4), vaug[:, vslot(5), :, :],
                          (("vaug", vslot(5)),), (("scr", l),))

                  stop(6)
                  if tile == "A":
                      vproj(640, 128, vaug[:, 6, :, 0:64], ("vaug", 6), vst[:, :], ("vst",))
                      for i in range(4):
                          cs = 640 + 8 * i
                          dma("sp", "cache0", kc32[0][:, :], ck_d[l, i], (), (("kc32", 0),))
                          dma("sp", "cache0", vc32[0][:, :], cv_d[l, i], (), (("vc32", 0),))
                          dma("sp", "occ", ks_d[l, i, 0:120, :], ck_d[l, i, 8:128, :], (), (("oks_c", l, i),))
                          dma("sp", "occ", vs_d[l, i, 0:120, :], cv_d[l, i, 8:128, :], (), (("ovs_c", l, i),))
                          kcv = kc32[0][:, :].rearrange("p (h d) -> p h d", h=4)
                          act(kcd[:, :, 0, 0:64], kcv, AF.Copy, (("kc32", 0),), (("kcd",),))
                          act(kcd[:, :, 1, 64:128], kcv, AF.Copy, (("kc32", 0),), (("kcd",),))
                          act(vcaug[:, :, 0:64], vc32[0][:, :].rearrange("p (h d) -> p h d", h=4), AF.Copy,
                              (("vc32", 0),), (("vcaug",),))
                          kcd8 = kcd[:, :, :, :].rearrange("p a b c -> p (a b) c")
                          for half in range(2):
                              bank = balloc()
                              pv = psb[bank][:, 0:256].bitcast(BF16)
                              items = [("T", pv[:, j * 128:(j + 1) * 128], kcd8[:, half * 4 + j, :], identb) for j in range(4)]
                              pe(items, (("kcd",),), (("ps", bank),))
                              act(kcT[:, half * 4:half * 4 + 4, :], pv[:, 0:512].rearrange("p (c t) -> p c t", c=4), AF.Copy,
                                  (("ps", bank),), atok_all[half * 4:half * 4 + 4])
                              bfree(bank)
                          bc = balloc(); bn = balloc()
                          items = [(psb[bc][:, 0:128], identb, maskc, True, False)]
                          for hd in range(16):
                              hf = hd % 2; kv = hd // 4; pr = hd // 2
                              items.append((psb[bc][:, hd * 8:(hd + 1) * 8], kcT[:, 2 * kv + hf, :],
                                            qT[:, pr, cs:cs + 8], False, hd == 15))
                          pe(items, atok_all + qkeys, (("ps", bc),))
                          items = [(psb[bn][:, 0:128], identb, maskn(i), True, False)]
                          for hd in range(16):
                              hf = hd % 2; kv = hd // 4; pr = hd // 2
                              items.append((psb[bn][:, hd * 8:(hd + 1) * 8], kT[:, 2 * kv + hf, 640:768],
                                            qT[:, pr, cs:cs + 8], False, hd == 15))
                          pe(items, kkeys + qkeys, (("ps", bn),))
                          act(pT[0][:, 0:128], psb[bc][:, 0:128], AF.Exp, (("ps", bc),), (("pT", 0),), scale=0.125)
                          act(pTn[0:32, 0:128], psb[bn][0:32, 0:128], AF.Exp, (("ps", bn),), (("pTn",),), scale=0.125)
                          bfree(bc); bfree(bn)
                          obanks = [balloc(), balloc(), balloc()]
                          items = []
                          for hd in range(16):
                              kv = hd // 4
                              ob = obanks[hd // 7]; off = (hd % 7) * 65
                              items.append((psb[ob][:, off:off + 65], pT[0][:, hd * 8:hd * 8 + 128], vcaug[:, kv, :], True, False))
                              items.append((psb[ob][:, off:off + 65], pTn[:, hd * 8:hd * 8 + 128], vaug[:, 6, kv, :], False, True))
                          pe(items, (("pT", 0), ("pTn",), ("vcaug",), ("vaug", 6)), tuple(("ps", o) for o in obanks))
                          attn_norm(8, obanks)
                          attn_tr(cs, 8)
                      bank = balloc()
                      items = [("T", psb[bank][:, j * 128:(j + 1) * 128], kfin[:, j, :], identf) for j in range(4)]
                      pe(items, (("kfin",),), (("ps", bank),))
                      act(kst[:, :].rearrange("p (h d) -> p h d", h=4), psb[bank][:, :].rearrange("p (h e) -> p h e", h=4)[:, :, 0:64],
                          AF.Copy, (("ps", bank),), (("kst",),))
                      bfree(bank)
                      for i in range(4):
                          dma("sp", "oks", ks_d[l, i, 120:128, :], kst[8 * i:8 * i + 8, :], (("kst",),), (("oks", l, i),))
                          dma("sp", "ovs", vs_d[l, i, 120:128, :], vst[8 * i:8 * i + 8, :], (("vst",),), (("ovs", l, i),))

                  stop(7)
                  uT = R1[:, 0:8, :]
                  ukeys = tuple(("R1", c) for c in range(8))

                  def u_ep(ci, seg, bank):
                      c0, n, tag = seg
                      act(uT[:, ci, c0:c0 + n], psb[bank][:, 0:n], AF.Gelu, (("ps", bank),), (("R1", ci),))
                  proj([(win_d[l], 0, NCH, OU, xn_src, xn_keys)], 1024, full_segs, u_ep)

                  vsw = []
                  for hv in range(2):
                      for kh in range(2):
                          view = win_d[l][kh * 1024:(kh + 1) * 1024, OVS + hv * 512: OVS + (hv + 1) * 512] \
                              .rearrange("(k p) n -> p k n", p=128)
                          vsw.append(wload(view, 8, 512))
                  vsn_all = R1[:, 8:16, :].rearrange("p k t -> p (k t)")
                  vsnk = lambda sl: (("vsn", sl),)
                  gv = SC[:, 0:1024]; lnt = SC[:, 1024:2048]
                  mkeys = tuple(("R2", 8 + c) for c in range(8))

                  def sgu_stage1(colstart, m, vsn_bf, vsn_key, is_sample, samp_idx=None, par=0):
                      for hv in range(2):
                          bank = balloc()
                          items = []
                          for kh in range(2):
                              wdst, wkey = vsw[hv * 2 + kh]
                              for k in range(8):
                                  kk = kh * 8 + k
                                  items.append((psb[bank][:, 0:512], xn[:, kk, colstart:colstart + 128], wdst[:, k, :],
                                                kk == 0, kk == NCH - 1))
                          pe(items, tuple(w[1] for w in vsw) + xn_keys, (("ps", bank),))
                          gvp = gv if par == 0 else gv2
                          gkeys = (("S", hv),) if par == 0 else (("bs", 0), ("bs", 1), ("kfin",))
                          act(gvp[0:m, hv * 512:(hv + 1) * 512], psb[bank][0:m, 0:512], AF.Gelu, (("ps", bank),), gkeys)
                          bfree(bank)
                      if par == 0:
                          G = gv[0:m, :]; T = lnt[0:m, :]
                          sk = (("S", 0), ("S", 1)); tk = (("S", 2), ("S", 3))
                      else:
                          G = gv2[0:m, :]; T = lnt2[0:m, :]
                          sk = (("bs", 0), ("bs", 1), ("kfin",)); tk = atok_all + (("pT", 0), ("pT", 1))
                      so = 4 * par
                      dve(lambda h: h.tensor_reduce(out=st[0:m, so:so + 1], in_=G, axis=AX.X, op=ALU.add), sk, (("st", so),))
                      dve(lambda h: h.tensor_scalar(out=st[0:m, so + 1:so + 2], in0=st[0:m, so:so + 1], scalar1=-1.0 / 1024, scalar2=None, op0=ALU.mult),
                          (("st", so),), (("st", so + 1),))
                      act(T, G, AF.Identity, sk + (("st", so + 1),), tk, bias=st[0:m, so + 1:so + 2])
                      dve(lambda h: h.tensor_tensor(out=G, in0=T, in1=T, op=ALU.mult), tk, sk)
                      dve(lambda h: h.tensor_reduce(out=st[0:m, so + 2:so + 3], in_=G, axis=AX.X, op=ALU.add), sk, (("st", so + 2),))
                      rstd_ops(st[0:m, so + 3:so + 4], st[0:m, so + 2:so + 3], 1.0 / 1024, 1e-5, (("st", so + 2),), (("st", so + 3),))
                      dve(lambda h: h.scalar_tensor_tensor(out=T, in0=T, scalar=st[0:m, so + 3:so + 4], in1=lnG[0:m, :],
                                                           op0=ALU.mult, op1=ALU.mult), tk + (("st", so + 3), ("lnG",)), tk)
                      if is_sample:
                          dve(lambda h: h.tensor_tensor(out=T, in0=T, in1=lnB[0:m, :], op=ALU.add), tk + (("lnB",),), tk)
                          dve(lambda h: h.memset(vsn_bf[:, :], 0.0), (), vsn_key)
                          act(vsn_bf[0:m, :], T, AF.Copy, tk, vsn_key)
                          dma("sp", "osv", sv_d[l, 8 * samp_idx:8 * samp_idx + 8, :], T, tk, (("osv", l, samp_idx),))
                      else:
                          dve(lambda h: h.tensor_tensor(out=vsn_bf[0:m, :], in0=T, in1=lnB[0:m, :], op=ALU.add), tk + (("lnB",),), vsn_key)

                  def sgu_stage2(colstart, m, vsn_bf, vsn_key, is_sample, samp_idx=None):
                      for half in range(2):
                          bank = balloc()
                          items = []
                          for j in range(4):
                              g = half * 4 + j
                              o = psb[bank][:, j * 128:j * 128 + m]
                              items.append((o, vsn_bf[:, g * 128:(g + 1) * 128], wsT[:, g, 0:m], True, False))
                              items.append((o, onesb, brow[:, g * 128:g * 128 + m], False, True))
                          pe(items, vsn_key + (("wsT",), ("brow",)), (("ps", bank),))
                          dve(lambda h, half=half, bank=bank: h.tensor_tensor(
                              out=mT[:, half * 4:half * 4 + 4, colstart:colstart + m],
                              in0=psb[bank][:, :].rearrange("p (g t) -> p g t", g=4)[:, :, 0:m],
                              in1=uT[:, half * 4:half * 4 + 4, colstart:colstart + m], op=ALU.mult),
                              (("ps", bank),) + ukeys[half * 4:half * 4 + 4], mkeys[half * 4:half * 4 + 4])
                          bfree(bank)

                  r1hi = tuple(("R1", 8 + j) for j in range(8))
                  vsn_allk = tuple(("vsn", j) for j in range(6))
                  dve(lambda h: h.memset(dummy[:, 0:1], 0.0), (), r1hi + vsn_allk)
                  sgu_list = []
                  for b in blocks:
                      sl = vslot(b)
                      sgu_list.append((bcol(b), 128, vsn_all[:, sl * 1024:(sl + 1) * 1024], vsnk(sl), False, None, len(sgu_list) % 2))
                  if tile == "A":
                      for i in range(4):
                          sgu_list.append((640 + 8 * i, 8, a_tok[:, :], atok_all, True, i, 0))
                  prev_s = None
                  for it in sgu_list:
                      if it[4]:
                          if prev_s is not None:
                              sgu_stage2(*prev_s[0:6]); prev_s = None
                          sgu_stage1(*it)
                          sgu_stage2(*it[0:6])
                          continue
                      sgu_stage1(*it)
                      if prev_s is not None:
                          sgu_stage2(*prev_s[0:6])
                      prev_s = it
                  if prev_s is not None:
                      sgu_stage2(*prev_s[0:6])

                  stop(8)
                  merged = R1
                  dve(lambda h: h.memset(dummy[:, 1:2], 0.0), (), r1hi + vsn_allk)
                  mg_keys = lambda ci: (("R1", ci),)
                  for pas in range(2):
                      Wup = wau_d[l] if pas == 0 else wsu_d[l]
                      srcT = aT if pas == 0 else mT
                      skeys_ = akeys if pas == 0 else mkeys
                      gcol = OGA if pas == 0 else OGM
                      for g in range(8):
                          upw, upk = wload(Wup[:, g * 256:(g + 1) * 256].rearrange("(k p) n -> p k n", p=128), 8, 256)
                          gw, gk = wload(win_d[l][:, gcol + g * 256: gcol + (g + 1) * 256].rearrange("(k p) n -> p k n", p=128), NCH, 256)
                          for (c0, n, tag) in full_segs:
                              for cl in range(2):
                                  ci = g * 2 + cl
                                  bp = balloc(); bg = balloc()
                                  items = [(psb[bp][:, 0:n], upw[:, k, cl * 128:(cl + 1) * 128], srcT[:, k, c0:c0 + n], k == 0, k == 7)
                                           for k in range(8)]
                                  pe(items, (upk,) + skeys_, (("ps", bp),))
                                  items = [(psb[bg][:, 0:n], gw[:, k, cl * 128:(cl + 1) * 128], xn[:, k, c0:c0 + n], k == 0, k == NCH - 1)
                                           for k in range(NCH)]
                                  pe(items, (gk,) + xn_keys, (("ps", bg),))
                                  sg = SC[:, 0:n]; tt = SC[:, 512:512 + n]
                                  act(sg, psb[bg][:, 0:n], AF.Sigmoid, (("ps", bg),), (("S", 0),))
                                  if pas == 0:
                                      dve(lambda h, bp=bp, ci=ci, c0=c0, n=n, sg=sg: h.tensor_tensor(
                                          out=merged[:, ci, c0:c0 + n], in0=psb[bp][:, 0:n], in1=sg, op=ALU.mult),
                                          (("ps", bp), ("S", 0)), mg_keys(ci))
                                  else:
                                      dve(lambda h, bp=bp, tt=tt, n=n, sg=sg: h.tensor_tensor(out=tt, in0=psb[bp][:, 0:n], in1=sg, op=ALU.mult),
                                          (("ps", bp), ("S", 0)), (("S", 1),))
                                      dve(lambda h, ci=ci, c0=c0, n=n, tt=tt: h.tensor_tensor(
                                          out=merged[:, ci, c0:c0 + n], in0=merged[:, ci, c0:c0 + n], in1=tt, op=ALU.add),
                                          (("S", 1),) + mg_keys(ci), mg_keys(ci))
                                  bfree(bp); bfree(bg)

                  stop(9)
                  mgall = tuple(("R1", c) for c in range(NCH))

                  def out_ep(ci, seg, bank):
                      c0, n, tag = seg
                      dve(lambda h: h.tensor_tensor(out=x[:, ci, c0:c0 + n], in0=psb[bank][:, 0:n], in1=x[:, ci, c0:c0 + n], op=ALU.add),
                          (("ps", bank), xk(ci)), (xk(ci),))
                      if ssq["banks"] is not None:
                          sb_ = ssq["banks"][c0]
                          sq = bscr[:, (ci % 2) * 512:(ci % 2) * 512 + n]
                          act(sq, x[:, ci, c0:c0 + n], AF.Square, (xk(ci),), (("bs", ci % 2),))
                          pe([(psb[sb_][:, 0:n], onesb, sq, ci == 0, ci == NCH - 1)], (("bs", ci % 2),), (("ps", sb_),))
                  ssq_open(full_segs)
                  proj([(wout_d[l], 0, NCH, 0, lambda k, c0, n: merged[:, k, c0:c0 + n], mgall)], D, full_segs, out_ep)
                  ss5 = ssq_take()

                  stop(10)
                  for (a, n) in segs(cf0, W):
                      rmsnorm(False, a, n, g2, a, ssbank=ss5[a])

                  fT = R1
                  fkeys = tuple(("R1", c) for c in range(NCH))
                  tgl = [0]

                  def f_ep(ci, seg, bank):
                      c0, n, tag = seg
                      q = tgl[0] % 2; tgl[0] += 1
                      r = SC[:, 1024 + q * 512: 1024 + q * 512 + n]
                      act(r, psb[bank][:, 0:n], AF.Relu, (("ps", bank),), (("S", 2 + q),))
                      dve(lambda h: h.tensor_tensor(out=fT[:, ci, c0:c0 + n], in0=r, in1=r, op=ALU.mult),
                          (("S", 2 + q),), (("R1", ci),))
                  for j in range(4):
                      proj([(w1_d[l], 0, NCH, j * 2048, xn_src, xn_keys)], 2048, full_segs, f_ep)
                      if j == 3 and l + 1 < NL:
                          ssq_open(full_segs)
                      proj([(w2_d[l], j * 2048, NCH, 0, lambda k, c0, n: fT[:, k, c0:c0 + n], fkeys)], D, full_segs, out_ep)
                  if l + 1 < NL:
                      carry = ssq_take()

              yv = (yA_d if tile == "A" else yB_d).rearrange("(k p) n -> p k n", p=128)
              dma("sp", "oy", yv[:, :, :], x[:, :, 0:W], tuple(xk(k) for k in range(NCH)), (("oy", tile),))

        except _Stop:
            pass

        allk = [k for k in S.last_w if isinstance(k, tuple) and isinstance(k[0], str) and k[0].startswith("o")]
        S.add("sp", None, reads=tuple(allk), writes=())

        S.finalize(eng_sems, chain_sems)
        with nc.Block() as block:
            @block.tensor
            def _(h):
                S.emit_engine("pe", h)

            @block.scalar
            def _(h):
                S.emit_engine("act", h)

            @block.vector
            def _(h):
                S.emit_engine("dve", h)

            @block.gpsimd
            def _(h):
                S.emit_engine("pool", h)

            @block.sync
            def _(h):
                S.emit_engine("sp", h)
    return nc


_CACHE = {}


def _consts():
    bf = ml_dtypes.bfloat16
    ident = np.eye(128, dtype=np.float32)
    ones = np.ones((128, 128), np.float32)
    p = np.arange(128)
    Bd = (p[:, None] // 64 == p[None, :] // 64).astype(np.float32)
    Rm = np.zeros((128, 128), np.float32)
    for m in range(128):
        if m % 64 < 32:
            Rm[m + 32, m] = -1.0
        else:
            Rm[m - 32, m] = 1.0
    k = np.arange(128)[:, None]; q = np.arange(128)[None, :]
    prev = np.where(k > q, 0.0, NEG).astype(np.float32)
    cur = np.where(k <= q, 0.0, NEG).astype(np.float32)
    maskN = np.concatenate([prev, cur, prev, cur], axis=1)
    allneg = np.full((128, 128), NEG, np.float32)
    maskF0 = np.concatenate([allneg, cur, allneg, cur], axis=1)
    q8 = np.arange(8)[None, :]
    mc = np.where(k > q8, 0.0, NEG).astype(np.float32)
    maskc = np.tile(mc, (1, 16))
    k8 = np.arange(8)[:, None]
    mn = np.where(k8 <= q8, 0.0, NEG).astype(np.float32)
    maskn = np.full((128, 4 * 128), NEG, np.float32)
    for i in range(4):
        maskn[8 * i:8 * i + 8, 128 * i:128 * (i + 1)] = np.tile(mn, (1, 16))
    tril = np.tril(np.ones((128, 128), np.float32))
    cf = np.concatenate([ident, tril], axis=1).astype(np.float32)
    return ident, ones, Bd, Rm, maskN, maskF0, maskc, maskn, cf, bf


def prep_inputs(x_prompt, x_sample, cache_k, cache_v, norm1_g, w_in, q_norm_g, k_norm_g,
                attn_sinks, sgu_ln_g, sgu_ln_b, sgu_w, sgu_b, w_attn_up, w_sgu_up, w_out,
                norm2_g, w_ff1, w_ff2):
    f32 = np.float32
    x_prompt = np.asarray(x_prompt, f32); x_sample = np.asarray(x_sample, f32)
    cache_k = np.asarray(cache_k, f32); cache_v = np.asarray(cache_v, f32)
    w_in = np.asarray(w_in, f32)
    ident, ones, Bd, Rm, maskN, maskF0, maskc, maskn, cf, bf = _consts()

    w_in_p = np.zeros((DEPTH, D, IN_COLS_P), f32)
    w_in_p[:, :, 0:1024] = w_in[:, :, 0:1024]
    for j in range(4):
        kj = w_in[:, :, 1024 + 64 * j: 1024 + 64 * (j + 1)]
        base = OK_ + 256 * j
        w_in_p[:, :, base: base + 64] = kj
        w_in_p[:, :, base + 192: base + 256] = kj
    w_in_p[:, :, OV:OV + 256] = w_in[:, :, 1280:1536]
    w_in_p[:, :, OU:OU + 1024] = w_in[:, :, 1536:2560]
    w_in_p[:, :, OVS:OVS + 1024] = w_in[:, :, 2560:3584]
    w_in_p[:, :, OGA:OGA + 2048] = w_in[:, :, 3584:5632]
    w_in_p[:, :, OGM:OGM + 2048] = w_in[:, :, 5632:7680]

    vecs = np.zeros((128, NV_L * DEPTH), f32)
    for l in range(DEPTH):
        vecs[:, l * NV_L: l * NV_L + 16] = np.asarray(norm1_g[l], f32).reshape(16, 128).T
        vecs[:, l * NV_L + 16: l * NV_L + 32] = np.asarray(norm2_g[l], f32).reshape(16, 128).T
        vecs[:, l * NV_L + 32] = np.tile(np.asarray(q_norm_g[l], f32), 2)
        vecs[:, l * NV_L + 33] = np.tile(np.asarray(k_norm_g[l], f32), 2)
    lnG = np.ascontiguousarray(np.broadcast_to(np.asarray(sgu_ln_g, f32)[:, None, :], (DEPTH, 128, 1024)))
    lnB = np.ascontiguousarray(np.broadcast_to(np.asarray(sgu_ln_b, f32)[:, None, :], (DEPTH, 128, 1024)))
    sinks = np.ascontiguousarray(np.broadcast_to(np.asarray(attn_sinks, f32)[:, None, :], (DEPTH, 128, 16)))
    sgub = np.asarray(sgu_b, f32).reshape(DEPTH, 1, 1024)

    inv = np.power(np.float32(10000.0), -np.arange(32, dtype=np.float32) / np.float32(32))
    invp = np.tile(inv, 4).astype(f32)

    def tables(pos):
        ang = invp[:, None] * pos.astype(f32)[None, :]
        return np.cos(ang).astype(f32), np.sin(ang).astype(f32)

    shared = dict(w_in=w_in_p, w_au=np.asarray(w_attn_up, f32), w_su=np.asarray(w_sgu_up, f32),
                  w_out=np.asarray(w_out, f32), w_ff1=np.asarray(w_ff1, f32), w_ff2=np.asarray(w_ff2, f32),
                  vecs=vecs, lnG=lnG, lnB=lnB, sinks=sinks, sgu_w=np.asarray(sgu_w, f32), sgu_b=sgub, cf32=cf)
    in_maps = []
    for c in range(8):
        seq, part = c // 4, c % 4
        start = part * 1024 - 512
        xs = np.zeros((12 * 128, D), f32)
        lo = max(start, 0)
        xs[lo - start:] = x_prompt[seq, lo:start + 1536]
        pos = start + np.arange(1536)
        samp = x_sample[4 * c:4 * c + 4].reshape(32, D)
        pos_s = 16384 + np.tile(np.arange(8), 4)
        xa = np.concatenate([xs[128:768], samp, xs[0:128]], axis=0)
        pa = np.concatenate([pos[128:768], pos_s, pos[0:128]])
        xb = xs[768:1536]; pb = pos[768:1536]
        cA, sA = tables(pa); cB, sB = tables(pb)
        maskF = maskF0 if part == 0 else maskN
        cbf = np.concatenate([ident, ones, Bd, Rm, maskN, maskF, maskc, maskn], axis=1).astype(bf)
        m = dict(shared)
        m.update(xA=np.ascontiguousarray(xa.T), xB=np.ascontiguousarray(xb.T), cosA=cA, sinA=sA, cosB=cB, sinB=sB,
                 cache_k=np.ascontiguousarray(cache_k[:, 4 * c:4 * c + 4].reshape(DEPTH, 4, 128, 256)),
                 cache_v=np.ascontiguousarray(cache_v[:, 4 * c:4 * c + 4].reshape(DEPTH, 4, 128, 256)),
                 cbf=cbf)
        in_maps.append(m)
    return in_maps


def kernel(**inputs):
    in_maps = prep_inputs(**inputs)
    if "nc" not in _CACHE:
        _CACHE["nc"] = build_program()
    nc = _CACHE["nc"]
    res = run_bass_kernel_spmd(nc, in_maps, core_ids=list(range(8)))
    return assemble(res.results)


def assemble(R):
    f32 = np.float32
    y_prompt = np.zeros((2, 4096, D), f32)
    y_sample = np.zeros((32, 8, D), f32)
    nkp = np.zeros((DEPTH, 2, 128, 4, 64), f32); nvp = np.zeros_like(nkp)
    nks = np.zeros((DEPTH, 32, 128, 4, 64), f32); nvs = np.zeros_like(nks)
    nsv = np.zeros((DEPTH, 32, 8, 8, 128), f32)
    for c in range(8):
        seq, part = c // 4, c % 4
        yA = np.asarray(R[c]["yA"]).T
        yB = np.asarray(R[c]["yB"]).T
        base = part * 1024
        y_prompt[seq, base:base + 256] = yA[384:640]
        y_prompt[seq, base + 256:base + 1024] = yB
        y_sample[4 * c:4 * c + 4] = yA[640:672].reshape(4, 8, D)
        if part == 3:
            nkp[:, seq] = np.asarray(R[c]["kp"]).reshape(DEPTH, 128, 4, 64)
            nvp[:, seq] = np.asarray(R[c]["vp"]).reshape(DEPTH, 128, 4, 64)
        nks[:, 4 * c:4 * c + 4] = np.asarray(R[c]["ks"]).reshape(DEPTH, 4, 128, 4, 64)
        nvs[:, 4 * c:4 * c + 4] = np.asarray(R[c]["vs"]).reshape(DEPTH, 4, 128, 4, 64)
        nsv[:, 4 * c:4 * c + 4] = np.asarray(R[c]["sv"]).reshape(DEPTH, 4, 8, 8, 128)
    return (y_prompt, y_sample, nkp, nvp, nks, nvs, nsv)
```

```python
import math
from contextlib import ExitStack

import numpy as np
import ml_dtypes

import concourse.bass as bass
import concourse.mybir as mybir
from concourse.bass_utils import run_bass_kernel_spmd

F32 = mybir.dt.float32
BF16 = mybir.dt.bfloat16
AF = mybir.ActivationFunctionType
ALU = mybir.AluOpType
AX = mybir.AxisListType

D = 2048
NCH = 16
DEPTH = 4
WA = 672
WB = 768
XW = 800
IN_COLS_P = 8448
OQ, OK_, OV, OU, OVS, OGA, OGM = 0, 1024, 2048, 2304, 3328, 4352, 6400
NV_L = 34
NEG = -30000.0
EPL = 9
ATL = 9


class Op:
    __slots__ = ("eng", "emit", "deps", "needs_inc", "ticket", "chain", "semh")


class Sched:
    ENGS = ("pe", "act", "dve", "pool", "sp")

    def __init__(self):
        self.ops = {e: [] for e in self.ENGS}
        self.last_w = {}
        self.readers = {}
        self.chains = {}

    def add(self, eng, emit, reads=(), writes=(), chain=None):
        op = Op()
        op.eng = eng; op.emit = emit; op.deps = set(); op.needs_inc = False
        op.chain = chain; op.ticket = 0; op.semh = None
        lw = self.last_w; rd = self.readers
        for k in reads:
            w = lw.get(k)
            if w is not None:
                op.deps.add(w)
        for k in writes:
            w = lw.get(k)
            if w is not None:
                op.deps.add(w)
            r = rd.get(k)
            if r:
                op.deps.update(r.values())
        rk = ("c", chain, id(op)) if chain is not None else eng
        for k in reads:
            r = rd.get(k)
            if r is None:
                r = rd[k] = {}
            r[rk] = op
        for k in writes:
            lw[k] = op
            rd[k] = {}
        if chain is not None:
            ch = self.chains.setdefault(chain, [])
            if ch:
                op.deps.add(ch[-1])
            ch.append(op)
            op.needs_inc = True
        op.deps.discard(op)
        if eng == "pe":
            op.deps = {d for d in op.deps if not (d.eng == "pe" and d.chain is None)}
        for d in op.deps:
            d.needs_inc = True
        self.ops[eng].append(op)
        return op

    def finalize(self, eng_sems, chain_sems):
        for e in self.ENGS:
            n = 0
            for op in self.ops[e]:
                if op.chain is not None:
                    continue
                op.semh = eng_sems[e]
                if op.needs_inc:
                    n += 1
                    op.ticket = n
        for cname, ch in self.chains.items():
            for i, op in enumerate(ch):
                op.semh = chain_sems[cname]
                op.ticket = 16 * (i + 1)

    def emit_engine(self, e, h):
        waited = {}
        for op in self.ops[e]:
            need = {}
            for d in op.deps:
                key = id(d.semh)
                if need.get(key, (None, 0))[1] < d.ticket:
                    need[key] = (d.semh, d.ticket)
            for key, (s, v) in need.items():
                if waited.get(key, 0) < v:
                    h.wait_ge(s, v)
                    waited[key] = v
            if op.emit is not None:
                ins = op.emit(h)
                if op.needs_inc:
                    ins.then_inc(op.semh, 16 if op.chain is not None else 1)


def segs(c0, c1):
    n = c1 - c0
    ns = -(-n // 512)
    w = n // ns
    out = []
    for i in range(ns):
        a = c0 + i * w
        out.append((a, (c1 - a) if i == ns - 1 else w))
    return out


class _Stop(Exception):
    pass


def build_program(NL=DEPTH, TILES=("A", "B"), STOP=None):
    nc = bass.Bass("TRN2", target_bir_lowering=False)
    S = Sched()

    def din(name, shape, dt=F32):
        return nc.dram_tensor(name, list(shape), dt, kind="ExternalInput").ap()

    def dout(name, shape, dt=F32):
        return nc.dram_tensor(name, list(shape), dt, kind="ExternalOutput").ap()

    xA_d = din("xA", [D, XW]); xB_d = din("xB", [D, WB])
    cosA_d = din("cosA", [128, XW]); sinA_d = din("sinA", [128, XW])
    cosB_d = din("cosB", [128, WB]); sinB_d = din("sinB", [128, WB])
    win_d = din("w_in", [DEPTH, D, IN_COLS_P])
    wau_d = din("w_au", [DEPTH, 1024, D]); wsu_d = din("w_su", [DEPTH, 1024, D])
    wout_d = din("w_out", [DEPTH, D, D])
    w1_d = din("w_ff1", [DEPTH, D, 8192]); w2_d = din("w_ff2", [DEPTH, 8192, D])
    vecs_d = din("vecs", [128, NV_L * DEPTH])
    lnG_d = din("lnG", [DEPTH, 128, 1024]); lnB_d = din("lnB", [DEPTH, 128, 1024])
    sinks_d = din("sinks", [DEPTH, 128, 16])
    sguw_d = din("sgu_w", [DEPTH, 8, 128, 128]); sgub_d = din("sgu_b", [DEPTH, 1, 1024])
    ck_d = din("cache_k", [DEPTH, 4, 128, 256]); cv_d = din("cache_v", [DEPTH, 4, 128, 256])
    cb_d = din("cbf", [128, 4 * 128 + 2 * 512 + 128 + 512], BF16)
    cf_d = din("cf32", [128, 256])

    yA_d = dout("yA", [D, WA]); yB_d = dout("yB", [D, WB])
    kp_d = dout("kp", [DEPTH, 128, 256]); vp_d = dout("vp", [DEPTH, 128, 256])
    ks_d = dout("ks", [DEPTH, 4, 128, 256]); vs_d = dout("vs", [DEPTH, 4, 128, 256])
    sv_d = dout("sv", [DEPTH, 32, 1024])
    scr_d = nc.dram_tensor("scr", [DEPTH, 128, 1024 + 260], BF16, kind="Internal").ap()

    es = ExitStack()
    with es:
        def sb(name, shape, dt):
            return es.enter_context(nc.sbuf_tensor("sb_" + name, list(shape), dt))

        x = sb("x", [128, NCH, WB], F32)
        xn = sb("xn", [128, NCH, XW], BF16)
        R1 = sb("R1", [128, NCH, WB], BF16)
        R2 = sb("R2", [128, NCH, WB], BF16)
        wsl = [sb(f"wsl{i}", [128, 4096], BF16) for i in range(4)]
        cosT = sb("cosT", [128, XW], F32); sinT = sb("sinT", [128, XW], F32)
        vaug = sb("vaug", [128, 8, 4, 65], BF16)
        kTprev = sb("kTprev", [128, 8, 128], BF16)
        SC = sb("SC", [128, 2048], F32)
        arena = sb("arena", [128, 4096], BF16)
        bscr = arena[:, 0:1024]
        lnG = sb("lnG", [128, 1024], F32); lnB = sb("lnB", [128, 1024], F32)
        a_tok = arena[:, 2048:3072]
        pT = [arena[:, 3072:3584], arena[:, 3584:4096]]
        pTn = sb("pTn", [128, 256], BF16)
        wsT = sb("wsT", [128, 8, 128], BF16)
        brow = sb("brow", [128, 1024], BF16)
        kc32 = [sb("kc32_0", [128, 256], F32)] * 2
        vc32 = [sb("vc32_0", [128, 256], F32)] * 2
        kcd = sb("kcd", [128, 4, 2, 128], BF16)
        vcaug = sb("vcaug", [128, 4, 65], BF16)
        cb = sb("cb", [128, 4 * 128 + 2 * 512 + 128 + 512], BF16)
        cf = sb("cf", [128, 256], F32)
        vecs = sb("vecs", [128, NV_L * DEPTH], F32)
        sk32 = sb("sk32", [128, 16], F32); esk = sb("esk", [128, 16], F32)
        den = sb("den", [128, 16], F32); rden = sb("rden", [128, 16], F32)
        st = sb("st", [128, 8], F32)
        dummy = sb("dummy", [128, 2], F32)
        kst = sb("kst", [128, 256], F32); vst = sb("vst", [128, 256], F32)
        kfin = arena[:, 1024:2048].bitcast(F32).rearrange("p (a b) -> p a b", a=4)
        gv2 = arena[:, 0:2048].bitcast(F32); lnt2 = arena[:, 2048:4096].bitcast(F32)
        kcT = a_tok[:, :].rearrange("p (c t) -> p c t", c=8)
        psb = [es.enter_context(nc.psum_tensor(f"ps{i}", [128, 512], F32)) for i in range(8)]
        _CACHE["sbuf_left"] = nc.sbuf_bytes_remaining

        identb = cb[:, 0:128]; onesb = cb[:, 128:256]; Bd = cb[:, 256:384]; Rm = cb[:, 384:512]
        maskN = cb[:, 512:1024]; maskF = cb[:, 1024:1536]; maskc = cb[:, 1536:1664]
        maskn = lambda i: cb[:, 1664 + 128 * i: 1792 + 128 * i]
        identf = cf[:, 0:128]; tril = cf[:, 128:256]
        ones_row = cb[0:1, 128:256]

        eng_sems = {e: es.enter_context(nc.semaphore("sem_" + e)) for e in Sched.ENGS}
        chain_names = ["w0", "w1", "w2", "w3", "const", "xload", "par", "par2", "cache0", "cache1", "scrw", "scrr",
                       "oy", "okp", "ovp", "oks", "ovs", "osv", "occ"]
        chain_sems = {c: es.enter_context(nc.semaphore("ch_" + c)) for c in chain_names}

        free_banks = list(range(8))

        def balloc():
            assert free_banks, "out of PSUM banks"
            return free_banks.pop(0)

        def bfree(b):
            free_banks.append(b)

        def dma(queue, chain, out, in_, reads, writes):
            def emit(h, out=out, in_=in_):
                return h.dma_start(out=out, in_=in_)
            S.add(queue, emit, reads=reads, writes=writes, chain=chain)

        def act(out, in_, func, reads, writes, scale=None, bias=None):
            def emit(h):
                kw = {}
                if scale is not None:
                    kw["scale"] = scale
                if bias is not None:
                    kw["bias"] = bias
                return h.activation(out=out, in_=in_, func=func, **kw)
            S.add("act", emit, reads=reads, writes=writes)

        def dve(fn, reads, writes):
            S.add("dve", fn, reads=reads, writes=writes)

        def pe(items, reads, writes):
            def emit(h, items=items):
                ins = None
                for it in items:
                    if it[0] == "T":
                        ins = h.transpose(out=it[1], in_=it[2], identity=it[3])
                    else:
                        ins = h.matmul(it[0], it[1], it[2], start=it[3], stop=it[4])
                return ins
            S.add("pe", emit, reads=reads, writes=writes)

        def rstd_ops(out, in_ps, scale, eps, reads, writes):
            act(out, in_ps, AF.Ln, reads, writes, scale=scale, bias=eps)
            act(out, out, AF.Exp, writes, writes, scale=-0.5)

        wstate = {"n": 0}
        ssq = {"banks": None}

        def ssq_open(seglist):
            ssq["banks"] = {c0: balloc() for (c0, n, tag) in seglist}

        def ssq_take():
            b = ssq["banks"]; ssq["banks"] = None
            return b

        def wload(view, nk, ncols):
            s = wstate["n"] % 4
            wstate["n"] += 1
            dst = wsl[s][:, 0:nk * ncols].rearrange("p (k n) -> p k n", k=nk)
            dma("pool", f"w{s}", dst, view, reads=(), writes=(("w", s),))
            return dst, ("w", s)

        def proj(parts, ncols, seglist, epilogue, cg=512):
            ngroups = -(-ncols // cg)
            pending = [None]
            for g in range(ngroups):
                gc = min(cg, ncols - g * cg)
                loaded = []
                for (W, row0, nk, col0, src_fn, src_keys) in parts:
                    k0 = 0
                    while k0 < nk:
                        kk = min(4096 // gc, nk - k0)
                        view = W[row0 + k0 * 128: row0 + (k0 + kk) * 128, col0 + g * cg: col0 + g * cg + gc] \
                            .rearrange("(k p) n -> p k n", p=128)
                        dst, key = wload(view, kk, gc)
                        loaded.append((dst, key, k0, kk, src_fn, src_keys))
                        k0 += kk
                for seg in seglist:
                    c0, n, tag = seg
                    for cl in range(gc // 128):
                        ci = g * (cg // 128) + cl
                        bank = balloc()
                        items = []
                        rkeys = []
                        tot = sum(l[3] for l in loaded)
                        i = 0
                        for (dst, key, k0, kk, src_fn, src_keys) in loaded:
                            rkeys.append(key)
                            rkeys.extend(src_keys)
                            for k in range(kk):
                                items.append((psb[bank][:, 0:n], dst[:, k, cl * 128:(cl + 1) * 128],
                                              src_fn(k0 + k, c0, n), i == 0, i == tot - 1))
                                i += 1
                        pe(items, reads=rkeys, writes=(("ps", bank),))
                        if pending[0] is not None:
                            pending[0](); pending[0] = None
                        r = epilogue(ci, seg, bank)
                        bfree(bank)
                        if callable(r):
                            pending[0] = r
            if pending[0] is not None:
                pending[0](); pending[0] = None

        dma("sp", "const", cb[:, :], cb_d[:, :], (), (("cb",),))
        dma("sp", "const", cf[:, :], cf_d[:, :], (), (("cf",),))
        dma("sp", "const", vecs[:, :], vecs_d[:, :], (), (("vecs",),))
        for e in ("pe", "act", "dve"):
            S.add(e, None, reads=(("cb",), ("cf",), ("vecs",)), writes=())
        dve(lambda h: h.memset(vaug[:, :, :, :].rearrange("p a b c -> p (a b c)"), 1.0), (), tuple(("vaug", i) for i in range(8)))
        dve(lambda h: h.memset(vcaug[:, :, :].rearrange("p a b -> p (a b)"), 1.0), (), (("vcaug",),))
        dve(lambda h: h.memset(brow[:, :], 0.0), (), (("brow",),))
        dve(lambda h: h.memset(pTn[:, :], 0.0), (), (("pTn",),))
        dve(lambda h: h.memset(kcd[:, :, :, :].rearrange("p a b c -> p (a b c)"), 0.0), (), (("kcd",),))
        dve(lambda h: h.memset(a_tok[:, :], 0.0), (), tuple(("a_tok", c) for c in range(8)))
        dve(lambda h: h.memset(kfin[:, :, :].rearrange("p a b -> p (a b)"), 0.0), (), (("kfin",),))
        for i_ in range(2):
            dve(lambda h, i_=i_: h.memset(pT[i_][:, :], 0.0), (), (("pT", i_),))
        dve(lambda h: h.memset(R1[:, :, :].rearrange("p a b -> p (a b)"), 0.0), (), tuple(("R1", c) for c in range(NCH)))
        dve(lambda h: h.memset(R2[:, :, :].rearrange("p a b -> p (a b)"), 0.0), (), tuple(("R2", c) for c in range(NCH)))
        dve(lambda h: h.memset(xn[:, :, :].rearrange("p a b -> p (a b)"), 0.0), (), tuple(("xn", c) for c in range(NCH)))

        def xk(k):
            return ("x", k)

        def stop(n):
            if STOP == n:
                raise _Stop()

        try:
          for tile in TILES:
              W = WA if tile == "A" else WB
              nblk_tile = 5 if tile == "A" else 6
              if tile == "A":
                  xv = xA_d.rearrange("(k p) n -> p k n", p=128)
                  dma("sp", "xload", x[:, :, 0:WA], xv[:, :, 0:WA], (), tuple(xk(k) for k in range(NCH)))
                  x0 = SC[:, :].rearrange("p (k n) -> p k n", k=NCH)
                  dma("sp", "xload", x0, xv[:, :, WA:XW], (), tuple(("S", i) for i in range(4)))
                  dma("sp", "xload", cosT[:, :], cosA_d[:, :], (), (("cos",),))
                  dma("sp", "xload", sinT[:, :], sinA_d[:, :], (), (("sin",),))
              else:
                  xv = xB_d.rearrange("(k p) n -> p k n", p=128)
                  dma("sp", "xload", x[:, :, 0:WB], xv[:, :, :], (), tuple(xk(k) for k in range(NCH)))
                  dma("sp", "xload", cosT[:, 0:WB], cosB_d[:, :], (), (("cos",),))
                  dma("sp", "xload", sinT[:, 0:WB], sinB_d[:, :], (), (("sin",),))

              carry = None
              for l in range(NL):
                  vb = l * NV_L
                  g1 = lambda k, vb=vb: vecs[:, vb + k: vb + k + 1]
                  g2 = lambda k, vb=vb: vecs[:, vb + 16 + k: vb + 17 + k]
                  qg = vecs[:, vb + 32: vb + 33]; kg = vecs[:, vb + 33: vb + 34]
                  if tile == "A":
                      cf0 = 128 * l
                      ck0 = 128 * (l - 1) if l >= 1 else 0
                      blocks = list(range(l + 1, 6))
                      bcol = lambda b: (b - 1) * 128
                      kvblk = l
                  else:
                      cf0 = 0; ck0 = 0
                      blocks = list(range(6, 12))
                      bcol = lambda b: (b - 6) * 128
                      kvblk = None
                  vslot = lambda b: (b - 1) if tile == "A" else (b - 6)
                  full_segs = [(a, n, "m") for (a, n) in segs(cf0, W)]
                  kv_segs = [(a, n, "m") for (a, n) in segs(ck0, W)]
                  if tile == "A" and l == 0:
                      kv_segs = kv_segs + [(WA, 128, "b0")]

                  dma("sp", "par", lnG[:, :], lnG_d[l], (), (("lnG",),))
                  dma("sp", "par", lnB[:, :], lnB_d[l], (), (("lnB",),))
                  dma("sp", "par", sk32[:, :], sinks_d[l], (), (("sk32",),))
                  dma("pool", "par2", brow[0:1, :], sgub_d[l], (), (("brow",),))
                  act(esk[:, :], sk32[:, :], AF.Exp, (("sk32",),), (("esk",),))

                  def rmsnorm(src_is_x0, c0, n, gfn, dstc0, ssbank=None):
                      bank = balloc() if ssbank is None else ssbank
                      for k in range(NCH if ssbank is None else 0):
                          sq = bscr[:, (k % 2) * 512:(k % 2) * 512 + n]
                          src = x0[:, k, 0:n] if src_is_x0 else x[:, k, c0:c0 + n]
                          skeys = (("S", 0), ("S", 1), ("S", 2), ("S", 3)) if src_is_x0 else (xk(k),)
                          if k % 2 == 0:
                              act(sq, src, AF.Square, skeys, (("bs", 0),))
                          else:
                              dve(lambda h, sq=sq, src=src: h.tensor_tensor(out=sq, in0=src, in1=src, op=ALU.mult), skeys, (("bs", 1),))
                          pe([(psb[bank][:, 0:n], onesb, sq, k == 0, k == NCH - 1)], (("bs", k % 2),), (("ps", bank),))
                      rs = SC[:, 0:n] if not src_is_x0 else None
                      if src_is_x0:
                          rs = kfin[:, 0, 0:n]
                          rkey = ("kfin",)
                      else:
                          rkey = ("S", 0) if n <= 512 else None
                      wk = (rkey,) if rkey is not None else (("S", 0), ("S", 1))
                      rstd_ops(rs, psb[bank][:, 0:n], 1.0 / D, 1e-6, (("ps", bank),), wk)
                      bfree(bank)
                      for k in range(NCH):
                          src = x0[:, k, 0:n] if src_is_x0 else x[:, k, c0:c0 + n]
                          skeys = (("S", 0), ("S", 1), ("S", 2), ("S", 3)) if src_is_x0 else (xk(k),)
                          dve(lambda h, src=src, k=k: h.scalar_tensor_tensor(
                              out=xn[:, k, dstc0:dstc0 + n], in0=src, scalar=gfn(k), in1=rs, op0=ALU.mult, op1=ALU.mult),
                              skeys + wk, (("xn", k),))

                  if tile == "A" and l == 0:
                      rmsnorm(True, 0, 128, g1, WA)
                  for (a, n) in segs(ck0, W):
                      rmsnorm(False, a, n, g1, a, ssbank=(carry[a] if carry is not None else None))
                  carry = None
                  xn_keys = tuple(("xn", k) for k in range(NCH))
                  xn_src = lambda k, c0, n: xn[:, k, c0:c0 + n]

                  stop(1)
                  ws32 = SC[:, 1024:2048].rearrange("p (g s) -> p g s", g=8)
                  dma("sp", "par", ws32, sguw_d[l].rearrange("g t s -> t g s"), (), (("S", 2), ("S", 3)))
                  for g in range(8):
                      dve(lambda h, g=g: h.tensor_tensor(out=ws32[:, g, :], in0=ws32[:, g, :], in1=tril, op=ALU.mult),
                          (("S", 2), ("S", 3)), (("S", 2), ("S", 3)))
                  for half in range(2):
                      bank = balloc()
                      items = [("T", psb[bank][:, j * 128:(j + 1) * 128], ws32[:, half * 4 + j, :], identf) for j in range(4)]
                      pe(items, (("S", 2), ("S", 3)), (("ps", bank),))
                      act(wsT[:, half * 4:half * 4 + 4, :], psb[bank][:, :].rearrange("p (g t) -> p g t", g=4), AF.Copy,
                          (("ps", bank),), (("wsT",),))
                      bfree(bank)

                  stop(2)
                  qT = R1[:, 0:8, :]; kT = R1[:, 8:16, :]

                  def qk_epilogue(is_q):
                      gain = qg if is_q else kg

                      def ep(ci, seg, bank):
                          c0, n, tag = seg
                          z = psb[bank][:, 0:n]
                          sqz = bscr[:, 0:n]; y = bscr[:, 512:512 + n]
                          rs = SC[:, 0:n]; t1 = SC[:, 512:512 + n]; t2 = SC[:, 1024:1024 + n]
                          act(sqz, z, AF.Square, (("ps", bank),), (("bs", 0),))
                          act(y, z, AF.Copy, (("ps", bank),), (("bs", 1),), scale=gain)

                          def later():
                              b2 = balloc(); b3 = balloc()
                              pe([(psb[b2][:, 0:n], Bd, sqz, True, True)], (("bs", 0),), (("ps", b2),))
                              pe([(psb[b3][:, 0:n], Rm, y, True, True)], (("bs", 1),), (("ps", b3),))
                              rstd_ops(rs, psb[b2][:, 0:n], 1.0 / 64, 1e-6, (("ps", b2),), (("S", 0),))
                              dve(lambda h: h.tensor_tensor(out=t1, in0=y, in1=cosT[:, c0:c0 + n], op=ALU.mult),
                                  (("bs", 1), ("cos",)), (("S", 1),))
                              dve(lambda h: h.tensor_tensor(out=t2, in0=psb[b3][:, 0:n], in1=sinT[:, c0:c0 + n], op=ALU.mult),
                                  (("ps", b3), ("sin",)), (("S", 2),))
                              dve(lambda h: h.tensor_tensor(out=t1, in0=t1, in1=t2, op=ALU.add),
                                  (("S", 1), ("S", 2)), (("S", 1),))
                              if is_q:
                                  o = qT[:, ci, c0:c0 + n]; ok = ("R1", ci)
                              elif tag == "b0":
                                  o = kTprev[:, ci, 0:n]; ok = ("kTprev",)
                              else:
                                  o = kT[:, ci, c0:c0 + n]; ok = ("R1", 8 + ci)
                              dve(lambda h: h.tensor_tensor(out=o, in0=t1, in1=rs, op=ALU.mult),
                                  (("S", 1), ("S", 0)), (ok,))
                              if not is_q:
                                  if tile == "B" and c0 + n == WB and ci % 2 == 0:
                                      lo = WB - 128 - c0
                                      dve(lambda h: h.tensor_tensor(out=kfin[:, ci // 2, :], in0=t1[:, lo:lo + 128], in1=rs[:, lo:lo + 128], op=ALU.mult),
                                          (("S", 1), ("S", 0)), (("kfin",),))
                                  if tile == "A" and tag == "m" and c0 + n == WA and ci % 2 == 0:
                                      lo = WA - 32 - c0
                                      dve(lambda h: h.tensor_tensor(out=kfin[:, ci // 2, 0:32], in0=t1[:, lo:lo + 32], in1=rs[:, lo:lo + 32], op=ALU.mult),
                                          (("S", 1), ("S", 0)), (("kfin",),))
                              bfree(b2); bfree(b3)
                          return later
                      return ep

                  proj([(win_d[l], 0, NCH, OQ, xn_src, xn_keys)], 1024, full_segs, qk_epilogue(True))
                  proj([(win_d[l], 0, NCH, OK_, xn_src, xn_keys)], 1024, kv_segs, qk_epilogue(False))

                  stop(3)
                  vview = win_d[l][:, OV:OV + 256].rearrange("(k p) n -> p k n", p=128)
                  wv, wvkey = wload(vview, NCH, 256)

                  def vproj(colstart, m, dst_ap, dkey, f32_dst=None, f32key=None):
                      bank = balloc()
                      items = [(psb[bank][:, 0:256], xn[:, k, colstart:colstart + 128], wv[:, k, :], k == 0, k == NCH - 1)
                               for k in range(NCH)]
                      pe(items, (wvkey,) + xn_keys, (("ps", bank),))
                      act(dst_ap, psb[bank][:, 0:256].rearrange("p (h d) -> p h d", h=4), AF.Copy, (("ps", bank),), (dkey,))
                      if f32_dst is not None:
                          act(f32_dst, psb[bank][:, 0:256], AF.Copy, (("ps", bank),), (f32key,))
                      bfree(bank)

                  if tile == "A" and l == 0:
                      vproj(WA, 128, vaug[:, 7, :, 0:64], ("vaug", 7))
                  kvb_list = ([kvblk] if (kvblk is not None and l >= 1) else []) + blocks
                  for b in kvb_list:
                      last = (tile == "B" and b == 11)
                      vproj(bcol(b), 128, vaug[:, vslot(b), :, 0:64], ("vaug", vslot(b)),
                            vst[:, :] if last else None, ("vst",) if last else None)
                  if tile == "B":
                      dma("sp", "scrr", kTprev[:, :, :], scr_d[l][:, 0:1024].rearrange("p (c t) -> p c t", c=8),
                          (("scr", l),), (("kTprev",),))
                      dma("sp", "scrr", vaug[:, 7, :, :], scr_d[l][:, 1024:1284].rearrange("p (h d) -> p h d", h=4),
                          (("scr", l),), (("vaug", 7),))

                  stop(4)
                  aT = R2[:, 0:8, :]; mT = R2[:, 8:16, :]
                  qkeys = tuple(("R1", c) for c in range(8))
                  kkeys = tuple(("R1", 8 + c) for c in range(8))
                  akeys = tuple(("R2", c) for c in range(8))

                  def attn_norm(np_, obanks):
                      for bi, ob in enumerate(obanks):
                          h0 = bi * 7; nh = min(7, 16 - h0)
                          ov = psb[ob][0:np_, 0:nh * 65].rearrange("p (h e) -> p h e", h=nh)
                          dve(lambda h, ov=ov, h0=h0, nh=nh: h.tensor_tensor(out=den[0:np_, h0:h0 + nh], in0=ov[:, :, 64],
                                                                           in1=esk[0:np_, h0:h0 + nh], op=ALU.add),
                              (("ps", ob), ("esk",)), (("den", bi),))
                          dve(lambda h, h0=h0, nh=nh: h.reciprocal(out=rden[0:np_, h0:h0 + nh], in_=den[0:np_, h0:h0 + nh]),
                              (("den", bi),), (("rden", bi),))
                          dve(lambda h, ov=ov, h0=h0, nh=nh: h.tensor_tensor(
                              out=a_tok[0:np_, h0 * 64:(h0 + nh) * 64].rearrange("p (h d) -> p h d", h=nh), in0=ov[:, :, 0:64],
                              in1=rden[0:np_, h0:h0 + nh].unsqueeze(2).broadcast_to([np_, nh, 64]), op=ALU.mult),
                              (("ps", ob), ("rden", bi)), tuple(("a_tok", c) for c in range(h0 // 2, (h0 + nh - 1) // 2 + 1)))
                          bfree(ob)

                  def attn_tr(col0, ncols):
                      for half in range(2):
                          bank = balloc()
                          pv = psb[bank][:, 0:256].bitcast(BF16)
                          items = [("T", pv[:, j * 128:(j + 1) * 128], a_tok[:, (half * 4 + j) * 128:(half * 4 + j + 1) * 128],
                                    identb) for j in range(4)]
                          pe(items, tuple(("a_tok", half * 4 + j) for j in range(4)), (("ps", bank),))
                          act(aT[:, half * 4:half * 4 + 4, col0:col0 + ncols],
                              pv[:, 0:512].rearrange("p (c t) -> p c t", c=4)[:, :, 0:ncols], AF.Copy,
                              (("ps", bank),), akeys[half * 4:half * 4 + 4])
                          bfree(bank)

                  atok_all = tuple(("a_tok", c) for c in range(8))
                  pti = [0]
                  pend_tr = [None]
                  for b in blocks:
                      c_q = bcol(b)
                      first = (b == blocks[0]) and (tile == "B" or l == 0)
                      if first:
                          kprev = lambda kv, hf: kTprev[:, 2 * kv + hf, :]
                          kprev_key = ("kTprev",); vprev = 7
                      else:
                          cp = bcol(b - 1)
                          kprev = lambda kv, hf, cp=cp: kT[:, 2 * kv + hf, cp:cp + 128]
                          kprev_key = None; vprev = vslot(b - 1)
                      mask_ap = maskF if (b == 4) else maskN
                      obanks = [balloc(), balloc(), balloc()]

                      def pv_mm(pr, ps_i, obanks=obanks, vprev=vprev, b=b):
                          items = []
                          for hf in range(2):
                              hd = 2 * pr + hf; kv = hd // 4
                              ob = obanks[hd // 7]; off = (hd % 7) * 65
                              items.append((psb[ob][:, off:off + 65], pT[ps_i][:, (hf * 2) * 128:(hf * 2 + 1) * 128],
                                            vaug[:, vprev, kv, :], True, False))
                              items.append((psb[ob][:, off:off + 65], pT[ps_i][:, (hf * 2 + 1) * 128:(hf * 2 + 2) * 128],
                                            vaug[:, vslot(b), kv, :], False, True))
                          pe(items, (("pT", ps_i), ("vaug", vprev), ("vaug", vslot(b))), tuple(("ps", o) for o in obanks))

                      prev = None
                      for pr in range(8):
                          bank = balloc()
                          items = [(psb[bank][:, 0:512], identb, mask_ap, True, False)]
                          for hf in range(2):
                              hd = 2 * pr + hf; kv = hd // 4
                              q_ap = qT[:, pr, c_q:c_q + 128]
                              items.append((psb[bank][:, (hf * 2) * 128:(hf * 2 + 1) * 128], kprev(kv, hf), q_ap, False, False))
                              items.append((psb[bank][:, (hf * 2 + 1) * 128:(hf * 2 + 2) * 128],
                                            kT[:, 2 * kv + hf, c_q:c_q + 128], q_ap, False, hf == 1))
                          rk = qkeys + kkeys + ((kprev_key,) if kprev_key else ())
                          pe(items, rk, (("ps", bank),))
                          ps_i = pti[0] % 2; pti[0] += 1
                          act(pT[ps_i][:, :], psb[bank][:, 0:512], AF.Exp, (("ps", bank),), (("pT", ps_i),), scale=0.125)
                          bfree(bank)
                          if prev is not None:
                              pv_mm(*prev)
                          prev = (pr, ps_i)
                          if pr == 2 and pend_tr[0] is not None:
                              attn_tr(*pend_tr[0]); pend_tr[0] = None
                      pv_mm(*prev)
                      attn_norm(128, obanks)
                      pend_tr[0] = (c_q, 128)
                  if pend_tr[0] is not None:
                      attn_tr(*pend_tr[0]); pend_tr[0] = None

                  stop(5)
                  if tile == "B":
                      bank = balloc()
                      items = [("T", psb[bank][:, j * 128:(j + 1) * 128], kfin[:, j, :], identf) for j in range(4)]
                      pe(items, (("kfin",),), (("ps", bank),))
                      act(kst[:, :].rearrange("p (h d) -> p h d", h=4), psb[bank][:, :].rearrange("p (h e) -> p h e", h=4)[:, :, 0:64],
                          AF.Copy, (("ps", bank),), (("kst",),))
                      bfree(bank)
                      dma("sp", "okp", kp_d[l], kst[:, :], (("kst",),), (("okp", l),))
                      dma("sp", "ovp", vp_d[l], vst[:, :], (("vst",),), (("ovp", l),))
                  else:
                      c5 = bcol(5)
                      dma("sp", "scrw", scr_d[l][:, 0:1024].rearrange("p (c t) -> p c t", c=8), kT[:, :, c5:c5 + 128],
                          kkeys, (("scr", l),))
                      dma("sp", "scrw", scr_d[l][:, 1024:1284].rearrange("p (h d) -> p h d", h=4), vaug[:, vslot(5), :, :],
                          (("vaug", vslot(5)),), (("scr", l),))

                  stop(6)
                  if tile == "A":
                      vproj(640, 128, vaug[:, 6, :, 0:64], ("vaug", 6), vst[:, :], ("vst",))
                      for i in range(4):
                          cs = 640 + 8 * i
                          dma("sp", "cache0", kc32[0][:, :], ck_d[l, i], (), (("kc32", 0),))
                          dma("sp", "cache0", vc32[0][:, :], cv_d[l, i], (), (("vc32", 0),))
                          dma("sp", "occ", ks_d[l, i, 0:120, :], ck_d[l, i, 8:128, :], (), (("oks_c", l, i),))
                          dma("sp", "occ", vs_d[l, i, 0:120, :], cv_d[l, i, 8:128, :], (), (("ovs_c", l, i),))
                          kcv = kc32[0][:, :].rearrange("p (h d) -> p h d", h=4)
                          act(kcd[:, :, 0, 0:64], kcv, AF.Copy, (("kc32", 0),), (("kcd",),))
                          act(kcd[:, :, 1, 64:128], kcv, AF.Copy, (("kc32", 0),), (("kcd",),))
                          act(vcaug[:, :, 0:64], vc32[0][:, :].rearrange("p (h d) -> p h d", h=4), AF.Copy,
                              (("vc32", 0),), (("vcaug",),))
                          kcd8 = kcd[:, :, :, :].rearrange("p a b c -> p (a b) c")
                          for half in range(2):
                              bank = balloc()
                              pv = psb[bank][:, 0:256].bitcast(BF16)
                              items = [("T", pv[:, j * 128:(j + 1) * 128], kcd8[:, half * 4 + j, :], identb) for j in range(4)]
                              pe(items, (("kcd",),), (("ps", bank),))
                              act(kcT[:, half * 4:half * 4 + 4, :], pv[:, 0:512].rearrange("p (c t) -> p c t", c=4), AF.Copy,
                                  (("ps", bank),), atok_all[half * 4:half * 4 + 4])
                              bfree(bank)
                          bc = balloc(); bn = balloc()
                          items = [(psb[bc][:, 0:128], identb, maskc, True, False)]
                          for hd in range(16):
                              hf = hd % 2; kv = hd // 4; pr = hd // 2
                              items.append((psb[bc][:, hd * 8:(hd + 1) * 8], kcT[:, 2 * kv + hf, :],
                                            qT[:, pr, cs:cs + 8], False, hd == 15))
                          pe(items, atok_all + qkeys, (("ps", bc),))
                          items = [(psb[bn][:, 0:128], identb, maskn(i), True, False)]
                          for hd in range(16):
                              hf = hd % 2; kv = hd // 4; pr = hd // 2
                              items.append((psb[bn][:, hd * 8:(hd + 1) * 8], kT[:, 2 * kv + hf, 640:768],
                                            qT[:, pr, cs:cs + 8], False, hd == 15))
                          pe(items, kkeys + qkeys, (("ps", bn),))
                          act(pT[0][:, 0:128], psb[bc][:, 0:128], AF.Exp, (("ps", bc),), (("pT", 0),), scale=0.125)
                          act(pTn[0:32, 0:128], psb[bn][0:32, 0:128], AF.Exp, (("ps", bn),), (("pTn",),), scale=0.125)
                          bfree(bc); bfree(bn)
                          obanks = [balloc(), balloc(), balloc()]
                          items = []
                          for hd in range(16):
                              kv = hd // 4
                              ob = obanks[hd // 7]; off = (hd % 7) * 65
                              items.append((psb[ob][:, off:off + 65], pT[0][:, hd * 8:hd * 8 + 128], vcaug[:, kv, :], True, False))
                              items.append((psb[ob][:, off:off + 65], pTn[:, hd * 8:hd * 8 + 128], vaug[:, 6, kv, :], False, True))
                          pe(items, (("pT", 0), ("pTn",), ("vcaug",), ("vaug", 6)), tuple(("ps", o) for o in obanks))
                          attn_norm(8, obanks)
                          attn_tr(cs, 8)
                      bank = balloc()
                      items = [("T", psb[bank][:, j * 128:(j + 1) * 128], kfin[:, j, :], identf) for j in range(4)]
                      pe(items, (("kfin",),), (("ps", bank),))
                      act(kst[:, :].rearrange("p (h d) -> p h d", h=4), psb[bank][:, :].rearrange("p (h e) -> p h e", h=4)[:, :, 0:64],
                          AF.Copy, (("ps", bank),), (("kst",),))
                      bfree(bank)
                      for i in range(4):
                          dma("sp", "oks", ks_d[l, i, 120:128, :], kst[8 * i:8 * i + 8, :], (("kst",),), (("oks", l, i),))
                          dma("sp", "ovs", vs_d[l, i, 120:128, :], vst[8 * i:8 * i + 8, :], (("vst",),), (("ovs", l, i),))

                  stop(7)
                  uT = R1[:, 0:8, :]
                  ukeys = tuple(("R1", c) for c in range(8))

                  def u_ep(ci, seg, bank):
                      c0, n, tag = seg
                      act(uT[:, ci, c0:c0 + n], psb[bank][:, 0:n], AF.Gelu, (("ps", bank),), (("R1", ci),))
                  proj([(win_d[l], 0, NCH, OU, xn_src, xn_keys)], 1024, full_segs, u_ep)

                  vsw = []
                  for hv in range(2):
                      for kh in range(2):
                          view = win_d[l][kh * 1024:(kh + 1) * 1024, OVS + hv * 512: OVS + (hv + 1) * 512] \
                              .rearrange("(k p) n -> p k n", p=128)
                          vsw.append(wload(view, 8, 512))
                  vsn_all = R1[:, 8:16, :].rearrange("p k t -> p (k t)")
                  vsnk = lambda sl: (("vsn", sl),)
                  gv = SC[:, 0:1024]; lnt = SC[:, 1024:2048]
                  mkeys = tuple(("R2", 8 + c) for c in range(8))

                  def sgu_stage1(colstart, m, vsn_bf, vsn_key, is_sample, samp_idx=None, par=0):
                      for hv in range(2):
                          bank = balloc()
                          items = []
                          for kh in range(2):
                              wdst, wkey = vsw[hv * 2 + kh]
                              for k in range(8):
                                  kk = kh * 8 + k
                                  items.append((psb[bank][:, 0:512], xn[:, kk, colstart:colstart + 128], wdst[:, k, :],
                                                kk == 0, kk == NCH - 1))
                          pe(items, tuple(w[1] for w in vsw) + xn_keys, (("ps", bank),))
                          gvp = gv if par == 0 else gv2
                          gkeys = (("S", hv),) if par == 0 else (("bs", 0), ("bs", 1), ("kfin",))
                          act(gvp[0:m, hv * 512:(hv + 1) * 512], psb[bank][0:m, 0:512], AF.Gelu, (("ps", bank),), gkeys)
                          bfree(bank)
                      if par == 0:
                          G = gv[0:m, :]; T = lnt[0:m, :]
                          sk = (("S", 0), ("S", 1)); tk = (("S", 2), ("S", 3))
                      else:
                          G = gv2[0:m, :]; T = lnt2[0:m, :]
                          sk = (("bs", 0), ("bs", 1), ("kfin",)); tk = atok_all + (("pT", 0), ("pT", 1))
                      so = 4 * par
                      dve(lambda h: h.tensor_reduce(out=st[0:m, so:so + 1], in_=G, axis=AX.X, op=ALU.add), sk, (("st", so),))
                      dve(lambda h: h.tensor_scalar(out=st[0:m, so + 1:so + 2], in0=st[0:m, so:so + 1], scalar1=-1.0 / 1024, scalar2=None, op0=ALU.mult),
                          (("st", so),), (("st", so + 1),))
                      act(T, G, AF.Identity, sk + (("st", so + 1),), tk, bias=st[0:m, so + 1:so + 2])
                      dve(lambda h: h.tensor_tensor(out=G, in0=T, in1=T, op=ALU.mult), tk, sk)
                      dve(lambda h: h.tensor_reduce(out=st[0:m, so + 2:so + 3], in_=G, axis=AX.X, op=ALU.add), sk, (("st", so + 2),))
                      rstd_ops(st[0:m, so + 3:so + 4], st[0:m, so + 2:so + 3], 1.0 / 1024, 1e-5, (("st", so + 2),), (("st", so + 3),))
                      dve(lambda h: h.scalar_tensor_tensor(out=T, in0=T, scalar=st[0:m, so + 3:so + 4], in1=lnG[0:m, :],
                                                           op0=ALU.mult, op1=ALU.mult), tk + (("st", so + 3), ("lnG",)), tk)
                      if is_sample:
                          dve(lambda h: h.tensor_tensor(out=T, in0=T, in1=lnB[0:m, :], op=ALU.add), tk + (("lnB",),), tk)
                          dve(lambda h: h.memset(vsn_bf[:, :], 0.0), (), vsn_key)
                          act(vsn_bf[0:m, :], T, AF.Copy, tk, vsn_key)
                          dma("sp", "osv", sv_d[l, 8 * samp_idx:8 * samp_idx + 8, :], T, tk, (("osv", l, samp_idx),))
                      else:
                          dve(lambda h: h.tensor_tensor(out=vsn_bf[0:m, :], in0=T, in1=lnB[0:m, :], op=ALU.add), tk + (("lnB",),), vsn_key)

                  def sgu_stage2(colstart, m, vsn_bf, vsn_key, is_sample, samp_idx=None):
                      for half in range(2):
                          bank = balloc()
                          items = []
                          for j in range(4):
                              g = half * 4 + j
                              o = psb[bank][:, j * 128:j * 128 + m]
                              items.append((o, vsn_bf[:, g * 128:(g + 1) * 128], wsT[:, g, 0:m], True, False))
                              items.append((o, onesb, brow[:, g * 128:g * 128 + m], False, True))
                          pe(items, vsn_key + (("wsT",), ("brow",)), (("ps", bank),))
                          dve(lambda h, half=half, bank=bank: h.tensor_tensor(
                              out=mT[:, half * 4:half * 4 + 4, colstart:colstart + m],
                              in0=psb[bank][:, :].rearrange("p (g t) -> p g t", g=4)[:, :, 0:m],
                              in1=uT[:, half * 4:half * 4 + 4, colstart:colstart + m], op=ALU.mult),
                              (("ps", bank),) + ukeys[half * 4:half * 4 + 4], mkeys[half * 4:half * 4 + 4])
                          bfree(bank)

                  r1hi = tuple(("R1", 8 + j) for j in range(8))
                  vsn_allk = tuple(("vsn", j) for j in range(6))
                  dve(lambda h: h.memset(dummy[:, 0:1], 0.0), (), r1hi + vsn_allk)
                  sgu_list = []
                  for b in blocks:
                      sl = vslot(b)
                      sgu_list.append((bcol(b), 128, vsn_all[:, sl * 1024:(sl + 1) * 1024], vsnk(sl), False, None, len(sgu_list) % 2))
                  if tile == "A":
                      for i in range(4):
                          sgu_list.append((640 + 8 * i, 8, a_tok[:, :], atok_all, True, i, 0))
                  prev_s = None
                  for it in sgu_list:
                      if it[4]:
                          if prev_s is not None:
                              sgu_stage2(*prev_s[0:6]); prev_s = None
                          sgu_stage1(*it)
                          sgu_stage2(*it[0:6])
                          continue
                      sgu_stage1(*it)
                      if prev_s is not None:
                          sgu_stage2(*prev_s[0:6])
                      prev_s = it
                  if prev_s is not None:
                      sgu_stage2(*prev_s[0:6])

                  stop(8)
                  merged = R1
                  dve(lambda h: h.memset(dummy[:, 1:2], 0.0), (), r1hi + vsn_allk)
                  mg_keys = lambda ci: (("R1", ci),)
                  for pas in range(2):
                      Wup = wau_d[l] if pas == 0 else wsu_d[l]
                      srcT = aT if pas == 0 else mT
                      skeys_ = akeys if pas == 0 else mkeys
                      gcol = OGA if pas == 0 else OGM
                      for g in range(8):
                          upw, upk = wload(Wup[:, g * 256:(g + 1) * 256].rearrange("(k p) n -> p k n", p=128), 8, 256)
                          gw, gk = wload(win_d[l][:, gcol + g * 256: gcol + (g + 1) * 256].rearrange("(k p) n -> p k n", p=128), NCH, 256)
                          for (c0, n, tag) in full_segs:
                              for cl in range(2):
                                  ci = g * 2 + cl
                                  bp = balloc(); bg = balloc()
                                  items = [(psb[bp][:, 0:n], upw[:, k, cl * 128:(cl + 1) * 128], srcT[:, k, c0:c0 + n], k == 0, k == 7)
                                           for k in range(8)]
                                  pe(items, (upk,) + skeys_, (("ps", bp),))
                                  items = [(psb[bg][:, 0:n], gw[:, k, cl * 128:(cl + 1) * 128], xn[:, k, c0:c0 + n], k == 0, k == NCH - 1)
                                           for k in range(NCH)]
                                  pe(items, (gk,) + xn_keys, (("ps", bg),))
                                  sg = SC[:, 0:n]; tt = SC[:, 512:512 + n]
                                  act(sg, psb[bg][:, 0:n], AF.Sigmoid, (("ps", bg),), (("S", 0),))
                                  if pas == 0:
                                      dve(lambda h, bp=bp, ci=ci, c0=c0, n=n, sg=sg: h.tensor_tensor(
                                          out=merged[:, ci, c0:c0 + n], in0=psb[bp][:, 0:n], in1=sg, op=ALU.mult),
                                          (("ps", bp), ("S", 0)), mg_keys(ci))
                                  else:
                                      dve(lambda h, bp=bp, tt=tt, n=n, sg=sg: h.tensor_tensor(out=tt, in0=psb[bp][:, 0:n], in1=sg, op=ALU.mult),
                                          (("ps", bp), ("S", 0)), (("S", 1),))
                                      dve(lambda h, ci=ci, c0=c0, n=n, tt=tt: h.tensor_tensor(
                                          out=merged[:, ci, c0:c0 + n], in0=merged[:, ci, c0:c0 + n], in1=tt, op=ALU.add),
                                          (("S", 1),) + mg_keys(ci), mg_keys(ci))
                                  bfree(bp); bfree(bg)

                  stop(9)
                  mgall = tuple(("R1", c) for c in range(NCH))

                  def out_ep(ci, seg, bank):
                      c0, n, tag = seg
                      dve(lambda h: h.tensor_tensor(out=x[:, ci, c0:c0 + n], in0=psb[bank][:, 0:n], in1=x[:, ci, c0:c0 + n], op=ALU.add),
                          (("ps", bank), xk(ci)), (xk(ci),))
                      if ssq["banks"] is not None:
                          sb_ = ssq["banks"][c0]
                          sq = bscr[:, (ci % 2) * 512:(ci % 2) * 512 + n]
                          act(sq, x[:, ci, c0:c0 + n], AF.Square, (xk(ci),), (("bs", ci % 2),))
                          return lambda: pe([(psb[sb_][:, 0:n], onesb, sq, ci == 0, ci == NCH - 1)], (("bs", ci % 2),), (("ps", sb_),))
                  ssq_open(full_segs)
                  proj([(wout_d[l], 0, NCH, 0, lambda k, c0, n: merged[:, k, c0:c0 + n], mgall)], D, full_segs, out_ep)
                  ss5 = ssq_take()

                  stop(10)
                  for (a, n) in segs(cf0, W):
                      rmsnorm(False, a, n, g2, a, ssbank=ss5[a])

                  fT = R1
                  fkeys = tuple(("R1", c) for c in range(NCH))
                  tgl = [0]

                  def f_ep(ci, seg, bank):
                      c0, n, tag = seg
                      q = tgl[0] % 2; tgl[0] += 1
                      r = SC[:, 1024 + q * 512: 1024 + q * 512 + n]
                      act(r, psb[bank][:, 0:n], AF.Relu, (("ps", bank),), (("S", 2 + q),))
                      dve(lambda h: h.tensor_tensor(out=fT[:, ci, c0:c0 + n], in0=r, in1=r, op=ALU.mult),
                          (("S", 2 + q),), (("R1", ci),))
                  for j in range(4):
                      proj([(w1_d[l], 0, NCH, j * 2048, xn_src, xn_keys)], 2048, full_segs, f_ep)
                      if j == 3 and l + 1 < NL:
                          ssq_open(full_segs)
                      proj([(w2_d[l], j * 2048, NCH, 0, lambda k, c0, n: fT[:, k, c0:c0 + n], fkeys)], D, full_segs, out_ep)
                  if l + 1 < NL:
                      carry = ssq_take()

              yv = (yA_d if tile == "A" else yB_d).rearrange("(k p) n -> p k n", p=128)
              dma("sp", "oy", yv[:, :, :], x[:, :, 0:W], tuple(xk(k) for k in range(NCH)), (("oy", tile),))

        except _Stop:
            pass

        allk = [k for k in S.last_w if isinstance(k, tuple) and isinstance(k[0], str) and k[0].startswith("o")]
        S.add("sp", None, reads=tuple(allk), writes=())

        S.finalize(eng_sems, chain_sems)
        with nc.Block() as block:
            @block.tensor
            def _(h):
                S.emit_engine("pe", h)

            @block.scalar
            def _(h):
                S.emit_engine("act", h)

            @block.vector
            def _(h):
                S.emit_engine("dve", h)

            @block.gpsimd
            def _(h):
                S.emit_engine("pool", h)

            @block.sync
            def _(h):
                S.emit_engine("sp", h)
    return nc


_CACHE = {}


def _consts():
    bf = ml_dtypes.bfloat16
    ident = np.eye(128, dtype=np.float32)
    ones = np.ones((128, 128), np.float32)
    p = np.arange(128)
    Bd = (p[:, None] // 64 == p[None, :] // 64).astype(np.float32)
    Rm = np.zeros((128, 128), np.float32)
    for m in range(128):
        if m % 64 < 32:
            Rm[m + 32, m] = -1.0
        else:
            Rm[m - 32, m] = 1.0
    k = np.arange(128)[:, None]; q = np.arange(128)[None, :]
    prev = np.where(k > q, 0.0, NEG).astype(np.float32)
    cur = np.where(k <= q, 0.0, NEG).astype(np.float32)
    maskN = np.concatenate([prev, cur, prev, cur], axis=1)
    allneg = np.full((128, 128), NEG, np.float32)
    maskF0 = np.concatenate([allneg, cur, allneg, cur], axis=1)
    q8 = np.arange(8)[None, :]
    mc = np.where(k > q8, 0.0, NEG).astype(np.float32)
    maskc = np.tile(mc, (1, 16))
    k8 = np.arange(8)[:, None]
    mn = np.where(k8 <= q8, 0.0, NEG).astype(np.float32)
    maskn = np.full((128, 4 * 128), NEG, np.float32)
    for i in range(4):
        maskn[8 * i:8 * i + 8, 128 * i:128 * (i + 1)] = np.tile(mn, (1, 16))
    tril = np.tril(np.ones((128, 128), np.float32))
    cf = np.concatenate([ident, tril], axis=1).astype(np.float32)
    return ident, ones, Bd, Rm, maskN, maskF0, maskc, maskn, cf, bf


def prep_inputs(x_prompt, x_sample, cache_k, cache_v, norm1_g, w_in, q_norm_g, k_norm_g,
                attn_sinks, sgu_ln_g, sgu_ln_b, sgu_w, sgu_b, w_attn_up, w_sgu_up, w_out,
                norm2_g, w_ff1, w_ff2):
    f32 = np.float32
    x_prompt = np.asarray(x_prompt, f32); x_sample = np.asarray(x_sample, f32)
    cache_k = np.asarray(cache_k, f32); cache_v = np.asarray(cache_v, f32)
    w_in = np.asarray(w_in, f32)
    ident, ones, Bd, Rm, maskN, maskF0, maskc, maskn, cf, bf = _consts()

    w_in_p = np.zeros((DEPTH, D, IN_COLS_P), f32)
    w_in_p[:, :, 0:1024] = w_in[:, :, 0:1024]
    for j in range(4):
        kj = w_in[:, :, 1024 + 64 * j: 1024 + 64 * (j + 1)]
        base = OK_ + 256 * j
        w_in_p[:, :, base: base + 64] = kj
        w_in_p[:, :, base + 192: base + 256] = kj
    w_in_p[:, :, OV:OV + 256] = w_in[:, :, 1280:1536]
    w_in_p[:, :, OU:OU + 1024] = w_in[:, :, 1536:2560]
    w_in_p[:, :, OVS:OVS + 1024] = w_in[:, :, 2560:3584]
    w_in_p[:, :, OGA:OGA + 2048] = w_in[:, :, 3584:5632]
    w_in_p[:, :, OGM:OGM + 2048] = w_in[:, :, 5632:7680]

    vecs = np.zeros((128, NV_L * DEPTH), f32)
    for l in range(DEPTH):
        vecs[:, l * NV_L: l * NV_L + 16] = np.asarray(norm1_g[l], f32).reshape(16, 128).T
        vecs[:, l * NV_L + 16: l * NV_L + 32] = np.asarray(norm2_g[l], f32).reshape(16, 128).T
        vecs[:, l * NV_L + 32] = np.tile(np.asarray(q_norm_g[l], f32), 2)
        vecs[:, l * NV_L + 33] = np.tile(np.asarray(k_norm_g[l], f32), 2)
    lnG = np.ascontiguousarray(np.broadcast_to(np.asarray(sgu_ln_g, f32)[:, None, :], (DEPTH, 128, 1024)))
    lnB = np.ascontiguousarray(np.broadcast_to(np.asarray(sgu_ln_b, f32)[:, None, :], (DEPTH, 128, 1024)))
    sinks = np.ascontiguousarray(np.broadcast_to(np.asarray(attn_sinks, f32)[:, None, :], (DEPTH, 128, 16)))
    sgub = np.asarray(sgu_b, f32).reshape(DEPTH, 1, 1024)

    inv = np.power(np.float32(10000.0), -np.arange(32, dtype=np.float32) / np.float32(32))
    invp = np.tile(inv, 4).astype(f32)

    def tables(pos):
        ang = invp[:, None] * pos.astype(f32)[None, :]
        return np.cos(ang).astype(f32), np.sin(ang).astype(f32)

    shared = dict(w_in=w_in_p, w_au=np.asarray(w_attn_up, f32), w_su=np.asarray(w_sgu_up, f32),
                  w_out=np.asarray(w_out, f32), w_ff1=np.asarray(w_ff1, f32), w_ff2=np.asarray(w_ff2, f32),
                  vecs=vecs, lnG=lnG, lnB=lnB, sinks=sinks, sgu_w=np.asarray(sgu_w, f32), sgu_b=sgub, cf32=cf)
    in_maps = []
    for c in range(8):
        seq, part = c // 4, c % 4
        start = part * 1024 - 512
        xs = np.zeros((12 * 128, D), f32)
        lo = max(start, 0)
        xs[lo - start:] = x_prompt[seq, lo:start + 1536]
        pos = start + np.arange(1536)
        samp = x_sample[4 * c:4 * c + 4].reshape(32, D)
        pos_s = 16384 + np.tile(np.arange(8), 4)
        xa = np.concatenate([xs[128:768], samp, xs[0:128]], axis=0)
        pa = np.concatenate([pos[128:768], pos_s, pos[0:128]])
        xb = xs[768:1536]; pb = pos[768:1536]
        cA, sA = tables(pa); cB, sB = tables(pb)
        maskF = maskF0 if part == 0 else maskN
        cbf = np.concatenate([ident, ones, Bd, Rm, maskN, maskF, maskc, maskn], axis=1).astype(bf)
        m = dict(shared)
        m.update(xA=np.ascontiguousarray(xa.T), xB=np.ascontiguousarray(xb.T), cosA=cA, sinA=sA, cosB=cB, sinB=sB,
                 cache_k=np.ascontiguousarray(cache_k[:, 4 * c:4 * c + 4].reshape(DEPTH, 4, 128, 256)),
                 cache_v=np.ascontiguousarray(cache_v[:, 4 * c:4 * c + 4].reshape(DEPTH, 4, 128, 256)),
                 cbf=cbf)
        in_maps.append(m)
    return in_maps


def kernel(**inputs):
    in_maps = prep_inputs(**inputs)
    if "nc" not in _CACHE:
        _CACHE["nc"] = build_program()
    nc = _CACHE["nc"]
    res = run_bass_kernel_spmd(nc, in_maps, core_ids=list(range(8)))
    return assemble(res.results)


def assemble(R):
    f32 = np.float32
    y_prompt = np.zeros((2, 4096, D), f32)
    y_sample = np.zeros((32, 8, D), f32)
    nkp = np.zeros((DEPTH, 2, 128, 4, 64), f32); nvp = np.zeros_like(nkp)
    nks = np.zeros((DEPTH, 32, 128, 4, 64), f32); nvs = np.zeros_like(nks)
    nsv = np.zeros((DEPTH, 32, 8, 8, 128), f32)
    for c in range(8):
        seq, part = c // 4, c % 4
        yA = np.asarray(R[c]["yA"]).T
        yB = np.asarray(R[c]["yB"]).T
        base = part * 1024
        y_prompt[seq, base:base + 256] = yA[384:640]
        y_prompt[seq, base + 256:base + 1024] = yB
        y_sample[4 * c:4 * c + 4] = yA[640:672].reshape(4, 8, D)
        if part == 3:
            nkp[:, seq] = np.asarray(R[c]["kp"]).reshape(DEPTH, 128, 4, 64)
            nvp[:, seq] = np.asarray(R[c]["vp"]).reshape(DEPTH, 128, 4, 64)
        nks[:, 4 * c:4 * c + 4] = np.asarray(R[c]["ks"]).reshape(DEPTH, 4, 128, 4, 64)
        nvs[:, 4 * c:4 * c + 4] = np.asarray(R[c]["vs"]).reshape(DEPTH, 4, 128, 4, 64)
        nsv[:, 4 * c:4 * c + 4] = np.asarray(R[c]["sv"]).reshape(DEPTH, 4, 8, 8, 128)
    return (y_prompt, y_sample, nkp, nvp, nks, nvs, nsv)
```

```python
import math
from contextlib import ExitStack

import numpy as np
import ml_dtypes

import concourse.bass as bass
import concourse.mybir as mybir
from concourse.bass_utils import run_bass_kernel_spmd

F32 = mybir.dt.float32
BF16 = mybir.dt.bfloat16
AF = mybir.ActivationFunctionType
ALU = mybir.AluOpType
AX = mybir.AxisListType

D = 2048
NCH = 16
DEPTH = 4
WA = 672
WB = 768
XW = 800
IN_COLS_P = 8448
OQ, OK_, OV, OU, OVS, OGA, OGM = 0, 1024, 2048, 2304, 3328, 4352, 6400
NV_L = 34
NEG = -30000.0
EPL = 9
ATL = 9


class Op:
    __slots__ = ("eng", "emit", "deps", "needs_inc", "ticket", "chain", "semh")


class Sched:
    ENGS = ("pe", "act", "dve", "pool", "sp")

    def __init__(self):
        self.ops = {e: [] for e in self.ENGS}
        self.last_w = {}
        self.readers = {}
        self.chains = {}

    def add(self, eng, emit, reads=(), writes=(), chain=None):
        op = Op()
        op.eng = eng; op.emit = emit; op.deps = set(); op.needs_inc = False
        op.chain = chain; op.ticket = 0; op.semh = None
        lw = self.last_w; rd = self.readers
        for k in reads:
            w = lw.get(k)
            if w is not None:
                op.deps.add(w)
        for k in writes:
            w = lw.get(k)
            if w is not None:
                op.deps.add(w)
            r = rd.get(k)
            if r:
                op.deps.update(r.values())
        rk = ("c", chain, id(op)) if chain is not None else eng
        for k in reads:
            r = rd.get(k)
            if r is None:
                r = rd[k] = {}
            r[rk] = op
        for k in writes:
            lw[k] = op
            rd[k] = {}
        if chain is not None:
            ch = self.chains.setdefault(chain, [])
            if ch:
                op.deps.add(ch[-1])
            ch.append(op)
            op.needs_inc = True
        op.deps.discard(op)
        if eng == "pe":
            op.deps = {d for d in op.deps if not (d.eng == "pe" and d.chain is None)}
        for d in op.deps:
            d.needs_inc = True
        self.ops[eng].append(op)
        return op

    def finalize(self, eng_sems, chain_sems):
        for e in self.ENGS:
            n = 0
            for op in self.ops[e]:
                if op.chain is not None:
                    continue
                op.semh = eng_sems[e]
                if op.needs_inc:
                    n += 1
                    op.ticket = n
        for cname, ch in self.chains.items():
            for i, op in enumerate(ch):
                op.semh = chain_sems[cname]
                op.ticket = 16 * (i + 1)

    def emit_engine(self, e, h):
        waited = {}
        for op in self.ops[e]:
            need = {}
            for d in op.deps:
                key = id(d.semh)
                if need.get(key, (None, 0))[1] < d.ticket:
                    need[key] = (d.semh, d.ticket)
            for key, (s, v) in need.items():
                if waited.get(key, 0) < v:
                    h.wait_ge(s, v)
                    waited[key] = v
            if op.emit is not None:
                ins = op.emit(h)
                if op.needs_inc:
                    ins.then_inc(op.semh, 16 if op.chain is not None else 1)


def segs(c0, c1):
    n = c1 - c0
    ns = -(-n // 512)
    w = n // ns
    out = []
    for i in range(ns):
        a = c0 + i * w
        out.append((a, (c1 - a) if i == ns - 1 else w))
    return out


class _Stop(Exception):
    pass


def build_program(NL=DEPTH, TILES=("A", "B"), STOP=None):
    nc = bass.Bass("TRN2", target_bir_lowering=False)
    S = Sched()

    def din(name, shape, dt=F32):
        return nc.dram_tensor(name, list(shape), dt, kind="ExternalInput").ap()

    def dout(name, shape, dt=F32):
        return nc.dram_tensor(name, list(shape), dt, kind="ExternalOutput").ap()

    xA_d = din("xA", [D, XW]); xB_d = din("xB", [D, WB])
    cosA_d = din("cosA", [128, XW]); sinA_d = din("sinA", [128, XW])
    cosB_d = din("cosB", [128, WB]); sinB_d = din("sinB", [128, WB])
    win_d = din("w_in", [DEPTH, D, IN_COLS_P])
    wau_d = din("w_au", [DEPTH, 1024, D]); wsu_d = din("w_su", [DEPTH, 1024, D])
    wout_d = din("w_out", [DEPTH, D, D])
    w1_d = din("w_ff1", [DEPTH, D, 8192]); w2_d = din("w_ff2", [DEPTH, 8192, D])
    vecs_d = din("vecs", [128, NV_L * DEPTH])
    lnG_d = din("lnG", [DEPTH, 128, 1024]); lnB_d = din("lnB", [DEPTH, 128, 1024])
    sinks_d = din("sinks", [DEPTH, 128, 16])
    sguw_d = din("sgu_w", [DEPTH, 8, 128, 128]); sgub_d = din("sgu_b", [DEPTH, 1, 1024])
    ck_d = din("cache_k", [DEPTH, 4, 128, 256]); cv_d = din("cache_v", [DEPTH, 4, 128, 256])
    cb_d = din("cbf", [128, 4 * 128 + 2 * 512 + 128 + 512], BF16)
    cf_d = din("cf32", [128, 256])

    yA_d = dout("yA", [D, WA]); yB_d = dout("yB", [D, WB])
    kp_d = dout("kp", [DEPTH, 128, 256]); vp_d = dout("vp", [DEPTH, 128, 256])
    ks_d = dout("ks", [DEPTH, 4, 128, 256]); vs_d = dout("vs", [DEPTH, 4, 128, 256])
    sv_d = dout("sv", [DEPTH, 32, 1024])
    scr_d = nc.dram_tensor("scr", [DEPTH, 128, 1024 + 260], BF16, kind="Internal").ap()

    es = ExitStack()
    with es:
        def sb(name, shape, dt):
            return es.enter_context(nc.sbuf_tensor("sb_" + name, list(shape), dt))

        x = sb("x", [128, NCH, WB], F32)
        xn = sb("xn", [128, NCH, XW], BF16)
        R1 = sb("R1", [128, NCH, WB], BF16)
        R2 = sb("R2", [128, NCH, WB], BF16)
        wsl = [sb(f"wsl{i}", [128, 4096], BF16) for i in range(4)]
        cosT = sb("cosT", [128, XW], F32); sinT = sb("sinT", [128, XW], F32)
        vaug = sb("vaug", [128, 8, 4, 65], BF16)
        kTprev = sb("kTprev", [128, 8, 128], BF16)
        SC = sb("SC", [128, 2048], F32)
        arena = sb("arena", [128, 4096], BF16)
        bscr = arena[:, 0:1024]
        lnG = sb("lnG", [128, 1024], F32); lnB = sb("lnB", [128, 1024], F32)
        a_tok = arena[:, 2048:3072]
        pT = [arena[:, 3072:3584], arena[:, 3584:4096]]
        pTn = sb("pTn", [128, 256], BF16)
        wsT = sb("wsT", [128, 8, 128], BF16)
        wsT_bd = sb("wsT_bd", [128, 8, 32], BF16)
        brow_s = sb("brow_s", [128, 8, 32], BF16)
        brow = sb("brow", [128, 1024], BF16)
        kc32 = [sb("kc32_0", [128, 256], F32)] * 2
        vc32 = [sb("vc32_0", [128, 256], F32)] * 2
        kcd = sb("kcd", [128, 4, 2, 128], BF16)
        vcaug = sb("vcaug", [128, 4, 65], BF16)
        cb = sb("cb", [128, 4 * 128 + 2 * 512 + 128 + 512], BF16)
        cf = sb("cf", [128, 256], F32)
        vecs = sb("vecs", [128, NV_L * DEPTH], F32)
        sk32 = sb("sk32", [128, 16], F32); esk = sb("esk", [128, 16], F32)
        den = sb("den", [128, 16], F32); rden = sb("rden", [128, 16], F32)
        st = sb("st", [128, 8], F32)
        dummy = sb("dummy", [128, 2], F32)
        kst = sb("kst", [128, 256], F32); vst = sb("vst", [128, 256], F32)
        kfin = arena[:, 1024:2048].bitcast(F32).rearrange("p (a b) -> p a b", a=4)
        gv2 = arena[:, 0:2048].bitcast(F32); lnt2 = arena[:, 2048:4096].bitcast(F32)
        kcT = a_tok[:, :].rearrange("p (c t) -> p c t", c=8)
        psb = [es.enter_context(nc.psum_tensor(f"ps{i}", [128, 512], F32)) for i in range(8)]
        _CACHE["sbuf_left"] = nc.sbuf_bytes_remaining

        identb = cb[:, 0:128]; onesb = cb[:, 128:256]; Bd = cb[:, 256:384]; Rm = cb[:, 384:512]
        maskN = cb[:, 512:1024]; maskF = cb[:, 1024:1536]; maskc = cb[:, 1536:1664]
        maskn = lambda i: cb[:, 1664 + 128 * i: 1792 + 128 * i]
        identf = cf[:, 0:128]; tril = cf[:, 128:256]
        ones_row = cb[0:1, 128:256]

        eng_sems = {e: es.enter_context(nc.semaphore("sem_" + e)) for e in Sched.ENGS}
        chain_names = ["w0", "w1", "w2", "w3", "const", "xload", "par", "par2", "bd", "cache0", "cache1", "scrw", "scrr",
                       "oy", "okp", "ovp", "oks", "ovs", "osv", "occ"]
        chain_sems = {c: es.enter_context(nc.semaphore("ch_" + c)) for c in chain_names}

        free_banks = list(range(8))

        def balloc():
            assert free_banks, "out of PSUM banks"
            return free_banks.pop(0)

        def bfree(b):
            free_banks.append(b)

        def dma(queue, chain, out, in_, reads, writes):
            def emit(h, out=out, in_=in_):
                return h.dma_start(out=out, in_=in_)
            S.add(queue, emit, reads=reads, writes=writes, chain=chain)

        def act(out, in_, func, reads, writes, scale=None, bias=None):
            def emit(h):
                kw = {}
                if scale is not None:
                    kw["scale"] = scale
                if bias is not None:
                    kw["bias"] = bias
                return h.activation(out=out, in_=in_, func=func, **kw)
            S.add("act", emit, reads=reads, writes=writes)

        def dve(fn, reads, writes):
            S.add("dve", fn, reads=reads, writes=writes)

        def pe(items, reads, writes):
            def emit(h, items=items):
                ins = None
                for it in items:
                    if it[0] == "T":
                        ins = h.transpose(out=it[1], in_=it[2], identity=it[3])
                    else:
                        ins = h.matmul(it[0], it[1], it[2], start=it[3], stop=it[4])
                return ins
            S.add("pe", emit, reads=reads, writes=writes)

        def rstd_ops(out, in_ps, scale, eps, reads, writes):
            act(out, in_ps, AF.Ln, reads, writes, scale=scale, bias=eps)
            act(out, out, AF.Exp, writes, writes, scale=-0.5)

        wstate = {"n": 0}
        ssq = {"banks": None}

        def ssq_open(seglist):
            ssq["banks"] = {c0: balloc() for (c0, n, tag) in seglist}

        def ssq_take():
            b = ssq["banks"]; ssq["banks"] = None
            return b

        def wload(view, nk, ncols):
            s = wstate["n"] % 4
            wstate["n"] += 1
            dst = wsl[s][:, 0:nk * ncols].rearrange("p (k n) -> p k n", k=nk)
            dma("pool", f"w{s}", dst, view, reads=(), writes=(("w", s),))
            return dst, ("w", s)

        def proj(parts, ncols, seglist, epilogue, cg=512):
            ngroups = -(-ncols // cg)
            pending = [None]
            for g in range(ngroups):
                gc = min(cg, ncols - g * cg)
                loaded = []
                for (W, row0, nk, col0, src_fn, src_keys) in parts:
                    k0 = 0
                    while k0 < nk:
                        kk = min(4096 // gc, nk - k0)
                        view = W[row0 + k0 * 128: row0 + (k0 + kk) * 128, col0 + g * cg: col0 + g * cg + gc] \
                            .rearrange("(k p) n -> p k n", p=128)
                        dst, key = wload(view, kk, gc)
                        loaded.append((dst, key, k0, kk, src_fn, src_keys))
                        k0 += kk
                for seg in seglist:
                    c0, n, tag = seg
                    for cl in range(gc // 128):
                        ci = g * (cg // 128) + cl
                        bank = balloc()
                        items = []
                        rkeys = []
                        tot = sum(l[3] for l in loaded)
                        i = 0
                        for (dst, key, k0, kk, src_fn, src_keys) in loaded:
                            rkeys.append(key)
                            rkeys.extend(src_keys)
                            for k in range(kk):
                                items.append((psb[bank][:, 0:n], dst[:, k, cl * 128:(cl + 1) * 128],
                                              src_fn(k0 + k, c0, n), i == 0, i == tot - 1))
                                i += 1
                        pe(items, reads=rkeys, writes=(("ps", bank),))
                        if pending[0] is not None:
                            pending[0](); pending[0] = None
                        r = epilogue(ci, seg, bank)
                        bfree(bank)
                        if callable(r):
                            pending[0] = r
            if pending[0] is not None:
                pending[0](); pending[0] = None

        dma("sp", "const", cb[:, :], cb_d[:, :], (), (("cb",),))
        dma("sp", "const", cf[:, :], cf_d[:, :], (), (("cf",),))
        dma("sp", "const", vecs[:, :], vecs_d[:, :], (), (("vecs",),))
        for e in ("pe", "act", "dve"):
            S.add(e, None, reads=(("cb",), ("cf",), ("vecs",)), writes=())
        dve(lambda h: h.memset(vaug[:, :, :, :].rearrange("p a b c -> p (a b c)"), 1.0), (), tuple(("vaug", i) for i in range(8)))
        dve(lambda h: h.memset(vcaug[:, :, :].rearrange("p a b -> p (a b)"), 1.0), (), (("vcaug",),))
        dve(lambda h: h.memset(brow[:, :], 0.0), (), (("brow",),))
        dve(lambda h: h.memset(wsT_bd[:, :, :].rearrange("p a b -> p (a b)"), 0.0), (), (("wsT_bd",),))
        dve(lambda h: h.memset(brow_s[:, :, :].rearrange("p a b -> p (a b)"), 0.0), (), (("brow_s",),))
        dve(lambda h: h.memset(pTn[:, :], 0.0), (), (("pTn",),))
        dve(lambda h: h.memset(kcd[:, :, :, :].rearrange("p a b c -> p (a b c)"), 0.0), (), (("kcd",),))
        dve(lambda h: h.memset(a_tok[:, :], 0.0), (), tuple(("a_tok", c) for c in range(8)))
        dve(lambda h: h.memset(kfin[:, :, :].rearrange("p a b -> p (a b)"), 0.0), (), (("kfin",),))
        for i_ in range(2):
            dve(lambda h, i_=i_: h.memset(pT[i_][:, :], 0.0), (), (("pT", i_),))
        dve(lambda h: h.memset(R1[:, :, :].rearrange("p a b -> p (a b)"), 0.0), (), tuple(("R1", c) for c in range(NCH)))
        dve(lambda h: h.memset(R2[:, :, :].rearrange("p a b -> p (a b)"), 0.0), (), tuple(("R2", c) for c in range(NCH)))
        dve(lambda h: h.memset(xn[:, :, :].rearrange("p a b -> p (a b)"), 0.0), (), tuple(("xn", c) for c in range(NCH)))

        def xk(k):
            return ("x", k)

        def stop(n):
            if STOP == n:
                raise _Stop()

        try:
          for tile in TILES:
              W = WA if tile == "A" else WB
              nblk_tile = 5 if tile == "A" else 6
              if tile == "A":
                  xv = xA_d.rearrange("(k p) n -> p k n", p=128)
                  dma("sp", "xload", x[:, :, 0:WA], xv[:, :, 0:WA], (), tuple(xk(k) for k in range(NCH)))
                  x0 = SC[:, :].rearrange("p (k n) -> p k n", k=NCH)
                  dma("sp", "xload", x0, xv[:, :, WA:XW], (), tuple(("S", i) for i in range(4)))
                  dma("sp", "xload", cosT[:, :], cosA_d[:, :], (), (("cos",),))
                  dma("sp", "xload", sinT[:, :], sinA_d[:, :], (), (("sin",),))
              else:
                  xv = xB_d.rearrange("(k p) n -> p k n", p=128)
                  dma("sp", "xload", x[:, :, 0:WB], xv[:, :, :], (), tuple(xk(k) for k in range(NCH)))
                  dma("sp", "xload", cosT[:, 0:WB], cosB_d[:, :], (), (("cos",),))
                  dma("sp", "xload", sinT[:, 0:WB], sinB_d[:, :], (), (("sin",),))

              carry = None
              for l in range(NL):
                  vb = l * NV_L
                  g1 = lambda k, vb=vb: vecs[:, vb + k: vb + k + 1]
                  g2 = lambda k, vb=vb: vecs[:, vb + 16 + k: vb + 17 + k]
                  qg = vecs[:, vb + 32: vb + 33]; kg = vecs[:, vb + 33: vb + 34]
                  if tile == "A":
                      cf0 = 128 * l
                      ck0 = 128 * (l - 1) if l >= 1 else 0
                      blocks = list(range(l + 1, 6))
                      bcol = lambda b: (b - 1) * 128
                      kvblk = l
                  else:
                      cf0 = 0; ck0 = 0
                      blocks = list(range(6, 12))
                      bcol = lambda b: (b - 6) * 128
                      kvblk = None
                  vslot = lambda b: (b - 1) if tile == "A" else (b - 6)
                  full_segs = [(a, n, "m") for (a, n) in segs(cf0, W)]
                  kv_segs = [(a, n, "m") for (a, n) in segs(ck0, W)]
                  if tile == "A" and l == 0:
                      kv_segs = kv_segs + [(WA, 128, "b0")]

                  dma("sp", "par", lnG[:, :], lnG_d[l], (), (("lnG",),))
                  dma("sp", "par", lnB[:, :], lnB_d[l], (), (("lnB",),))
                  dma("sp", "par", sk32[:, :], sinks_d[l], (), (("sk32",),))
                  dma("pool", "par2", brow[0:1, :], sgub_d[l], (), (("brow",),))
                  act(esk[:, :], sk32[:, :], AF.Exp, (("sk32",),), (("esk",),))

                  def rmsnorm(src_is_x0, c0, n, gfn, dstc0, ssbank=None):
                      bank = balloc() if ssbank is None else ssbank
                      for k in range(NCH if ssbank is None else 0):
                          sq = bscr[:, (k % 2) * 512:(k % 2) * 512 + n]
                          src = x0[:, k, 0:n] if src_is_x0 else x[:, k, c0:c0 + n]
                          skeys = (("S", 0), ("S", 1), ("S", 2), ("S", 3)) if src_is_x0 else (xk(k),)
                          if k % 2 == 0:
                              act(sq, src, AF.Square, skeys, (("bs", 0),))
                          else:
                              dve(lambda h, sq=sq, src=src: h.tensor_tensor(out=sq, in0=src, in1=src, op=ALU.mult), skeys, (("bs", 1),))
                          pe([(psb[bank][:, 0:n], onesb, sq, k == 0, k == NCH - 1)], (("bs", k % 2),), (("ps", bank),))
                      rs = SC[:, 0:n] if not src_is_x0 else None
                      if src_is_x0:
                          rs = kfin[:, 0, 0:n]
                          rkey = ("kfin",)
                      else:
                          rkey = ("S", 0) if n <= 512 else None
                      wk = (rkey,) if rkey is not None else (("S", 0), ("S", 1))
                      rstd_ops(rs, psb[bank][:, 0:n], 1.0 / D, 1e-6, (("ps", bank),), wk)
                      bfree(bank)
                      for k in range(NCH):
                          src = x0[:, k, 0:n] if src_is_x0 else x[:, k, c0:c0 + n]
                          skeys = (("S", 0), ("S", 1), ("S", 2), ("S", 3)) if src_is_x0 else (xk(k),)
                          dve(lambda h, src=src, k=k: h.scalar_tensor_tensor(
                              out=xn[:, k, dstc0:dstc0 + n], in0=src, scalar=gfn(k), in1=rs, op0=ALU.mult, op1=ALU.mult),
                              skeys + wk, (("xn", k),))

                  if tile == "A" and l == 0:
                      rmsnorm(True, 0, 128, g1, WA)
                  for (a, n) in segs(ck0, W):
                      rmsnorm(False, a, n, g1, a, ssbank=(carry[a] if carry is not None else None))
                  carry = None
                  xn_keys = tuple(("xn", k) for k in range(NCH))
                  xn_src = lambda k, c0, n: xn[:, k, c0:c0 + n]

                  stop(1)
                  ws32 = SC[:, 1024:2048].rearrange("p (g s) -> p g s", g=8)
                  dma("sp", "par", ws32, sguw_d[l].rearrange("g t s -> t g s"), (), (("S", 2), ("S", 3)))
                  for g in range(8):
                      dve(lambda h, g=g: h.tensor_tensor(out=ws32[:, g, :], in0=ws32[:, g, :], in1=tril, op=ALU.mult),
                          (("S", 2), ("S", 3)), (("S", 2), ("S", 3)))
                  for half in range(2):
                      bank = balloc()
                      items = [("T", psb[bank][:, j * 128:(j + 1) * 128], ws32[:, half * 4 + j, :], identf) for j in range(4)]
                      pe(items, (("S", 2), ("S", 3)), (("ps", bank),))
                      act(wsT[:, half * 4:half * 4 + 4, :], psb[bank][:, :].rearrange("p (g t) -> p g t", g=4), AF.Copy,
                          (("ps", bank),), (("wsT",),))
                      bfree(bank)

                  if tile == "A":
                      for i in range(4):
                          dma("sp", "bd", wsT_bd[8 * i:8 * i + 8, :, 8 * i:8 * i + 8], wsT[0:8, :, 0:8], (("wsT",),), (("wsT_bd",),))
                          dma("sp", "bd", brow_s[0:1, :, 8 * i:8 * i + 8],
                              brow[0:1, :].rearrange("o (g t) -> o g t", g=8)[:, :, 0:8], (("brow",),), (("brow_s",),))
                  stop(2)
                  qT = R1[:, 0:8, :]; kT = R1[:, 8:16, :]

                  def qk_epilogue(is_q):
                      gain = qg if is_q else kg

                      def ep(ci, seg, bank):
                          c0, n, tag = seg
                          z = psb[bank][:, 0:n]
                          sqz = bscr[:, 0:n]; y = bscr[:, 512:512 + n]
                          rs = SC[:, 0:n]; t1 = SC[:, 512:512 + n]; t2 = SC[:, 1024:1024 + n]
                          act(sqz, z, AF.Square, (("ps", bank),), (("bs", 0),))
                          act(y, z, AF.Copy, (("ps", bank),), (("bs", 1),), scale=gain)

                          def later():
                              b2 = balloc(); b3 = balloc()
                              pe([(psb[b2][:, 0:n], Bd, sqz, True, True)], (("bs", 0),), (("ps", b2),))
                              pe([(psb[b3][:, 0:n], Rm, y, True, True)], (("bs", 1),), (("ps", b3),))
                              rstd_ops(rs, psb[b2][:, 0:n], 1.0 / 64, 1e-6, (("ps", b2),), (("S", 0),))
                              dve(lambda h: h.tensor_tensor(out=t1, in0=y, in1=cosT[:, c0:c0 + n], op=ALU.mult),
                                  (("bs", 1), ("cos",)), (("S", 1),))
                              dve(lambda h: h.tensor_tensor(out=t2, in0=psb[b3][:, 0:n], in1=sinT[:, c0:c0 + n], op=ALU.mult),
                                  (("ps", b3), ("sin",)), (("S", 2),))
                              dve(lambda h: h.tensor_tensor(out=t1, in0=t1, in1=t2, op=ALU.add),
                                  (("S", 1), ("S", 2)), (("S", 1),))
                              if is_q:
                                  o = qT[:, ci, c0:c0 + n]; ok = ("R1", ci)
                              elif tag == "b0":
                                  o = kTprev[:, ci, 0:n]; ok = ("kTprev",)
                              else:
                                  o = kT[:, ci, c0:c0 + n]; ok = ("R1", 8 + ci)
                              dve(lambda h: h.tensor_tensor(out=o, in0=t1, in1=rs, op=ALU.mult),
                                  (("S", 1), ("S", 0)), (ok,))
                              if not is_q:
                                  if tile == "B" and c0 + n == WB and ci % 2 == 0:
                                      lo = WB - 128 - c0
                                      dve(lambda h: h.tensor_tensor(out=kfin[:, ci // 2, :], in0=t1[:, lo:lo + 128], in1=rs[:, lo:lo + 128], op=ALU.mult),
                                          (("S", 1), ("S", 0)), (("kfin",),))
                                  if tile == "A" and tag == "m" and c0 + n == WA and ci % 2 == 0:
                                      lo = WA - 32 - c0
                                      dve(lambda h: h.tensor_tensor(out=kfin[:, ci // 2, 0:32], in0=t1[:, lo:lo + 32], in1=rs[:, lo:lo + 32], op=ALU.mult),
                                          (("S", 1), ("S", 0)), (("kfin",),))
                              bfree(b2); bfree(b3)
                          return later
                      return ep

                  proj([(win_d[l], 0, NCH, OQ, xn_src, xn_keys)], 1024, full_segs, qk_epilogue(True))
                  proj([(win_d[l], 0, NCH, OK_, xn_src, xn_keys)], 1024, kv_segs, qk_epilogue(False))

                  stop(3)
                  vview = win_d[l][:, OV:OV + 256].rearrange("(k p) n -> p k n", p=128)
                  wv, wvkey = wload(vview, NCH, 256)

                  def vproj(colstart, m, dst_ap, dkey, f32_dst=None, f32key=None):
                      bank = balloc()
                      items = [(psb[bank][:, 0:256], xn[:, k, colstart:colstart + 128], wv[:, k, :], k == 0, k == NCH - 1)
                               for k in range(NCH)]
                      pe(items, (wvkey,) + xn_keys, (("ps", bank),))
                      act(dst_ap, psb[bank][:, 0:256].rearrange("p (h d) -> p h d", h=4), AF.Copy, (("ps", bank),), (dkey,))
                      if f32_dst is not None:
                          act(f32_dst, psb[bank][:, 0:256], AF.Copy, (("ps", bank),), (f32key,))
                      bfree(bank)

                  if tile == "A" and l == 0:
                      vproj(WA, 128, vaug[:, 7, :, 0:64], ("vaug", 7))
                  kvb_list = ([kvblk] if (kvblk is not None and l >= 1) else []) + blocks
                  for b in kvb_list:
                      last = (tile == "B" and b == 11)
                      vproj(bcol(b), 128, vaug[:, vslot(b), :, 0:64], ("vaug", vslot(b)),
                            vst[:, :] if last else None, ("vst",) if last else None)
                  if tile == "B":
                      dma("sp", "scrr", kTprev[:, :, :], scr_d[l][:, 0:1024].rearrange("p (c t) -> p c t", c=8),
                          (("scr", l),), (("kTprev",),))
                      dma("sp", "scrr", vaug[:, 7, :, :], scr_d[l][:, 1024:1284].rearrange("p (h d) -> p h d", h=4),
                          (("scr", l),), (("vaug", 7),))

                  stop(4)
                  aT = R2[:, 0:8, :]; mT = R2[:, 8:16, :]
                  qkeys = tuple(("R1", c) for c in range(8))
                  kkeys = tuple(("R1", 8 + c) for c in range(8))
                  akeys = tuple(("R2", c) for c in range(8))

                  def attn_norm(np_, obanks):
                      for bi, ob in enumerate(obanks):
                          h0 = bi * 7; nh = min(7, 16 - h0)
                          ov = psb[ob][0:np_, 0:nh * 65].rearrange("p (h e) -> p h e", h=nh)
                          dve(lambda h, ov=ov, h0=h0, nh=nh: h.tensor_tensor(out=den[0:np_, h0:h0 + nh], in0=ov[:, :, 64],
                                                                           in1=esk[0:np_, h0:h0 + nh], op=ALU.add),
                              (("ps", ob), ("esk",)), (("den", bi),))
                          dve(lambda h, h0=h0, nh=nh: h.reciprocal(out=rden[0:np_, h0:h0 + nh], in_=den[0:np_, h0:h0 + nh]),
                              (("den", bi),), (("rden", bi),))
                          dve(lambda h, ov=ov, h0=h0, nh=nh: h.tensor_tensor(
                              out=a_tok[0:np_, h0 * 64:(h0 + nh) * 64].rearrange("p (h d) -> p h d", h=nh), in0=ov[:, :, 0:64],
                              in1=rden[0:np_, h0:h0 + nh].unsqueeze(2).broadcast_to([np_, nh, 64]), op=ALU.mult),
                              (("ps", ob), ("rden", bi)), tuple(("a_tok", c) for c in range(h0 // 2, (h0 + nh - 1) // 2 + 1)))
                          bfree(ob)

                  def attn_tr(col0, ncols):
                      for half in range(2):
                          bank = balloc()
                          pv = psb[bank][:, 0:256].bitcast(BF16)
                          items = [("T", pv[:, j * 128:(j + 1) * 128], a_tok[:, (half * 4 + j) * 128:(half * 4 + j + 1) * 128],
                                    identb) for j in range(4)]
                          pe(items, tuple(("a_tok", half * 4 + j) for j in range(4)), (("ps", bank),))
                          act(aT[:, half * 4:half * 4 + 4, col0:col0 + ncols],
                              pv[:, 0:512].rearrange("p (c t) -> p c t", c=4)[:, :, 0:ncols], AF.Copy,
                              (("ps", bank),), akeys[half * 4:half * 4 + 4])
                          bfree(bank)

                  atok_all = tuple(("a_tok", c) for c in range(8))
                  pti = [0]
                  pend_tr = [None]
                  for b in blocks:
                      c_q = bcol(b)
                      first = (b == blocks[0]) and (tile == "B" or l == 0)
                      if first:
                          kprev = lambda kv, hf: kTprev[:, 2 * kv + hf, :]
                          kprev_key = ("kTprev",); vprev = 7
                      else:
                          cp = bcol(b - 1)
                          kprev = lambda kv, hf, cp=cp: kT[:, 2 * kv + hf, cp:cp + 128]
                          kprev_key = None; vprev = vslot(b - 1)
                      mask_ap = maskF if (b == 4) else maskN
                      obanks = [balloc(), balloc(), balloc()]

                      def pv_mm(pr, ps_i, obanks=obanks, vprev=vprev, b=b):
                          items = []
                          for hf in range(2):
                              hd = 2 * pr + hf; kv = hd // 4
                              ob = obanks[hd // 7]; off = (hd % 7) * 65
                              items.append((psb[ob][:, off:off + 65], pT[ps_i][:, (hf * 2) * 128:(hf * 2 + 1) * 128],
                                            vaug[:, vprev, kv, :], True, False))
                              items.append((psb[ob][:, off:off + 65], pT[ps_i][:, (hf * 2 + 1) * 128:(hf * 2 + 2) * 128],
                                            vaug[:, vslot(b), kv, :], False, True))
                          pe(items, (("pT", ps_i), ("vaug", vprev), ("vaug", vslot(b))), tuple(("ps", o) for o in obanks))

                      prev = None
                      for pr in range(8):
                          bank = balloc()
                          items = [(psb[bank][:, 0:512], identb, mask_ap, True, False)]
                          for hf in range(2):
                              hd = 2 * pr + hf; kv = hd // 4
                              q_ap = qT[:, pr, c_q:c_q + 128]
                              items.append((psb[bank][:, (hf * 2) * 128:(hf * 2 + 1) * 128], kprev(kv, hf), q_ap, False, False))
                              items.append((psb[bank][:, (hf * 2 + 1) * 128:(hf * 2 + 2) * 128],
                                            kT[:, 2 * kv + hf, c_q:c_q + 128], q_ap, False, hf == 1))
                          rk = qkeys + kkeys + ((kprev_key,) if kprev_key else ())
                          pe(items, rk, (("ps", bank),))
                          ps_i = pti[0] % 2; pti[0] += 1
                          act(pT[ps_i][:, :], psb[bank][:, 0:512], AF.Exp, (("ps", bank),), (("pT", ps_i),), scale=0.125)
                          bfree(bank)
                          if prev is not None:
                              pv_mm(*prev)
                          prev = (pr, ps_i)
                          if pr == 2 and pend_tr[0] is not None:
                              attn_tr(*pend_tr[0]); pend_tr[0] = None
                      pv_mm(*prev)
                      attn_norm(128, obanks)
                      pend_tr[0] = (c_q, 128)
                  if pend_tr[0] is not None:
                      attn_tr(*pend_tr[0]); pend_tr[0] = None

                  stop(5)
                  if tile == "B":
                      bank = balloc()
                      items = [("T", psb[bank][:, j * 128:(j + 1) * 128], kfin[:, j, :], identf) for j in range(4)]
                      pe(items, (("kfin",),), (("ps", bank),))
                      act(kst[:, :].rearrange("p (h d) -> p h d", h=4), psb[bank][:, :].rearrange("p (h e) -> p h e", h=4)[:, :, 0:64],
                          AF.Copy, (("ps", bank),), (("kst",),))
                      bfree(bank)
                      dma("sp", "okp", kp_d[l], kst[:, :], (("kst",),), (("okp", l),))
                      dma("sp", "ovp", vp_d[l], vst[:, :], (("vst",),), (("ovp", l),))
                  else:
                      c5 = bcol(5)
                      dma("sp", "scrw", scr_d[l][:, 0:1024].rearrange("p (c t) -> p c t", c=8), kT[:, :, c5:c5 + 128],
                          kkeys, (("scr", l),))
                      dma("sp", "scrw", scr_d[l][:, 1024:1284].rearrange("p (h d) -> p h d", h=4), vaug[:, vslot(5), :, :],
                          (("vaug", vslot(5)),), (("scr", l),))

                  stop(6)
                  if tile == "A":
                      vproj(640, 128, vaug[:, 6, :, 0:64], ("vaug", 6), vst[:, :], ("vst",))
                      for i in range(4):
                          cs = 640 + 8 * i
                          dma("sp", "cache0", kc32[0][:, :], ck_d[l, i], (), (("kc32", 0),))
                          dma("sp", "cache0", vc32[0][:, :], cv_d[l, i], (), (("vc32", 0),))
                          dma("sp", "occ", ks_d[l, i, 0:120, :], ck_d[l, i, 8:128, :], (), (("oks_c", l, i),))
                          dma("sp", "occ", vs_d[l, i, 0:120, :], cv_d[l, i, 8:128, :], (), (("ovs_c", l, i),))
                          kcv = kc32[0][:, :].rearrange("p (h d) -> p h d", h=4)
                          act(kcd[:, :, 0, 0:64], kcv, AF.Copy, (("kc32", 0),), (("kcd",),))
                          act(kcd[:, :, 1, 64:128], kcv, AF.Copy, (("kc32", 0),), (("kcd",),))
                          act(vcaug[:, :, 0:64], vc32[0][:, :].rearrange("p (h d) -> p h d", h=4), AF.Copy,
                              (("vc32", 0),), (("vcaug",),))
                          kcd8 = kcd[:, :, :, :].rearrange("p a b c -> p (a b) c")
                          for half in range(2):
                              bank = balloc()
                              pv = psb[bank][:, 0:256].bitcast(BF16)
                              items = [("T", pv[:, j * 128:(j + 1) * 128], kcd8[:, half * 4 + j, :], identb) for j in range(4)]
                              pe(items, (("kcd",),), (("ps", bank),))
                              act(kcT[:, half * 4:half * 4 + 4, :], pv[:, 0:512].rearrange("p (c t) -> p c t", c=4), AF.Copy,
                                  (("ps", bank),), atok_all[half * 4:half * 4 + 4])
                              bfree(bank)
                          bc = balloc(); bn = balloc()
                          items = [(psb[bc][:, 0:128], identb, maskc, True, False)]
                          for hd in range(16):
                              hf = hd % 2; kv = hd // 4; pr = hd // 2
                              items.append((psb[bc][:, hd * 8:(hd + 1) * 8], kcT[:, 2 * kv + hf, :],
                                            qT[:, pr, cs:cs + 8], False, hd == 15))
                          pe(items, atok_all + qkeys, (("ps", bc),))
                          items = [(psb[bn][:, 0:128], identb, maskn(i), True, False)]
                          for hd in range(16):
                              hf = hd % 2; kv = hd // 4; pr = hd // 2
                              items.append((psb[bn][:, hd * 8:(hd + 1) * 8], kT[:, 2 * kv + hf, 640:768],
                                            qT[:, pr, cs:cs + 8], False, hd == 15))
                          pe(items, kkeys + qkeys, (("ps", bn),))
                          act(pT[0][:, 0:128], psb[bc][:, 0:128], AF.Exp, (("ps", bc),), (("pT", 0),), scale=0.125)
                          act(pTn[0:32, 0:128], psb[bn][0:32, 0:128], AF.Exp, (("ps", bn),), (("pTn",),), scale=0.125)
                          bfree(bc); bfree(bn)
                          obanks = [balloc(), balloc(), balloc()]
                          items = []
                          for hd in range(16):
                              kv = hd // 4
                              ob = obanks[hd // 7]; off = (hd % 7) * 65
                              items.append((psb[ob][:, off:off + 65], pT[0][:, hd * 8:hd * 8 + 128], vcaug[:, kv, :], True, False))
                              items.append((psb[ob][:, off:off + 65], pTn[:, hd * 8:hd * 8 + 128], vaug[:, 6, kv, :], False, True))
                          pe(items, (("pT", 0), ("pTn",), ("vcaug",), ("vaug", 6)), tuple(("ps", o) for o in obanks))
                          attn_norm(8, obanks)
                          attn_tr(cs, 8)
                      bank = balloc()
                      items = [("T", psb[bank][:, j * 128:(j + 1) * 128], kfin[:, j, :], identf) for j in range(4)]
                      pe(items, (("kfin",),), (("ps", bank),))
                      act(kst[:, :].rearrange("p (h d) -> p h d", h=4), psb[bank][:, :].rearrange("p (h e) -> p h e", h=4)[:, :, 0:64],
                          AF.Copy, (("ps", bank),), (("kst",),))
                      bfree(bank)
                      for i in range(4):
                          dma("sp", "oks", ks_d[l, i, 120:128, :], kst[8 * i:8 * i + 8, :], (("kst",),), (("oks", l, i),))
                          dma("sp", "ovs", vs_d[l, i, 120:128, :], vst[8 * i:8 * i + 8, :], (("vst",),), (("ovs", l, i),))

                  stop(7)
                  uT = R1[:, 0:8, :]
                  ukeys = tuple(("R1", c) for c in range(8))

                  def u_ep(ci, seg, bank):
                      c0, n, tag = seg
                      act(uT[:, ci, c0:c0 + n], psb[bank][:, 0:n], AF.Gelu, (("ps", bank),), (("R1", ci),))
                  proj([(win_d[l], 0, NCH, OU, xn_src, xn_keys)], 1024, full_segs, u_ep)

                  vsw = []
                  for hv in range(2):
                      for kh in range(2):
                          view = win_d[l][kh * 1024:(kh + 1) * 1024, OVS + hv * 512: OVS + (hv + 1) * 512] \
                              .rearrange("(k p) n -> p k n", p=128)
                          vsw.append(wload(view, 8, 512))
                  vsn_all = R1[:, 8:16, :].rearrange("p k t -> p (k t)")
                  vsnk = lambda sl: (("vsn", sl),)
                  gv = SC[:, 0:1024]; lnt = SC[:, 1024:2048]
                  mkeys = tuple(("R2", 8 + c) for c in range(8))

                  def sgu_stage1(colstart, m, vsn_bf, vsn_key, is_sample, samp_idx=None, par=0):
                      for hv in range(2):
                          bank = balloc()
                          items = []
                          for kh in range(2):
                              wdst, wkey = vsw[hv * 2 + kh]
                              for k in range(8):
                                  kk = kh * 8 + k
                                  items.append((psb[bank][:, 0:512], xn[:, kk, colstart:colstart + 128], wdst[:, k, :],
                                                kk == 0, kk == NCH - 1))
                          pe(items, tuple(w[1] for w in vsw) + xn_keys, (("ps", bank),))
                          gvp = gv if par == 0 else gv2
                          gkeys = (("S", hv),) if par == 0 else (("bs", 0), ("bs", 1), ("kfin",))
                          act(gvp[0:m, hv * 512:(hv + 1) * 512], psb[bank][0:m, 0:512], AF.Gelu, (("ps", bank),), gkeys)
                          bfree(bank)
                      if par == 0:
                          G = gv[0:m, :]; T = lnt[0:m, :]
                          sk = (("S", 0), ("S", 1)); tk = (("S", 2), ("S", 3))
                      else:
                          G = gv2[0:m, :]; T = lnt2[0:m, :]
                          sk = (("bs", 0), ("bs", 1), ("kfin",)); tk = atok_all + (("pT", 0), ("pT", 1))
                      so = 4 * par
                      dve(lambda h: h.tensor_reduce(out=st[0:m, so:so + 1], in_=G, axis=AX.X, op=ALU.add), sk, (("st", so),))
                      dve(lambda h: h.tensor_scalar(out=st[0:m, so + 1:so + 2], in0=st[0:m, so:so + 1], scalar1=-1.0 / 1024, scalar2=None, op0=ALU.mult),
                          (("st", so),), (("st", so + 1),))
                      act(T, G, AF.Identity, sk + (("st", so + 1),), tk, bias=st[0:m, so + 1:so + 2])
                      dve(lambda h: h.tensor_tensor(out=G, in0=T, in1=T, op=ALU.mult), tk, sk)
                      dve(lambda h: h.tensor_reduce(out=st[0:m, so + 2:so + 3], in_=G, axis=AX.X, op=ALU.add), sk, (("st", so + 2),))
                      rstd_ops(st[0:m, so + 3:so + 4], st[0:m, so + 2:so + 3], 1.0 / 1024, 1e-5, (("st", so + 2),), (("st", so + 3),))
                      dve(lambda h: h.scalar_tensor_tensor(out=T, in0=T, scalar=st[0:m, so + 3:so + 4], in1=lnG[0:m, :],
                                                           op0=ALU.mult, op1=ALU.mult), tk + (("st", so + 3), ("lnG",)), tk)
                      if is_sample:
                          dve(lambda h: h.tensor_tensor(out=T, in0=T, in1=lnB[0:m, :], op=ALU.add), tk + (("lnB",),), tk)
                          dve(lambda h: h.memset(vsn_bf[:, :], 0.0), (), vsn_key)
                          act(vsn_bf[0:m, :], T, AF.Copy, tk, vsn_key)
                          dma("sp", "osv", sv_d[l], T, tk, (("osv", l, samp_idx),))
                      else:
                          dve(lambda h: h.tensor_tensor(out=vsn_bf[0:m, :], in0=T, in1=lnB[0:m, :], op=ALU.add), tk + (("lnB",),), vsn_key)

                  def sgu_stage2(colstart, m, vsn_bf, vsn_key, is_sample, samp_idx=None):
                      for half in range(2):
                          bank = balloc()
                          items = []
                          for j in range(4):
                              g = half * 4 + j
                              o = psb[bank][:, j * 128:j * 128 + m]
                              if is_sample:
                                  items.append((o, vsn_bf[:, g * 128:(g + 1) * 128], wsT_bd[:, g, 0:m], True, False))
                                  items.append((o, onesb, brow_s[:, g, 0:m], False, True))
                              else:
                                  items.append((o, vsn_bf[:, g * 128:(g + 1) * 128], wsT[:, g, 0:m], True, False))
                                  items.append((o, onesb, brow[:, g * 128:g * 128 + m], False, True))
                          pe(items, vsn_key + (("wsT",), ("brow",), ("wsT_bd",), ("brow_s",)), (("ps", bank),))
                          dve(lambda h, half=half, bank=bank: h.tensor_tensor(
                              out=mT[:, half * 4:half * 4 + 4, colstart:colstart + m],
                              in0=psb[bank][:, :].rearrange("p (g t) -> p g t", g=4)[:, :, 0:m],
                              in1=uT[:, half * 4:half * 4 + 4, colstart:colstart + m], op=ALU.mult),
                              (("ps", bank),) + ukeys[half * 4:half * 4 + 4], mkeys[half * 4:half * 4 + 4])
                          bfree(bank)

                  r1hi = tuple(("R1", 8 + j) for j in range(8))
                  vsn_allk = tuple(("vsn", j) for j in range(6))
                  dve(lambda h: h.memset(dummy[:, 0:1], 0.0), (), r1hi + vsn_allk)
                  sgu_list = []
                  for b in blocks:
                      sl = vslot(b)
                      sgu_list.append((bcol(b), 128, vsn_all[:, sl * 1024:(sl + 1) * 1024], vsnk(sl), False, None, len(sgu_list) % 2))
                  if tile == "A":
                      sgu_list.append((640, 32, a_tok[:, :], atok_all, True, 0, 0))
                  prev_s = None
                  for it in sgu_list:
                      if it[4]:
                          if prev_s is not None:
                              sgu_stage2(*prev_s[0:6]); prev_s = None
                          sgu_stage1(*it)
                          sgu_stage2(*it[0:6])
                          continue
                      sgu_stage1(*it)
                      if prev_s is not None:
                          sgu_stage2(*prev_s[0:6])
                      prev_s = it
                  if prev_s is not None:
                      sgu_stage2(*prev_s[0:6])

                  stop(8)
                  merged = R1
                  dve(lambda h: h.memset(dummy[:, 1:2], 0.0), (), r1hi + vsn_allk)
                  mg_keys = lambda ci: (("R1", ci),)
                  for pas in range(2):
                      Wup = wau_d[l] if pas == 0 else wsu_d[l]
                      srcT = aT if pas == 0 else mT
                      skeys_ = akeys if pas == 0 else mkeys
                      gcol = OGA if pas == 0 else OGM
                      for g in range(8):
                          upw, upk = wload(Wup[:, g * 256:(g + 1) * 256].rearrange("(k p) n -> p k n", p=128), 8, 256)
                          gw, gk = wload(win_d[l][:, gcol + g * 256: gcol + (g + 1) * 256].rearrange("(k p) n -> p k n", p=128), NCH, 256)
                          for (c0, n, tag) in full_segs:
                              for cl in range(2):
                                  ci = g * 2 + cl
                                  bp = balloc(); bg = balloc()
                                  items = [(psb[bp][:, 0:n], upw[:, k, cl * 128:(cl + 1) * 128], srcT[:, k, c0:c0 + n], k == 0, k == 7)
                                           for k in range(8)]
                                  pe(items, (upk,) + skeys_, (("ps", bp),))
                                  items = [(psb[bg][:, 0:n], gw[:, k, cl * 128:(cl + 1) * 128], xn[:, k, c0:c0 + n], k == 0, k == NCH - 1)
                                           for k in range(NCH)]
                                  pe(items, (gk,) + xn_keys, (("ps", bg),))
                                  sg = SC[:, 0:n]; tt = SC[:, 512:512 + n]
                                  act(sg, psb[bg][:, 0:n], AF.Sigmoid, (("ps", bg),), (("S", 0),))
                                  if pas == 0:
                                      dve(lambda h, bp=bp, ci=ci, c0=c0, n=n, sg=sg: h.tensor_tensor(
                                          out=merged[:, ci, c0:c0 + n], in0=psb[bp][:, 0:n], in1=sg, op=ALU.mult),
                                          (("ps", bp), ("S", 0)), mg_keys(ci))
                                  else:
                                      dve(lambda h, bp=bp, tt=tt, n=n, sg=sg: h.tensor_tensor(out=tt, in0=psb[bp][:, 0:n], in1=sg, op=ALU.mult),
                                          (("ps", bp), ("S", 0)), (("S", 1),))
                                      dve(lambda h, ci=ci, c0=c0, n=n, tt=tt: h.tensor_tensor(
                                          out=merged[:, ci, c0:c0 + n], in0=merged[:, ci, c0:c0 + n], in1=tt, op=ALU.add),
                                          (("S", 1),) + mg_keys(ci), mg_keys(ci))
                                  bfree(bp); bfree(bg)

                  stop(9)
                  mgall = tuple(("R1", c) for c in range(NCH))

                  def out_ep(ci, seg, bank):
                      c0, n, tag = seg
                      dve(lambda h: h.tensor_tensor(out=x[:, ci, c0:c0 + n], in0=psb[bank][:, 0:n], in1=x[:, ci, c0:c0 + n], op=ALU.add),
                          (("ps", bank), xk(ci)), (xk(ci),))
                      if ssq["banks"] is not None:
                          sb_ = ssq["banks"][c0]
                          sq = bscr[:, (ci % 2) * 512:(ci % 2) * 512 + n]
                          act(sq, x[:, ci, c0:c0 + n], AF.Square, (xk(ci),), (("bs", ci % 2),))
                          return lambda: pe([(psb[sb_][:, 0:n], onesb, sq, ci == 0, ci == NCH - 1)], (("bs", ci % 2),), (("ps", sb_),))
                  ssq_open(full_segs)
                  proj([(wout_d[l], 0, NCH, 0, lambda k, c0, n: merged[:, k, c0:c0 + n], mgall)], D, full_segs, out_ep)
                  ss5 = ssq_take()

                  stop(10)
                  for (a, n) in segs(cf0, W):
                      rmsnorm(False, a, n, g2, a, ssbank=ss5[a])

                  fT = R1
                  fkeys = tuple(("R1", c) for c in range(NCH))
                  tgl = [0]

                  def f_ep(ci, seg, bank):
                      c0, n, tag = seg
                      q = tgl[0] % 2; tgl[0] += 1
                      r = SC[:, 1024 + q * 512: 1024 + q * 512 + n]
                      act(r, psb[bank][:, 0:n], AF.Relu, (("ps", bank),), (("S", 2 + q),))
                      dve(lambda h: h.tensor_tensor(out=fT[:, ci, c0:c0 + n], in0=r, in1=r, op=ALU.mult),
                          (("S", 2 + q),), (("R1", ci),))
                  for j in range(4):
                      proj([(w1_d[l], 0, NCH, j * 2048, xn_src, xn_keys)], 2048, full_segs, f_ep)
                      if j == 3 and l + 1 < NL:
                          ssq_open(full_segs)
                      proj([(w2_d[l], j * 2048, NCH, 0, lambda k, c0, n: fT[:, k, c0:c0 + n], fkeys)], D, full_segs, out_ep)
                  if l + 1 < NL:
                      carry = ssq_take()

              yv = (yA_d if tile == "A" else yB_d).rearrange("(k p) n -> p k n", p=128)
              dma("sp", "oy", yv[:, :, :], x[:, :, 0:W], tuple(xk(k) for k in range(NCH)), (("oy", tile),))

        except _Stop:
            pass

        allk = [k for k in S.last_w if isinstance(k, tuple) and isinstance(k[0], str) and k[0].startswith("o")]
        S.add("sp", None, reads=tuple(allk), writes=())

        S.finalize(eng_sems, chain_sems)
        with nc.Block() as block:
            @block.tensor
            def _(h):
                S.emit_engine("pe", h)

            @block.scalar
            def _(h):
                S.emit_engine("act", h)

            @block.vector
            def _(h):
                S.emit_engine("dve", h)

            @block.gpsimd
            def _(h):
                S.emit_engine("pool", h)

            @block.sync
            def _(h):
                S.emit_engine("sp", h)
    return nc


_CACHE = {}


def _consts():
    bf = ml_dtypes.bfloat16
    ident = np.eye(128, dtype=np.float32)
    ones = np.ones((128, 128), np.float32)
    p = np.arange(128)
    Bd = (p[:, None] // 64 == p[None, :] // 64).astype(np.float32)
    Rm = np.zeros((128, 128), np.float32)
    for m in range(128):
        if m % 64 < 32:
            Rm[m + 32, m] = -1.0
        else:
            Rm[m - 32, m] = 1.0
    k = np.arange(128)[:, None]; q = np.arange(128)[None, :]
    prev = np.where(k > q, 0.0, NEG).astype(np.float32)
    cur = np.where(k <= q, 0.0, NEG).astype(np.float32)
    maskN = np.concatenate([prev, cur, prev, cur], axis=1)
    allneg = np.full((128, 128), NEG, np.float32)
    maskF0 = np.concatenate([allneg, cur, allneg, cur], axis=1)
    q8 = np.arange(8)[None, :]
    mc = np.where(k > q8, 0.0, NEG).astype(np.float32)
    maskc = np.tile(mc, (1, 16))
    k8 = np.arange(8)[:, None]
    mn = np.where(k8 <= q8, 0.0, NEG).astype(np.float32)
    maskn = np.full((128, 4 * 128), NEG, np.float32)
    for i in range(4):
        maskn[8 * i:8 * i + 8, 128 * i:128 * (i + 1)] = np.tile(mn, (1, 16))
    tril = np.tril(np.ones((128, 128), np.float32))
    cf = np.concatenate([ident, tril], axis=1).astype(np.float32)
    return ident, ones, Bd, Rm, maskN, maskF0, maskc, maskn, cf, bf


def prep_inputs(x_prompt, x_sample, cache_k, cache_v, norm1_g, w_in, q_norm_g, k_norm_g,
                attn_sinks, sgu_ln_g, sgu_ln_b, sgu_w, sgu_b, w_attn_up, w_sgu_up, w_out,
                norm2_g, w_ff1, w_ff2):
    f32 = np.float32
    x_prompt = np.asarray(x_prompt, f32); x_sample = np.asarray(x_sample, f32)
    cache_k = np.asarray(cache_k, f32); cache_v = np.asarray(cache_v, f32)
    w_in = np.asarray(w_in, f32)
    ident, ones, Bd, Rm, maskN, maskF0, maskc, maskn, cf, bf = _consts()

    w_in_p = np.zeros((DEPTH, D, IN_COLS_P), f32)
    w_in_p[:, :, 0:1024] = w_in[:, :, 0:1024]
    for j in range(4):
        kj = w_in[:, :, 1024 + 64 * j: 1024 + 64 * (j + 1)]
        base = OK_ + 256 * j
        w_in_p[:, :, base: base + 64] = kj
        w_in_p[:, :, base + 192: base + 256] = kj
    w_in_p[:, :, OV:OV + 256] = w_in[:, :, 1280:1536]
    w_in_p[:, :, OU:OU + 1024] = w_in[:, :, 1536:2560]
    w_in_p[:, :, OVS:OVS + 1024] = w_in[:, :, 2560:3584]
    w_in_p[:, :, OGA:OGA + 2048] = w_in[:, :, 3584:5632]
    w_in_p[:, :, OGM:OGM + 2048] = w_in[:, :, 5632:7680]

    vecs = np.zeros((128, NV_L * DEPTH), f32)
    for l in range(DEPTH):
        vecs[:, l * NV_L: l * NV_L + 16] = np.asarray(norm1_g[l], f32).reshape(16, 128).T
        vecs[:, l * NV_L + 16: l * NV_L + 32] = np.asarray(norm2_g[l], f32).reshape(16, 128).T
        vecs[:, l * NV_L + 32] = np.tile(np.asarray(q_norm_g[l], f32), 2)
        vecs[:, l * NV_L + 33] = np.tile(np.asarray(k_norm_g[l], f32), 2)
    lnG = np.ascontiguousarray(np.broadcast_to(np.asarray(sgu_ln_g, f32)[:, None, :], (DEPTH, 128, 1024)))
    lnB = np.ascontiguousarray(np.broadcast_to(np.asarray(sgu_ln_b, f32)[:, None, :], (DEPTH, 128, 1024)))
    sinks = np.ascontiguousarray(np.broadcast_to(np.asarray(attn_sinks, f32)[:, None, :], (DEPTH, 128, 16)))
    sgub = np.asarray(sgu_b, f32).reshape(DEPTH, 1, 1024)

    inv = np.power(np.float32(10000.0), -np.arange(32, dtype=np.float32) / np.float32(32))
    invp = np.tile(inv, 4).astype(f32)

    def tables(pos):
        ang = invp[:, None] * pos.astype(f32)[None, :]
        return np.cos(ang).astype(f32), np.sin(ang).astype(f32)

    shared = dict(w_in=w_in_p, w_au=np.asarray(w_attn_up, f32), w_su=np.asarray(w_sgu_up, f32),
                  w_out=np.asarray(w_out, f32), w_ff1=np.asarray(w_ff1, f32), w_ff2=np.asarray(w_ff2, f32),
                  vecs=vecs, lnG=lnG, lnB=lnB, sinks=sinks, sgu_w=np.asarray(sgu_w, f32), sgu_b=sgub, cf32=cf)
    in_maps = []
    for c in range(8):
        seq, part = c // 4, c % 4
        start = part * 1024 - 512
        xs = np.zeros((12 * 128, D), f32)
        lo = max(start, 0)
        xs[lo - start:] = x_prompt[seq, lo:start + 1536]
        pos = start + np.arange(1536)
        samp = x_sample[4 * c:4 * c + 4].reshape(32, D)
        pos_s = 16384 + np.tile(np.arange(8), 4)
        xa = np.concatenate([xs[128:768], samp, xs[0:128]], axis=0)
        pa = np.concatenate([pos[128:768], pos_s, pos[0:128]])
        xb = xs[768:1536]; pb = pos[768:1536]
        cA, sA = tables(pa); cB, sB = tables(pb)
        maskF = maskF0 if part == 0 else maskN
        cbf = np.concatenate([ident, ones, Bd, Rm, maskN, maskF, maskc, maskn], axis=1).astype(bf)
        m = dict(shared)
        m.update(xA=np.ascontiguousarray(xa.T), xB=np.ascontiguousarray(xb.T), cosA=cA, sinA=sA, cosB=cB, sinB=sB,
                 cache_k=np.ascontiguousarray(cache_k[:, 4 * c:4 * c + 4].reshape(DEPTH, 4, 128, 256)),
                 cache_v=np.ascontiguousarray(cache_v[:, 4 * c:4 * c + 4].reshape(DEPTH, 4, 128, 256)),
                 cbf=cbf)
        in_maps.append(m)
    return in_maps


def kernel(**inputs):
    in_maps = prep_inputs(**inputs)
    if "nc" not in _CACHE:
        _CACHE["nc"] = build_program()
    nc = _CACHE["nc"]
    res = run_bass_kernel_spmd(nc, in_maps, core_ids=list(range(8)))
    return assemble(res.results)


def assemble(R):
    f32 = np.float32
    y_prompt = np.zeros((2, 4096, D), f32)
    y_sample = np.zeros((32, 8, D), f32)
    nkp = np.zeros((DEPTH, 2, 128, 4, 64), f32); nvp = np.zeros_like(nkp)
    nks = np.zeros((DEPTH, 32, 128, 4, 64), f32); nvs = np.zeros_like(nks)
    nsv = np.zeros((DEPTH, 32, 8, 8, 128), f32)
    for c in range(8):
        seq, part = c // 4, c % 4
        yA = np.asarray(R[c]["yA"]).T
        yB = np.asarray(R[c]["yB"]).T
        base = part * 1024
        y_prompt[seq, base:base + 256] = yA[384:640]
        y_prompt[seq, base + 256:base + 1024] = yB
        y_sample[4 * c:4 * c + 4] = yA[640:672].reshape(4, 8, D)
        if part == 3:
            nkp[:, seq] = np.asarray(R[c]["kp"]).reshape(DEPTH, 128, 4, 64)
            nvp[:, seq] = np.asarray(R[c]["vp"]).reshape(DEPTH, 128, 4, 64)
        nks[:, 4 * c:4 * c + 4] = np.asarray(R[c]["ks"]).reshape(DEPTH, 4, 128, 4, 64)
        nvs[:, 4 * c:4 * c + 4] = np.asarray(R[c]["vs"]).reshape(DEPTH, 4, 128, 4, 64)
        nsv[:, 4 * c:4 * c + 4] = np.asarray(R[c]["sv"]).reshape(DEPTH, 4, 8, 8, 128)
    return (y_prompt, y_sample, nkp, nvp, nks, nvs, nsv)
```

```python
import math
from contextlib import ExitStack

import numpy as np
import ml_dtypes

import concourse.bass as bass
import concourse.mybir as mybir
from concourse.bass_utils import run_bass_kernel_spmd

F32 = mybir.dt.float32
BF16 = mybir.dt.bfloat16
AF = mybir.ActivationFunctionType
ALU = mybir.AluOpType
AX = mybir.AxisListType

D = 2048
NCH = 16
DEPTH = 4
WA = 672
WB = 768
XW = 800
IN_COLS_P = 8448
OQ, OK_, OV, OU, OVS, OGA, OGM = 0, 1024, 2048, 2304, 3328, 4352, 6400
NV_L = 34
NEG = -30000.0
EPL = 9
ATL = 9


class Op:
    __slots__ = ("eng", "emit", "deps", "needs_inc", "ticket", "chain", "semh")


class Sched:
    ENGS = ("pe", "act", "dve", "pool", "sp")

    def __init__(self):
        self.ops = {e: [] for e in self.ENGS}
        self.last_w = {}
        self.readers = {}
        self.chains = {}

    def add(self, eng, emit, reads=(), writes=(), chain=None):
        op = Op()
        op.eng = eng; op.emit = emit; op.deps = set(); op.needs_inc = False
        op.chain = chain; op.ticket = 0; op.semh = None
        lw = self.last_w; rd = self.readers
        for k in reads:
            w = lw.get(k)
            if w is not None:
                op.deps.add(w)
        for k in writes:
            w = lw.get(k)
            if w is not None:
                op.deps.add(w)
            r = rd.get(k)
            if r:
                op.deps.update(r.values())
        rk = ("c", chain, id(op)) if chain is not None else eng
        for k in reads:
            r = rd.get(k)
            if r is None:
                r = rd[k] = {}
            r[rk] = op
        for k in writes:
            lw[k] = op
            rd[k] = {}
        if chain is not None:
            ch = self.chains.setdefault(chain, [])
            if ch:
                op.deps.add(ch[-1])
            ch.append(op)
            op.needs_inc = True
        op.deps.discard(op)
        if eng == "pe":
            op.deps = {d for d in op.deps if not (d.eng == "pe" and d.chain is None)}
        for d in op.deps:
            d.needs_inc = True
        self.ops[eng].append(op)
        return op

    def finalize(self, eng_sems, chain_sems):
        for e in self.ENGS:
            n = 0
            for op in self.ops[e]:
                if op.chain is not None:
                    continue
                op.semh = eng_sems[e]
                if op.needs_inc:
                    n += 1
                    op.ticket = n
        for cname, ch in self.chains.items():
            for i, op in enumerate(ch):
                op.semh = chain_sems[cname]
                op.ticket = 16 * (i + 1)

    def emit_engine(self, e, h):
        waited = {}
        for op in self.ops[e]:
            need = {}
            for d in op.deps:
                key = id(d.semh)
                if need.get(key, (None, 0))[1] < d.ticket:
                    need[key] = (d.semh, d.ticket)
            for key, (s, v) in need.items():
                if waited.get(key, 0) < v:
                    h.wait_ge(s, v)
                    waited[key] = v
            if op.emit is not None:
                ins = op.emit(h)
                if op.needs_inc:
                    ins.then_inc(op.semh, 16 if op.chain is not None else 1)


def segs(c0, c1):
    n = c1 - c0
    ns = -(-n // 512)
    w = n // ns
    out = []
    for i in range(ns):
        a = c0 + i * w
        out.append((a, (c1 - a) if i == ns - 1 else w))
    return out


class _Stop(Exception):
    pass


def build_program(NL=DEPTH, TILES=("A", "B"), STOP=None):
    nc = bass.Bass("TRN2", target_bir_lowering=False)
    S = Sched()

    def din(name, shape, dt=F32):
        return nc.dram_tensor(name, list(shape), dt, kind="ExternalInput").ap()

    def dout(name, shape, dt=F32):
        return nc.dram_tensor(name, list(shape), dt, kind="ExternalOutput").ap()

    xA_d = din("xA", [D, XW]); xB_d = din("xB", [D, WB])
    cosA_d = din("cosA", [128, XW]); sinA_d = din("sinA", [128, XW])
    cosB_d = din("cosB", [128, WB]); sinB_d = din("sinB", [128, WB])
    win_d = din("w_in", [DEPTH, D, IN_COLS_P])
    wau_d = din("w_au", [DEPTH, 1024, D]); wsu_d = din("w_su", [DEPTH, 1024, D])
    wout_d = din("w_out", [DEPTH, D, D])
    w1_d = din("w_ff1", [DEPTH, D, 8192]); w2_d = din("w_ff2", [DEPTH, 8192, D])
    vecs_d = din("vecs", [128, NV_L * DEPTH])
    lnG_d = din("lnG", [DEPTH, 128, 1024]); lnB_d = din("lnB", [DEPTH, 128, 1024])
    sinks_d = din("sinks", [DEPTH, 128, 16])
    sguw_d = din("sgu_w", [DEPTH, 8, 128, 128]); sgub_d = din("sgu_b", [DEPTH, 1, 1024])
    ck_d = din("cache_k", [DEPTH, 4, 128, 256]); cv_d = din("cache_v", [DEPTH, 4, 128, 256])
    cb_d = din("cbf", [128, 4 * 128 + 2 * 512 + 128 + 512], BF16)
    cf_d = din("cf32", [128, 256])

    yA_d = dout("yA", [D, WA]); yB_d = dout("yB", [D, WB])
    kp_d = dout("kp", [DEPTH, 128, 256]); vp_d = dout("vp", [DEPTH, 128, 256])
    ks_d = dout("ks", [DEPTH, 4, 128, 256]); vs_d = dout("vs", [DEPTH, 4, 128, 256])
    sv_d = dout("sv", [DEPTH, 32, 1024])
    scr_d = nc.dram_tensor("scr", [DEPTH, 128, 1024 + 260], BF16, kind="Internal").ap()

    es = ExitStack()
    with es:
        def sb(name, shape, dt):
            return es.enter_context(nc.sbuf_tensor("sb_" + name, list(shape), dt))

        x = sb("x", [128, NCH, WB], F32)
        xn = sb("xn", [128, NCH, XW], BF16)
        R1 = sb("R1", [128, NCH, WB], BF16)
        R2 = sb("R2", [128, NCH, WB], BF16)
        wsl = [sb(f"wsl{i}", [128, 4096], BF16) for i in range(4)]
        cosT = sb("cosT", [128, XW], F32); sinT = sb("sinT", [128, XW], F32)
        vaug = sb("vaug", [128, 8, 4, 65], BF16)
        kTprev = sb("kTprev", [128, 8, 128], BF16)
        SC = sb("SC", [128, 2048], F32)
        arena = sb("arena", [128, 4096], BF16)
        bscr = arena[:, 0:1024]
        lnG = sb("lnG", [128, 1024], F32); lnB = sb("lnB", [128, 1024], F32)
        a_tok = arena[:, 2048:3072]
        pT = [arena[:, 3072:3584], arena[:, 3584:4096]]
        pTn = sb("pTn", [128, 256], BF16)
        wsT = sb("wsT", [128, 8, 128], BF16)
        wsT_bd = sb("wsT_bd", [128, 8, 32], BF16)
        brow_s = sb("brow_s", [128, 8, 32], BF16)
        brow = sb("brow", [128, 1024], BF16)
        kc32 = [sb("kc32_0", [128, 256], F32)] * 2
        vc32 = [sb("vc32_0", [128, 256], F32)] * 2
        kcd = sb("kcd", [128, 4, 2, 128], BF16)
        vcaug = sb("vcaug", [128, 4, 65], BF16)
        cb = sb("cb", [128, 4 * 128 + 2 * 512 + 128 + 512], BF16)
        cf = sb("cf", [128, 256], F32)
        vecs = sb("vecs", [128, NV_L * DEPTH], F32)
        sk32 = sb("sk32", [128, 16], F32); esk = sb("esk", [128, 16], F32)
        den = sb("den", [128, 16], F32); rden = sb("rden", [128, 16], F32)
        st = sb("st", [128, 8], F32)
        dummy = sb("dummy", [128, 2], F32)
        kst = sb("kst", [128, 256], F32); vst = sb("vst", [128, 256], F32)
        kfin = arena[:, 1024:2048].bitcast(F32).rearrange("p (a b) -> p a b", a=4)
        gv2 = arena[:, 0:2048].bitcast(F32); lnt2 = arena[:, 2048:4096].bitcast(F32)
        kcT = a_tok[:, :].rearrange("p (c t) -> p c t", c=8)
        psb = [es.enter_context(nc.psum_tensor(f"ps{i}", [128, 512], F32)) for i in range(8)]
        _CACHE["sbuf_left"] = nc.sbuf_bytes_remaining

        identb = cb[:, 0:128]; onesb = cb[:, 128:256]; Bd = cb[:, 256:384]; Rm = cb[:, 384:512]
        maskN = cb[:, 512:1024]; maskF = cb[:, 1024:1536]; maskc = cb[:, 1536:1664]
        maskn = lambda i: cb[:, 1664 + 128 * i: 1792 + 128 * i]
        identf = cf[:, 0:128]; tril = cf[:, 128:256]
        ones_row = cb[0:1, 128:256]

        eng_sems = {e: es.enter_context(nc.semaphore("sem_" + e)) for e in Sched.ENGS}
        chain_names = ["w0", "w1", "w2", "w3", "const", "xload", "par", "par2", "bd", "cache0", "cache1", "scrw", "scrr",
                       "oy", "okp", "ovp", "oks", "ovs", "osv", "occ"]
        chain_sems = {c: es.enter_context(nc.semaphore("ch_" + c)) for c in chain_names}

        free_banks = list(range(8))

        def balloc():
            assert free_banks, "out of PSUM banks"
            return free_banks.pop(0)

        def bfree(b):
            free_banks.append(b)

        def dma(queue, chain, out, in_, reads, writes):
            def emit(h, out=out, in_=in_):
                return h.dma_start(out=out, in_=in_)
            S.add(queue, emit, reads=reads, writes=writes, chain=chain)

        def act(out, in_, func, reads, writes, scale=None, bias=None):
            def emit(h):
                kw = {}
                if scale is not None:
                    kw["scale"] = scale
                if bias is not None:
                    kw["bias"] = bias
                return h.activation(out=out, in_=in_, func=func, **kw)
            S.add("act", emit, reads=reads, writes=writes)

        def dve(fn, reads, writes):
            S.add("dve", fn, reads=reads, writes=writes)

        def pe(items, reads, writes):
            def emit(h, items=items):
                ins = None
                for it in items:
                    if it[0] == "T":
                        ins = h.transpose(out=it[1], in_=it[2], identity=it[3])
                    else:
                        ins = h.matmul(it[0], it[1], it[2], start=it[3], stop=it[4])
                return ins
            S.add("pe", emit, reads=reads, writes=writes)

        def rstd_ops(out, in_ps, scale, eps, reads, writes):
            act(out, in_ps, AF.Ln, reads, writes, scale=scale, bias=eps)
            act(out, out, AF.Exp, writes, writes, scale=-0.5)

        wstate = {"n": 0}
        ssq = {"banks": None}

        def ssq_open(seglist):
            ssq["banks"] = {c0: balloc() for (c0, n, tag) in seglist}

        def ssq_take():
            b = ssq["banks"]; ssq["banks"] = None
            return b

        def wload(view, nk, ncols):
            s = wstate["n"] % 4
            wstate["n"] += 1
            dst = wsl[s][:, 0:nk * ncols].rearrange("p (k n) -> p k n", k=nk)
            dma("pool", f"w{s}", dst, view, reads=(), writes=(("w", s),))
            return dst, ("w", s)

        def proj(parts, ncols, seglist, epilogue, cg=512):
            ngroups = -(-ncols // cg)
            pending = [None]
            for g in range(ngroups):
                gc = min(cg, ncols - g * cg)
                loaded = []
                for (W, row0, nk, col0, src_fn, src_keys) in parts:
                    k0 = 0
                    while k0 < nk:
                        kk = min(4096 // gc, nk - k0)
                        view = W[row0 + k0 * 128: row0 + (k0 + kk) * 128, col0 + g * cg: col0 + g * cg + gc] \
                            .rearrange("(k p) n -> p k n", p=128)
                        dst, key = wload(view, kk, gc)
                        loaded.append((dst, key, k0, kk, src_fn, src_keys))
                        k0 += kk
                for seg in seglist:
                    c0, n, tag = seg
                    for cl in range(gc // 128):
                        ci = g * (cg // 128) + cl
                        bank = balloc()
                        items = []
                        rkeys = []
                        tot = sum(l[3] for l in loaded)
                        i = 0
                        for (dst, key, k0, kk, src_fn, src_keys) in loaded:
                            rkeys.append(key)
                            rkeys.extend(src_keys)
                            for k in range(kk):
                                items.append((psb[bank][:, 0:n], dst[:, k, cl * 128:(cl + 1) * 128],
                                              src_fn(k0 + k, c0, n), i == 0, i == tot - 1))
                                i += 1
                        pe(items, reads=rkeys, writes=(("ps", bank),))
                        if pending[0] is not None:
                            pending[0](); pending[0] = None
                        r = epilogue(ci, seg, bank)
                        bfree(bank)
                        if callable(r):
                            pending[0] = r
            if pending[0] is not None:
                pending[0](); pending[0] = None

        dma("sp", "const", cb[:, :], cb_d[:, :], (), (("cb",),))
        dma("sp", "const", cf[:, :], cf_d[:, :], (), (("cf",),))
        dma("sp", "const", vecs[:, :], vecs_d[:, :], (), (("vecs",),))
        for e in ("pe", "act", "dve"):
            S.add(e, None, reads=(("cb",), ("cf",), ("vecs",)), writes=())
        dve(lambda h: h.memset(vaug[:, :, :, :].rearrange("p a b c -> p (a b c)"), 1.0), (), tuple(("vaug", i) for i in range(8)))
        dve(lambda h: h.memset(vcaug[:, :, :].rearrange("p a b -> p (a b)"), 1.0), (), (("vcaug",),))
        dve(lambda h: h.memset(brow[:, :], 0.0), (), (("brow",),))
        dve(lambda h: h.memset(wsT_bd[:, :, :].rearrange("p a b -> p (a b)"), 0.0), (), (("wsT_bd",),))
        dve(lambda h: h.memset(brow_s[:, :, :].rearrange("p a b -> p (a b)"), 0.0), (), (("brow_s",),))
        dve(lambda h: h.memset(pTn[:, :], 0.0), (), (("pTn",),))
        dve(lambda h: h.memset(kcd[:, :, :, :].rearrange("p a b c -> p (a b c)"), 0.0), (), (("kcd",),))
        dve(lambda h: h.memset(a_tok[:, :], 0.0), (), tuple(("a_tok", c) for c in range(8)))
        dve(lambda h: h.memset(kfin[:, :, :].rearrange("p a b -> p (a b)"), 0.0), (), (("kfin",),))
        for i_ in range(2):
            dve(lambda h, i_=i_: h.memset(pT[i_][:, :], 0.0), (), (("pT", i_),))
        dve(lambda h: h.memset(R1[:, :, :].rearrange("p a b -> p (a b)"), 0.0), (), tuple(("R1", c) for c in range(NCH)))
        dve(lambda h: h.memset(R2[:, :, :].rearrange("p a b -> p (a b)"), 0.0), (), tuple(("R2", c) for c in range(NCH)))
        dve(lambda h: h.memset(xn[:, :, :].rearrange("p a b -> p (a b)"), 0.0), (), tuple(("xn", c) for c in range(NCH)))

        def xk(k):
            return ("x", k)

        def stop(n):
            if STOP == n:
                raise _Stop()

        try:
          for tile in TILES:
              W = WA if tile == "A" else WB
              nblk_tile = 5 if tile == "A" else 6
              if tile == "A":
                  xv = xA_d.rearrange("(k p) n -> p k n", p=128)
                  dma("sp", "xload", x[:, :, 0:WA], xv[:, :, 0:WA], (), tuple(xk(k) for k in range(NCH)))
                  x0 = SC[:, :].rearrange("p (k n) -> p k n", k=NCH)
                  dma("sp", "xload", x0, xv[:, :, WA:XW], (), tuple(("S", i) for i in range(4)))
                  dma("sp", "xload", cosT[:, :], cosA_d[:, :], (), (("cos",),))
                  dma("sp", "xload", sinT[:, :], sinA_d[:, :], (), (("sin",),))
              else:
                  xv = xB_d.rearrange("(k p) n -> p k n", p=128)
                  dma("sp", "xload", x[:, :, 0:WB], xv[:, :, :], (), tuple(xk(k) for k in range(NCH)))
                  dma("sp", "xload", cosT[:, 0:WB], cosB_d[:, :], (), (("cos",),))
                  dma("sp", "xload", sinT[:, 0:WB], sinB_d[:, :], (), (("sin",),))

              carry = None
              for l in range(NL):
                  vb = l * NV_L
                  g1 = lambda k, vb=vb: vecs[:, vb + k: vb + k + 1]
                  g2 = lambda k, vb=vb: vecs[:, vb + 16 + k: vb + 17 + k]
                  qg = vecs[:, vb + 32: vb + 33]; kg = vecs[:, vb + 33: vb + 34]
                  if tile == "A":
                      cf0 = 128 * l
                      ck0 = 128 * (l - 1) if l >= 1 else 0
                      blocks = list(range(l + 1, 6))
                      bcol = lambda b: (b - 1) * 128
                      kvblk = l
                  else:
                      cf0 = 0; ck0 = 0
                      blocks = list(range(6, 12))
                      bcol = lambda b: (b - 6) * 128
                      kvblk = None
                  vslot = lambda b: (b - 1) if tile == "A" else (b - 6)
                  full_segs = [(a, n, "m") for (a, n) in segs(cf0, W)]
                  kv_segs = [(a, n, "m") for (a, n) in segs(ck0, W)]
                  if tile == "A" and l == 0:
                      kv_segs = kv_segs + [(WA, 128, "b0")]

                  dma("sp", "par", lnG[:, :], lnG_d[l], (), (("lnG",),))
                  dma("sp", "par", lnB[:, :], lnB_d[l], (), (("lnB",),))
                  dma("sp", "par", sk32[:, :], sinks_d[l], (), (("sk32",),))
                  dma("pool", "par2", brow[0:1, :], sgub_d[l], (), (("brow",),))
                  act(esk[:, :], sk32[:, :], AF.Exp, (("sk32",),), (("esk",),))

                  def rmsnorm(src_is_x0, c0, n, gfn, dstc0, ssbank=None):
                      bank = balloc() if ssbank is None else ssbank
                      for k in range(NCH if ssbank is None else 0):
                          sq = bscr[:, (k % 2) * 512:(k % 2) * 512 + n]
                          src = x0[:, k, 0:n] if src_is_x0 else x[:, k, c0:c0 + n]
                          skeys = (("S", 0), ("S", 1), ("S", 2), ("S", 3)) if src_is_x0 else (xk(k),)
                          if k % 2 == 0:
                              act(sq, src, AF.Square, skeys, (("bs", 0),))
                          else:
                              dve(lambda h, sq=sq, src=src: h.tensor_tensor(out=sq, in0=src, in1=src, op=ALU.mult), skeys, (("bs", 1),))
                          pe([(psb[bank][:, 0:n], onesb, sq, k == 0, k == NCH - 1)], (("bs", k % 2),), (("ps", bank),))
                      rs = SC[:, 0:n] if not src_is_x0 else None
                      if src_is_x0:
                          rs = kfin[:, 0, 0:n]
                          rkey = ("kfin",)
                      else:
                          rkey = ("S", 0) if n <= 512 else None
                      wk = (rkey,) if rkey is not None else (("S", 0), ("S", 1))
                      rstd_ops(rs, psb[bank][:, 0:n], 1.0 / D, 1e-6, (("ps", bank),), wk)
                      bfree(bank)
                      for k in range(NCH):
                          src = x0[:, k, 0:n] if src_is_x0 else x[:, k, c0:c0 + n]
                          skeys = (("S", 0), ("S", 1), ("S", 2), ("S", 3)) if src_is_x0 else (xk(k),)
                          dve(lambda h, src=src, k=k: h.scalar_tensor_tensor(
                              out=xn[:, k, dstc0:dstc0 + n], in0=src, scalar=gfn(k), in1=rs, op0=ALU.mult, op1=ALU.mult),
                              skeys + wk, (("xn", k),))

                  if tile == "A" and l == 0:
                      rmsnorm(True, 0, 128, g1, WA)
                  for (a, n) in segs(ck0, W):
                      rmsnorm(False, a, n, g1, a, ssbank=(carry[a] if carry is not None else None))
                  carry = None
                  xn_keys = tuple(("xn", k) for k in range(NCH))
                  xn_src = lambda k, c0, n: xn[:, k, c0:c0 + n]

                  stop(1)
                  ws32 = SC[:, 1024:2048].rearrange("p (g s) -> p g s", g=8)
                  dma("sp", "par", ws32, sguw_d[l].rearrange("g t s -> t g s"), (), (("S", 2), ("S", 3)))
                  for g in range(8):
                      dve(lambda h, g=g: h.tensor_tensor(out=ws32[:, g, :], in0=ws32[:, g, :], in1=tril, op=ALU.mult),
                          (("S", 2), ("S", 3)), (("S", 2), ("S", 3)))
                  for half in range(2):
                      bank = balloc()
                      items = [("T", psb[bank][:, j * 128:(j + 1) * 128], ws32[:, half * 4 + j, :], identf) for j in range(4)]
                      pe(items, (("S", 2), ("S", 3)), (("ps", bank),))
                      act(wsT[:, half * 4:half * 4 + 4, :], psb[bank][:, :].rearrange("p (g t) -> p g t", g=4), AF.Copy,
                          (("ps", bank),), (("wsT",),))
                      bfree(bank)

                  if tile == "A":
                      for i in range(4):
                          dma("sp", "bd", wsT_bd[8 * i:8 * i + 8, :, 8 * i:8 * i + 8], wsT[0:8, :, 0:8], (("wsT",),), (("wsT_bd",),))
                          dma("sp", "bd", brow_s[0:1, :, 8 * i:8 * i + 8],
                              brow[0:1, :].rearrange("o (g t) -> o g t", g=8)[:, :, 0:8], (("brow",),), (("brow_s",),))
                  stop(2)
                  qT = R1[:, 0:8, :]; kT = R1[:, 8:16, :]

                  def qk_epilogue(is_q):
                      gain = qg if is_q else kg

                      def ep(ci, seg, bank):
                          c0, n, tag = seg
                          z = psb[bank][:, 0:n]
                          sqz = bscr[:, 0:n]; y = bscr[:, 512:512 + n]
                          rs = SC[:, 0:n]; t1 = SC[:, 512:512 + n]; t2 = SC[:, 1024:1024 + n]
                          act(sqz, z, AF.Square, (("ps", bank),), (("bs", 0),))
                          act(y, z, AF.Copy, (("ps", bank),), (("bs", 1),), scale=gain)

                          def later():
                              b2 = balloc(); b3 = balloc()
                              pe([(psb[b2][:, 0:n], Bd, sqz, True, True)], (("bs", 0),), (("ps", b2),))
                              pe([(psb[b3][:, 0:n], Rm, y, True, True)], (("bs", 1),), (("ps", b3),))
                              rstd_ops(rs, psb[b2][:, 0:n], 1.0 / 64, 1e-6, (("ps", b2),), (("S", 0),))
                              dve(lambda h: h.tensor_tensor(out=t1, in0=y, in1=cosT[:, c0:c0 + n], op=ALU.mult),
                                  (("bs", 1), ("cos",)), (("S", 1),))
                              dve(lambda h: h.tensor_tensor(out=t2, in0=psb[b3][:, 0:n], in1=sinT[:, c0:c0 + n], op=ALU.mult),
                                  (("ps", b3), ("sin",)), (("S", 2),))
                              dve(lambda h: h.tensor_tensor(out=t1, in0=t1, in1=t2, op=ALU.add),
                                  (("S", 1), ("S", 2)), (("S", 1),))
                              if is_q:
                                  o = qT[:, ci, c0:c0 + n]; ok = ("R1", ci)
                              elif tag == "b0":
                                  o = kTprev[:, ci, 0:n]; ok = ("kTprev",)
                              else:
                                  o = kT[:, ci, c0:c0 + n]; ok = ("R1", 8 + ci)
                              dve(lambda h: h.tensor_tensor(out=o, in0=t1, in1=rs, op=ALU.mult),
                                  (("S", 1), ("S", 0)), (ok,))
                              if not is_q:
                                  if tile == "B" and c0 + n == WB and ci % 2 == 0:
                                      lo = WB - 128 - c0
                                      dve(lambda h: h.tensor_tensor(out=kfin[:, ci // 2, :], in0=t1[:, lo:lo + 128], in1=rs[:, lo:lo + 128], op=ALU.mult),
                                          (("S", 1), ("S", 0)), (("kfin",),))
                                  if tile == "A" and tag == "m" and c0 + n == WA and ci % 2 == 0:
                                      lo = WA - 32 - c0
                                      dve(lambda h: h.tensor_tensor(out=kfin[:, ci // 2, 0:32], in0=t1[:, lo:lo + 32], in1=rs[:, lo:lo + 32], op=ALU.mult),
                                          (("S", 1), ("S", 0)), (("kfin",),))
                              bfree(b2); bfree(b3)
                          return later
                      return ep

                  proj([(win_d[l], 0, NCH, OQ, xn_src, xn_keys)], 1024, full_segs, qk_epilogue(True))
                  proj([(win_d[l], 0, NCH, OK_, xn_src, xn_keys)], 1024, kv_segs, qk_epilogue(False))

                  stop(3)
                  vview = win_d[l][:, OV:OV + 256].rearrange("(k p) n -> p k n", p=128)
                  wv, wvkey = wload(vview, NCH, 256)

                  def vproj(colstart, m, dst_ap, dkey, f32_dst=None, f32key=None):
                      bank = balloc()
                      items = [(psb[bank][:, 0:256], xn[:, k, colstart:colstart + 128], wv[:, k, :], k == 0, k == NCH - 1)
                               for k in range(NCH)]
                      pe(items, (wvkey,) + xn_keys, (("ps", bank),))
                      act(dst_ap, psb[bank][:, 0:256].rearrange("p (h d) -> p h d", h=4), AF.Copy, (("ps", bank),), (dkey,))
                      if f32_dst is not None:
                          act(f32_dst, psb[bank][:, 0:256], AF.Copy, (("ps", bank),), (f32key,))
                      bfree(bank)

                  if tile == "A" and l == 0:
                      vproj(WA, 128, vaug[:, 7, :, 0:64], ("vaug", 7))
                  kvb_list = ([kvblk] if (kvblk is not None and l >= 1) else []) + blocks
                  for b in kvb_list:
                      last = (tile == "B" and b == 11)
                      vproj(bcol(b), 128, vaug[:, vslot(b), :, 0:64], ("vaug", vslot(b)),
                            vst[:, :] if last else None, ("vst",) if last else None)
                  if tile == "B":
                      dma("sp", "scrr", kTprev[:, :, :], scr_d[l][:, 0:1024].rearrange("p (c t) -> p c t", c=8),
                          (("scr", l),), (("kTprev",),))
                      dma("sp", "scrr", vaug[:, 7, :, :], scr_d[l][:, 1024:1284].rearrange("p (h d) -> p h d", h=4),
                          (("scr", l),), (("vaug", 7),))

                  stop(4)
                  aT = R2[:, 0:8, :]; mT = R2[:, 8:16, :]
                  qkeys = tuple(("R1", c) for c in range(8))
                  kkeys = tuple(("R1", 8 + c) for c in range(8))
                  akeys = tuple(("R2", c) for c in range(8))

                  def attn_norm(np_, obanks):
                      for bi, ob in enumerate(obanks):
                          h0 = bi * 7; nh = min(7, 16 - h0)
                          ov = psb[ob][0:np_, 0:nh * 65].rearrange("p (h e) -> p h e", h=nh)
                          dve(lambda h, ov=ov, h0=h0, nh=nh: h.tensor_tensor(out=den[0:np_, h0:h0 + nh], in0=ov[:, :, 64],
                                                                           in1=esk[0:np_, h0:h0 + nh], op=ALU.add),
                              (("ps", ob), ("esk",)), (("den", bi),))
                          dve(lambda h, h0=h0, nh=nh: h.reciprocal(out=rden[0:np_, h0:h0 + nh], in_=den[0:np_, h0:h0 + nh]),
                              (("den", bi),), (("rden", bi),))
                          dve(lambda h, ov=ov, h0=h0, nh=nh: h.tensor_tensor(
                              out=a_tok[0:np_, h0 * 64:(h0 + nh) * 64].rearrange("p (h d) -> p h d", h=nh), in0=ov[:, :, 0:64],
                              in1=rden[0:np_, h0:h0 + nh].unsqueeze(2).broadcast_to([np_, nh, 64]), op=ALU.mult),
                              (("ps", ob), ("rden", bi)), tuple(("a_tok", c) for c in range(h0 // 2, (h0 + nh - 1) // 2 + 1)))
                          bfree(ob)

                  def attn_tr(col0, ncols):
                      for half in range(2):
                          bank = balloc()
                          pv = psb[bank][:, 0:256].bitcast(BF16)
                          items = [("T", pv[:, j * 128:(j + 1) * 128], a_tok[:, (half * 4 + j) * 128:(half * 4 + j + 1) * 128],
                                    identb) for j in range(4)]
                          pe(items, tuple(("a_tok", half * 4 + j) for j in range(4)), (("ps", bank),))
                          act(aT[:, half * 4:half * 4 + 4, col0:col0 + ncols],
                              pv[:, 0:512].rearrange("p (c t) -> p c t", c=4)[:, :, 0:ncols], AF.Copy,
                              (("ps", bank),), akeys[half * 4:half * 4 + 4])
                          bfree(bank)

                  atok_all = tuple(("a_tok", c) for c in range(8))
                  pti = [0]
                  pend_tr = [None]
                  for b in blocks:
                      c_q = bcol(b)
                      first = (b == blocks[0]) and (tile == "B" or l == 0)
                      if first:
                          kprev = lambda kv, hf: kTprev[:, 2 * kv + hf, :]
                          kprev_key = ("kTprev",); vprev = 7
                      else:
                          cp = bcol(b - 1)
                          kprev = lambda kv, hf, cp=cp: kT[:, 2 * kv + hf, cp:cp + 128]
                          kprev_key = None; vprev = vslot(b - 1)
                      mask_ap = maskF if (b == 4) else maskN
                      obanks = [balloc(), balloc(), balloc()]

                      def pv_mm(pr, ps_i, obanks=obanks, vprev=vprev, b=b):
                          items = []
                          for hf in range(2):
                              hd = 2 * pr + hf; kv = hd // 4
                              ob = obanks[hd // 7]; off = (hd % 7) * 65
                              items.append((psb[ob][:, off:off + 65], pT[ps_i][:, (hf * 2) * 128:(hf * 2 + 1) * 128],
                                            vaug[:, vprev, kv, :], True, False))
                              items.append((psb[ob][:, off:off + 65], pT[ps_i][:, (hf * 2 + 1) * 128:(hf * 2 + 2) * 128],
                                            vaug[:, vslot(b), kv, :], False, True))
                          pe(items, (("pT", ps_i), ("vaug", vprev), ("vaug", vslot(b))), tuple(("ps", o) for o in obanks))

                      prev = None
                      for pr in range(8):
                          bank = balloc()
                          items = [(psb[bank][:, 0:512], identb, mask_ap, True, False)]
                          for hf in range(2):
                              hd = 2 * pr + hf; kv = hd // 4
                              q_ap = qT[:, pr, c_q:c_q + 128]
                              items.append((psb[bank][:, (hf * 2) * 128:(hf * 2 + 1) * 128], kprev(kv, hf), q_ap, False, False))
                              items.append((psb[bank][:, (hf * 2 + 1) * 128:(hf * 2 + 2) * 128],
                                            kT[:, 2 * kv + hf, c_q:c_q + 128], q_ap, False, hf == 1))
                          rk = qkeys + kkeys + ((kprev_key,) if kprev_key else ())
                          pe(items, rk, (("ps", bank),))
                          ps_i = pti[0] % 2; pti[0] += 1
                          act(pT[ps_i][:, :], psb[bank][:, 0:512], AF.Exp, (("ps", bank),), (("pT", ps_i),), scale=0.125)
                          bfree(bank)
                          if prev is not None:
                              pv_mm(*prev)
                          prev = (pr, ps_i)
                          if pr == 2 and pend_tr[0] is not None:
                              attn_tr(*pend_tr[0]); pend_tr[0] = None
                      pv_mm(*prev)
                      attn_norm(128, obanks)
                      pend_tr[0] = (c_q, 128)
                  if pend_tr[0] is not None:
                      attn_tr(*pend_tr[0]); pend_tr[0] = None

                  stop(5)
                  if tile == "B":
                      bank = balloc()
                      items = [("T", psb[bank][:, j * 128:(j + 1) * 128], kfin[:, j, :], identf) for j in range(4)]
                      pe(items, (("kfin",),), (("ps", bank),))
                      act(kst[:, :].rearrange("p (h d) -> p h d", h=4), psb[bank][:, :].rearrange("p (h e) -> p h e", h=4)[:, :, 0:64],
                          AF.Copy, (("ps", bank),), (("kst",),))
                      bfree(bank)
                      dma("sp", "okp", kp_d[l], kst[:, :], (("kst",),), (("okp", l),))
                      dma("sp", "ovp", vp_d[l], vst[:, :], (("vst",),), (("ovp", l),))
                  else:
                      c5 = bcol(5)
                      dma("sp", "scrw", scr_d[l][:, 0:1024].rearrange("p (c t) -> p c t", c=8), kT[:, :, c5:c5 + 128],
                          kkeys, (("scr", l),))
                      dma("sp", "scrw", scr_d[l][:, 1024:1284].rearrange("p (h d) -> p h d", h=4), vaug[:, vslot(5), :, :],
                          (("vaug", vslot(5)),), (("scr", l),))

                  stop(6)
                  if tile == "A":
                      vproj(640, 128, vaug[:, 6, :, 0:64], ("vaug", 6), vst[:, :], ("vst",))
                      for i in range(4):
                          cs = 640 + 8 * i
                          dma("sp", "cache0", kc32[0][:, :], ck_d[l, i], (), (("kc32", 0),))
                          dma("sp", "cache0", vc32[0][:, :], cv_d[l, i], (), (("vc32", 0),))
                          dma("sp", "occ", ks_d[l, i, 0:120, :], ck_d[l, i, 8:128, :], (), (("oks_c", l, i),))
                          dma("sp", "occ", vs_d[l, i, 0:120, :], cv_d[l, i, 8:128, :], (), (("ovs_c", l, i),))
                          kcv = kc32[0][:, :].rearrange("p (h d) -> p h d", h=4)
                          act(kcd[:, :, 0, 0:64], kcv, AF.Copy, (("kc32", 0),), (("kcd",),))
                          act(kcd[:, :, 1, 64:128], kcv, AF.Copy, (("kc32", 0),), (("kcd",),))
                          act(vcaug[:, :, 0:64], vc32[0][:, :].rearrange("p (h d) -> p h d", h=4), AF.Copy,
                              (("vc32", 0),), (("vcaug",),))
                          kcd8 = kcd[:, :, :, :].rearrange("p a b c -> p (a b) c")
                          for half in range(2):
                              bank = balloc()
                              pv = psb[bank][:, 0:256].bitcast(BF16)
                              items = [("T", pv[:, j * 128:(j + 1) * 128], kcd8[:, half * 4 + j, :], identb) for j in range(4)]
                              pe(items, (("kcd",),), (("ps", bank),))
                              act(kcT[:, half * 4:half * 4 + 4, :], pv[:, 0:512].rearrange("p (c t) -> p c t", c=4), AF.Copy,
                                  (("ps", bank),), atok_all[half * 4:half * 4 + 4])
                              bfree(bank)
                          bc = balloc(); bn = balloc()
                          items = [(psb[bc][:, 0:128], identb, maskc, True, False)]
                          for hd in range(16):
                              hf = hd % 2; kv = hd // 4; pr = hd // 2
                              items.append((psb[bc][:, hd * 8:(hd + 1) * 8], kcT[:, 2 * kv + hf, :],
                                            qT[:, pr, cs:cs + 8], False, hd == 15))
                          pe(items, atok_all + qkeys, (("ps", bc),))
                          items = [(psb[bn][:, 0:128], identb, maskn(i), True, False)]
                          for hd in range(16):
                              hf = hd % 2; kv = hd // 4; pr = hd // 2
                              items.append((psb[bn][:, hd * 8:(hd + 1) * 8], kT[:, 2 * kv + hf, 640:768],
                                            qT[:, pr, cs:cs + 8], False, hd == 15))
                          pe(items, kkeys + qkeys, (("ps", bn),))
                          act(pT[0][:, 0:128], psb[bc][:, 0:128], AF.Exp, (("ps", bc),), (("pT", 0),), scale=0.125)
                          act(pTn[0:32, 0:128], psb[bn][0:32, 0:128], AF.Exp, (("ps", bn),), (("pTn",),), scale=0.125)
                          bfree(bc); bfree(bn)
                          obanks = [balloc(), balloc(), balloc()]
                          items = []
                          for hd in range(16):
                              kv = hd // 4
                              ob = obanks[hd // 7]; off = (hd % 7) * 65
                              items.append((psb[ob][:, off:off + 65], pT[0][:, hd * 8:hd * 8 + 128], vcaug[:, kv, :], True, False))
                              items.append((psb[ob][:, off:off + 65], pTn[:, hd * 8:hd * 8 + 128], vaug[:, 6, kv, :], False, True))
                          pe(items, (("pT", 0), ("pTn",), ("vcaug",), ("vaug", 6)), tuple(("ps", o) for o in obanks))
                          attn_norm(8, obanks)
                          attn_tr(cs, 8)
                      bank = balloc()
                      items = [("T", psb[bank][:, j * 128:(j + 1) * 128], kfin[:, j, :], identf) for j in range(4)]
                      pe(items, (("kfin",),), (("ps", bank),))
                      act(kst[:, :].rearrange("p (h d) -> p h d", h=4), psb[bank][:, :].rearrange("p (h e) -> p h e", h=4)[:, :, 0:64],
                          AF.Copy, (("ps", bank),), (("kst",),))
                      bfree(bank)
                      for i in range(4):
                          dma("sp", "oks", ks_d[l, i, 120:128, :], kst[8 * i:8 * i + 8, :], (("kst",),), (("oks", l, i),))
                          dma("sp", "ovs", vs_d[l, i, 120:128, :], vst[8 * i:8 * i + 8, :], (("vst",),), (("ovs", l, i),))

                  stop(7)
                  uT = R1[:, 0:8, :]
                  ukeys = tuple(("R1", c) for c in range(8))

                  def u_ep(ci, seg, bank):
                      c0, n, tag = seg
                      act(uT[:, ci, c0:c0 + n], psb[bank][:, 0:n], AF.Gelu, (("ps", bank),), (("R1", ci),))
                  proj([(win_d[l], 0, NCH, OU, xn_src, xn_keys)], 1024, full_segs, u_ep, cg=256)

                  vsw = []
                  for hv in range(2):
                      for kh in range(2):
                          view = win_d[l][kh * 1024:(kh + 1) * 1024, OVS + hv * 512: OVS + (hv + 1) * 512] \
                              .rearrange("(k p) n -> p k n", p=128)
                          vsw.append(wload(view, 8, 512))
                  vsn_all = R1[:, 8:16, :].rearrange("p k t -> p (k t)")
                  vsnk = lambda sl: (("vsn", sl),)
                  gv = SC[:, 0:1024]; lnt = SC[:, 1024:2048]
                  mkeys = tuple(("R2", 8 + c) for c in range(8))

                  def sgu_stage1(colstart, m, vsn_bf, vsn_key, is_sample, samp_idx=None, par=0):
                      for hv in range(2):
                          bank = balloc()
                          items = []
                          for kh in range(2):
                              wdst, wkey = vsw[hv * 2 + kh]
                              for k in range(8):
                                  kk = kh * 8 + k
                                  items.append((psb[bank][:, 0:512], xn[:, kk, colstart:colstart + 128], wdst[:, k, :],
                                                kk == 0, kk == NCH - 1))
                          pe(items, tuple(w[1] for w in vsw) + xn_keys, (("ps", bank),))
                          gvp = gv if par == 0 else gv2
                          gkeys = (("S", hv),) if par == 0 else (("bs", 0), ("bs", 1), ("kfin",))
                          act(gvp[0:m, hv * 512:(hv + 1) * 512], psb[bank][0:m, 0:512], AF.Gelu, (("ps", bank),), gkeys)
                          bfree(bank)
                      if par == 0:
                          G = gv[0:m, :]; T = lnt[0:m, :]
                          sk = (("S", 0), ("S", 1)); tk = (("S", 2), ("S", 3))
                      else:
                          G = gv2[0:m, :]; T = lnt2[0:m, :]
                          sk = (("bs", 0), ("bs", 1), ("kfin",)); tk = atok_all + (("pT", 0), ("pT", 1))
                      so = 4 * par
                      dve(lambda h: h.tensor_reduce(out=st[0:m, so:so + 1], in_=G, axis=AX.X, op=ALU.add), sk, (("st", so),))
                      dve(lambda h: h.tensor_scalar(out=st[0:m, so + 1:so + 2], in0=st[0:m, so:so + 1], scalar1=-1.0 / 1024, scalar2=None, op0=ALU.mult),
                          (("st", so),), (("st", so + 1),))
                      act(T, G, AF.Identity, sk + (("st", so + 1),), tk, bias=st[0:m, so + 1:so + 2])
                      dve(lambda h: h.tensor_tensor(out=G, in0=T, in1=T, op=ALU.mult), tk, sk)
                      dve(lambda h: h.tensor_reduce(out=st[0:m, so + 2:so + 3], in_=G, axis=AX.X, op=ALU.add), sk, (("st", so + 2),))
                      rstd_ops(st[0:m, so + 3:so + 4], st[0:m, so + 2:so + 3], 1.0 / 1024, 1e-5, (("st", so + 2),), (("st", so + 3),))
                      dve(lambda h: h.scalar_tensor_tensor(out=T, in0=T, scalar=st[0:m, so + 3:so + 4], in1=lnG[0:m, :],
                                                           op0=ALU.mult, op1=ALU.mult), tk + (("st", so + 3), ("lnG",)), tk)
                      if is_sample:
                          dve(lambda h: h.tensor_tensor(out=T, in0=T, in1=lnB[0:m, :], op=ALU.add), tk + (("lnB",),), tk)
                          dve(lambda h: h.memset(vsn_bf[:, :], 0.0), (), vsn_key)
                          act(vsn_bf[0:m, :], T, AF.Copy, tk, vsn_key)
                          dma("sp", "osv", sv_d[l], T, tk, (("osv", l, samp_idx),))
                      else:
                          dve(lambda h: h.tensor_tensor(out=vsn_bf[0:m, :], in0=T, in1=lnB[0:m, :], op=ALU.add), tk + (("lnB",),), vsn_key)

                  def sgu_stage2(colstart, m, vsn_bf, vsn_key, is_sample, samp_idx=None):
                      for half in range(2):
                          bank = balloc()
                          items = []
                          for j in range(4):
                              g = half * 4 + j
                              o = psb[bank][:, j * 128:j * 128 + m]
                              if is_sample:
                                  items.append((o, vsn_bf[:, g * 128:(g + 1) * 128], wsT_bd[:, g, 0:m], True, False))
                                  items.append((o, onesb, brow_s[:, g, 0:m], False, True))
                              else:
                                  items.append((o, vsn_bf[:, g * 128:(g + 1) * 128], wsT[:, g, 0:m], True, False))
                                  items.append((o, onesb, brow[:, g * 128:g * 128 + m], False, True))
                          pe(items, vsn_key + (("wsT",), ("brow",), ("wsT_bd",), ("brow_s",)), (("ps", bank),))
                          dve(lambda h, half=half, bank=bank: h.tensor_tensor(
                              out=mT[:, half * 4:half * 4 + 4, colstart:colstart + m],
                              in0=psb[bank][:, :].rearrange("p (g t) -> p g t", g=4)[:, :, 0:m],
                              in1=uT[:, half * 4:half * 4 + 4, colstart:colstart + m], op=ALU.mult),
                              (("ps", bank),) + ukeys[half * 4:half * 4 + 4], mkeys[half * 4:half * 4 + 4])
                          bfree(bank)

                  r1hi = tuple(("R1", 8 + j) for j in range(8))
                  vsn_allk = tuple(("vsn", j) for j in range(6))
                  dve(lambda h: h.memset(dummy[:, 0:1], 0.0), (), r1hi + vsn_allk)
                  sgu_list = []
                  for b in blocks:
                      sl = vslot(b)
                      sgu_list.append((bcol(b), 128, vsn_all[:, sl * 1024:(sl + 1) * 1024], vsnk(sl), False, None, len(sgu_list) % 2))
                  if tile == "A":
                      sgu_list.append((640, 32, a_tok[:, :], atok_all, True, 0, 0))
                  prev_s = None
                  for it in sgu_list:
                      if it[4]:
                          if prev_s is not None:
                              sgu_stage2(*prev_s[0:6]); prev_s = None
                          sgu_stage1(*it)
                          sgu_stage2(*it[0:6])
                          continue
                      sgu_stage1(*it)
                      if prev_s is not None:
                          sgu_stage2(*prev_s[0:6])
                      prev_s = it
                  if prev_s is not None:
                      sgu_stage2(*prev_s[0:6])

                  stop(8)
                  merged = R1
                  dve(lambda h: h.memset(dummy[:, 1:2], 0.0), (), r1hi + vsn_allk)
                  mg_keys = lambda ci: (("R1", ci),)
                  for pas in range(2):
                      Wup = wau_d[l] if pas == 0 else wsu_d[l]
                      srcT = aT if pas == 0 else mT
                      skeys_ = akeys if pas == 0 else mkeys
                      gcol = OGA if pas == 0 else OGM
                      for g in range(8):
                          upw, upk = wload(Wup[:, g * 256:(g + 1) * 256].rearrange("(k p) n -> p k n", p=128), 8, 256)
                          gw, gk = wload(win_d[l][:, gcol + g * 256: gcol + (g + 1) * 256].rearrange("(k p) n -> p k n", p=128), NCH, 256)
                          for (c0, n, tag) in full_segs:
                              for cl in range(2):
                                  ci = g * 2 + cl
                                  bp = balloc(); bg = balloc()
                                  items = [(psb[bp][:, 0:n], upw[:, k, cl * 128:(cl + 1) * 128], srcT[:, k, c0:c0 + n], k == 0, k == 7)
                                           for k in range(8)]
                                  pe(items, (upk,) + skeys_, (("ps", bp),))
                                  items = [(psb[bg][:, 0:n], gw[:, k, cl * 128:(cl + 1) * 128], xn[:, k, c0:c0 + n], k == 0, k == NCH - 1)
                                           for k in range(NCH)]
                                  pe(items, (gk,) + xn_keys, (("ps", bg),))
                                  sg = SC[:, 0:n]; tt = SC[:, 512:512 + n]
                                  act(sg, psb[bg][:, 0:n], AF.Sigmoid, (("ps", bg),), (("S", 0),))
                                  if pas == 0:
                                      dve(lambda h, bp=bp, ci=ci, c0=c0, n=n, sg=sg: h.tensor_tensor(
                                          out=merged[:, ci, c0:c0 + n], in0=psb[bp][:, 0:n], in1=sg, op=ALU.mult),
                                          (("ps", bp), ("S", 0)), mg_keys(ci))
                                  else:
                                      dve(lambda h, bp=bp, tt=tt, n=n, sg=sg: h.tensor_tensor(out=tt, in0=psb[bp][:, 0:n], in1=sg, op=ALU.mult),
                                          (("ps", bp), ("S", 0)), (("S", 1),))
                                      dve(lambda h, ci=ci, c0=c0, n=n, tt=tt: h.tensor_tensor(
                                          out=merged[:, ci, c0:c0 + n], in0=merged[:, ci, c0:c0 + n], in1=tt, op=ALU.add),
                                          (("S", 1),) + mg_keys(ci), mg_keys(ci))
                                  bfree(bp); bfree(bg)

                  stop(9)
                  mgall = tuple(("R1", c) for c in range(NCH))

                  def out_ep(ci, seg, bank):
                      c0, n, tag = seg
                      dve(lambda h: h.tensor_tensor(out=x[:, ci, c0:c0 + n], in0=psb[bank][:, 0:n], in1=x[:, ci, c0:c0 + n], op=ALU.add),
                          (("ps", bank), xk(ci)), (xk(ci),))
                      if ssq["banks"] is not None:
                          sb_ = ssq["banks"][c0]
                          sq = bscr[:, (ci % 2) * 512:(ci % 2) * 512 + n]
                          act(sq, x[:, ci, c0:c0 + n], AF.Square, (xk(ci),), (("bs", ci % 2),))
                          return lambda: pe([(psb[sb_][:, 0:n], onesb, sq, ci == 0, ci == NCH - 1)], (("bs", ci % 2),), (("ps", sb_),))
                  ssq_open(full_segs)
                  proj([(wout_d[l], 0, NCH, 0, lambda k, c0, n: merged[:, k, c0:c0 + n], mgall)], D, full_segs, out_ep)
                  ss5 = ssq_take()

                  stop(10)
                  for (a, n) in segs(cf0, W):
                      rmsnorm(False, a, n, g2, a, ssbank=ss5[a])

                  fT = R1
                  fkeys = tuple(("R1", c) for c in range(NCH))
                  tgl = [0]

                  def f_ep(ci, seg, bank):
                      c0, n, tag = seg
                      q = tgl[0] % 2; tgl[0] += 1
                      r = SC[:, 1024 + q * 512: 1024 + q * 512 + n]
                      act(r, psb[bank][:, 0:n], AF.Relu, (("ps", bank),), (("S", 2 + q),))
                      dve(lambda h: h.tensor_tensor(out=fT[:, ci, c0:c0 + n], in0=r, in1=r, op=ALU.mult),
                          (("S", 2 + q),), (("R1", ci),))
                  for j in range(4):
                      proj([(w1_d[l], 0, NCH, j * 2048, xn_src, xn_keys)], 2048, full_segs, f_ep)
                      if j == 3 and l + 1 < NL:
                          ssq_open(full_segs)
                      proj([(w2_d[l], j * 2048, NCH, 0, lambda k, c0, n: fT[:, k, c0:c0 + n], fkeys)], D, full_segs, out_ep)
                  if l + 1 < NL:
                      carry = ssq_take()

              yv = (yA_d if tile == "A" else yB_d).rearrange("(k p) n -> p k n", p=128)
              dma("sp", "oy", yv[:, :, :], x[:, :, 0:W], tuple(xk(k) for k in range(NCH)), (("oy", tile),))

        except _Stop:
            pass

        allk = [k for k in S.last_w if isinstance(k, tuple) and isinstance(k[0], str) and k[0].startswith("o")]
        S.add("sp", None, reads=tuple(allk), writes=())

        S.finalize(eng_sems, chain_sems)
        with nc.Block() as block:
            @block.tensor
            def _(h):
                S.emit_engine("pe", h)

            @block.scalar
            def _(h):
                S.emit_engine("act", h)

            @block.vector
            def _(h):
                S.emit_engine("dve", h)

            @block.gpsimd
            def _(h):
                S.emit_engine("pool", h)

            @block.sync
            def _(h):
                S.emit_engine("sp", h)
    return nc


_CACHE = {}


def _consts():
    bf = ml_dtypes.bfloat16
    ident = np.eye(128, dtype=np.float32)
    ones = np.ones((128, 128), np.float32)
    p = np.arange(128)
    Bd = (p[:, None] // 64 == p[None, :] // 64).astype(np.float32)
    Rm = np.zeros((128, 128), np.float32)
    for m in range(128):
        if m % 64 < 32:
            Rm[m + 32, m] = -1.0
        else:
            Rm[m - 32, m] = 1.0
    k = np.arange(128)[:, None]; q = np.arange(128)[None, :]
    prev = np.where(k > q, 0.0, NEG).astype(np.float32)
    cur = np.where(k <= q, 0.0, NEG).astype(np.float32)
    maskN = np.concatenate([prev, cur, prev, cur], axis=1)
    allneg = np.full((128, 128), NEG, np.float32)
    maskF0 = np.concatenate([allneg, cur, allneg, cur], axis=1)
    q8 = np.arange(8)[None, :]
    mc = np.where(k > q8, 0.0, NEG).astype(np.float32)
    maskc = np.tile(mc, (1, 16))
    k8 = np.arange(8)[:, None]
    mn = np.where(k8 <= q8, 0.0, NEG).astype(np.float32)
    maskn = np.full((128, 4 * 128), NEG, np.float32)
    for i in range(4):
        maskn[8 * i:8 * i + 8, 128 * i:128 * (i + 1)] = np.tile(mn, (1, 16))
    tril = np.tril(np.ones((128, 128), np.float32))
    cf = np.concatenate([ident, tril], axis=1).astype(np.float32)
    return ident, ones, Bd, Rm, maskN, maskF0, maskc, maskn, cf, bf


def prep_inputs(x_prompt, x_sample, cache_k, cache_v, norm1_g, w_in, q_norm_g, k_norm_g,
                attn_sinks, sgu_ln_g, sgu_ln_b, sgu_w, sgu_b, w_attn_up, w_sgu_up, w_out,
                norm2_g, w_ff1, w_ff2):
    f32 = np.float32
    x_prompt = np.asarray(x_prompt, f32); x_sample = np.asarray(x_sample, f32)
    cache_k = np.asarray(cache_k, f32); cache_v = np.asarray(cache_v, f32)
    w_in = np.asarray(w_in, f32)
    ident, ones, Bd, Rm, maskN, maskF0, maskc, maskn, cf, bf = _consts()

    w_in_p = np.zeros((DEPTH, D, IN_COLS_P), f32)
    w_in_p[:, :, 0:1024] = w_in[:, :, 0:1024]
    for j in range(4):
        kj = w_in[:, :, 1024 + 64 * j: 1024 + 64 * (j + 1)]
        base = OK_ + 256 * j
        w_in_p[:, :, base: base + 64] = kj
        w_in_p[:, :, base + 192: base + 256] = kj
    w_in_p[:, :, OV:OV + 256] = w_in[:, :, 1280:1536]
    w_in_p[:, :, OU:OU + 1024] = w_in[:, :, 1536:2560]
    w_in_p[:, :, OVS:OVS + 1024] = w_in[:, :, 2560:3584]
    w_in_p[:, :, OGA:OGA + 2048] = w_in[:, :, 3584:5632]
    w_in_p[:, :, OGM:OGM + 2048] = w_in[:, :, 5632:7680]

    vecs = np.zeros((128, NV_L * DEPTH), f32)
    for l in range(DEPTH):
        vecs[:, l * NV_L: l * NV_L + 16] = np.asarray(norm1_g[l], f32).reshape(16, 128).T
        vecs[:, l * NV_L + 16: l * NV_L + 32] = np.asarray(norm2_g[l], f32).reshape(16, 128).T
        vecs[:, l * NV_L + 32] = np.tile(np.asarray(q_norm_g[l], f32), 2)
        vecs[:, l * NV_L + 33] = np.tile(np.asarray(k_norm_g[l], f32), 2)
    lnG = np.ascontiguousarray(np.broadcast_to(np.asarray(sgu_ln_g, f32)[:, None, :], (DEPTH, 128, 1024)))
    lnB = np.ascontiguousarray(np.broadcast_to(np.asarray(sgu_ln_b, f32)[:, None, :], (DEPTH, 128, 1024)))
    sinks = np.ascontiguousarray(np.broadcast_to(np.asarray(attn_sinks, f32)[:, None, :], (DEPTH, 128, 16)))
    sgub = np.asarray(sgu_b, f32).reshape(DEPTH, 1, 1024)

    inv = np.power(np.float32(10000.0), -np.arange(32, dtype=np.float32) / np.float32(32))
    invp = np.tile(inv, 4).astype(f32)

    def tables(pos):
        ang = invp[:, None] * pos.astype(f32)[None, :]
        return np.cos(ang).astype(f32), np.sin(ang).astype(f32)

    shared = dict(w_in=w_in_p, w_au=np.asarray(w_attn_up, f32), w_su=np.asarray(w_sgu_up, f32),
                  w_out=np.asarray(w_out, f32), w_ff1=np.asarray(w_ff1, f32), w_ff2=np.asarray(w_ff2, f32),
                  vecs=vecs, lnG=lnG, lnB=lnB, sinks=sinks, sgu_w=np.asarray(sgu_w, f32), sgu_b=sgub, cf32=cf)
    in_maps = []
    for c in range(8):
        seq, part = c // 4, c % 4
        start = part * 1024 - 512
        xs = np.zeros((12 * 128, D), f32)
        lo = max(start, 0)
        xs[lo - start:] = x_prompt[seq, lo:start + 1536]
        pos = start + np.arange(1536)
        samp = x_sample[4 * c:4 * c + 4].reshape(32, D)
        pos_s = 16384 + np.tile(np.arange(8), 4)
        xa = np.concatenate([xs[128:768], samp, xs[0:128]], axis=0)
        pa = np.concatenate([pos[128:768], pos_s, pos[0:128]])
        xb = xs[768:1536]; pb = pos[768:1536]
        cA, sA = tables(pa); cB, sB = tables(pb)
        maskF = maskF0 if part == 0 else maskN
        cbf = np.concatenate([ident, ones, Bd, Rm, maskN, maskF, maskc, maskn], axis=1).astype(bf)
        m = dict(shared)
        m.update(xA=np.ascontiguousarray(xa.T), xB=np.ascontiguousarray(xb.T), cosA=cA, sinA=sA, cosB=cB, sinB=sB,
                 cache_k=np.ascontiguousarray(cache_k[:, 4 * c:4 * c + 4].reshape(DEPTH, 4, 128, 256)),
                 cache_v=np.ascontiguousarray(cache_v[:, 4 * c:4 * c + 4].reshape(DEPTH, 4, 128, 256)),
                 cbf=cbf)
        in_maps.append(m)
    return in_maps


def kernel(**inputs):
    in_maps = prep_inputs(**inputs)
    if "nc" not in _CACHE:
        _CACHE["nc"] = build_program()
    nc = _CACHE["nc"]
    res = run_bass_kernel_spmd(nc, in_maps, core_ids=list(range(8)))
    return assemble(res.results)


def assemble(R):
    f32 = np.float32
    y_prompt = np.zeros((2, 4096, D), f32)
    y_sample = np.zeros((32, 8, D), f32)
    nkp = np.zeros((DEPTH, 2, 128, 4, 64), f32); nvp = np.zeros_like(nkp)
    nks = np.zeros((DEPTH, 32, 128, 4, 64), f32); nvs = np.zeros_like(nks)
    nsv = np.zeros((DEPTH, 32, 8, 8, 128), f32)
    for c in range(8):
        seq, part = c // 4, c % 4
        yA = np.asarray(R[c]["yA"]).T
        yB = np.asarray(R[c]["yB"]).T
        base = part * 1024
        y_prompt[seq, base:base + 256] = yA[384:640]
        y_prompt[seq, base + 256:base + 1024] = yB
        y_sample[4 * c:4 * c + 4] = yA[640:672].reshape(4, 8, D)
        if part == 3:
            nkp[:, seq] = np.asarray(R[c]["kp"]).reshape(DEPTH, 128, 4, 64)
            nvp[:, seq] = np.asarray(R[c]["vp"]).reshape(DEPTH, 128, 4, 64)
        nks[:, 4 * c:4 * c + 4] = np.asarray(R[c]["ks"]).reshape(DEPTH, 4, 128, 4, 64)
        nvs[:, 4 * c:4 * c + 4] = np.asarray(R[c]["vs"]).reshape(DEPTH, 4, 128, 4, 64)
        nsv[:, 4 * c:4 * c + 4] = np.asarray(R[c]["sv"]).reshape(DEPTH, 4, 8, 8, 128)
    return (y_prompt, y_sample, nkp, nvp, nks, nvs, nsv)
```
